# Optimizing a Trainium2 kernel written in Bass

```python
import math
import jax, jax.numpy as jnp
from jax import lax
import numpy as np

D_MODEL = 1024
BATCH = 16
SEQ = 2048
DEPTH = 4

ATTN_WIDTH = D_MODEL // 2
RNN_WIDTH = D_MODEL - ATTN_WIDTH
DIFF_HEAD_DIM = 64
N_DIFF_HEADS = ATTN_WIDTH // (2 * DIFF_HEAD_DIM)
V_HEAD_DIM = 2 * DIFF_HEAD_DIM
ROT_DIM = DIFF_HEAD_DIM // 4
ROPE_THETA = 500000.0
LRU_BLOCK = 64
N_LRU_BLOCKS = RNN_WIDTH // LRU_BLOCK
CONV_WIDTH = 4
LRU_C = 8.0
D_FF = 4 * D_MODEL
PLE_DIM = 256
Q_BLOCK = 128
EPS = 1e-6
IN_WIDTH = 3 * ATTN_WIDTH + 2 * RNN_WIDTH

kernel_name = 'hymba_diffattn_rglru_hybrid'


def rms_norm(x, g):
    xf = x.astype(jnp.float32)
    y = xf * lax.rsqrt(jnp.mean(xf * xf, axis=-1, keepdims=True) + EPS)
    return (y * g.astype(jnp.float32)).astype(x.dtype)


def partial_rope(x, positions):
    half = ROT_DIM // 2
    inv_freq = ROPE_THETA ** (-jnp.arange(half, dtype=jnp.float32) * 2.0 / ROT_DIM)
    ang = positions.astype(jnp.float32)[..., None] * inv_freq
    cos = jnp.cos(ang)[:, :, None, None, :]
    sin = jnp.sin(ang)[:, :, None, None, :]
    xr = x[..., :ROT_DIM].astype(jnp.float32)
    x1, x2 = xr[..., :half], xr[..., half:]
    rot = jnp.concatenate([x1 * cos - x2 * sin, x2 * cos + x1 * sin], axis=-1)
    return jnp.concatenate([rot.astype(x.dtype), x[..., ROT_DIM:]], axis=-1)


def diff_attention(q, k, v, lam):
    B, S = q.shape[0], q.shape[1]
    nb = S // Q_BLOCK
    scale = DIFF_HEAD_DIM ** -0.5
    qb = q.reshape(B, nb, Q_BLOCK, N_DIFF_HEADS, 2, DIFF_HEAD_DIM).transpose(1, 0, 2, 3, 4, 5)
    q_idx = jnp.arange(S).reshape(nb, Q_BLOCK)
    k_idx = jnp.arange(S)

    def block(args):
        q_blk, qi = args
        s = jnp.einsum('bqhcd,bkhcd->bhcqk', q_blk, k).astype(jnp.float32) * scale
        mask = qi[:, None] >= k_idx[None, :]
        s = jnp.where(mask, s, -jnp.inf)
        pr = jax.nn.softmax(s, axis=-1)
        attn = pr[:, :, 0] - lam * pr[:, :, 1]
        return jnp.einsum('bhqk,bkhe->bqhe', attn.astype(v.dtype), v)

    out = lax.map(block, (qb, q_idx))
    return out.transpose(1, 0, 2, 3, 4).reshape(B, S, N_DIFF_HEADS, V_HEAD_DIM)


def causal_depthwise_conv(x, w, b):
    y = lax.conv_general_dilated(
        x, w[:, None, :].astype(x.dtype), window_strides=(1,),
        padding=[(CONV_WIDTH - 1, 0)],
        dimension_numbers=('NWC', 'WIO', 'NWC'),
        feature_group_count=x.shape[-1])
    return y + b


def rg_lru(x, w_a, b_a, w_x, b_x, lam_param):
    B, S, _ = x.shape
    xb = x.reshape(B, S, N_LRU_BLOCKS, LRU_BLOCK)
    r = jax.nn.sigmoid((jnp.einsum('bsgi,gij->bsgj', xb, w_a).reshape(B, S, RNN_WIDTH) + b_a).astype(jnp.float32))
    i = jax.nn.sigmoid((jnp.einsum('bsgi,gij->bsgj', xb, w_x).reshape(B, S, RNN_WIDTH) + b_x).astype(jnp.float32))
    log_a = -LRU_C * r * jax.nn.softplus(-lam_param.astype(jnp.float32))
    a = jnp.exp(log_a)
    b = jnp.sqrt(-jnp.expm1(2.0 * log_a)) * (i * x.astype(jnp.float32))

    def combine(left, right):
        a_l, b_l = left
        a_r, b_r = right
        return a_l * a_r, a_r * b_l + b_r

    _, h = lax.associative_scan(combine, (a, b), axis=1)
    return h.astype(x.dtype)


def setup_inputs(seed: int = 0) -> dict:
    key = jax.random.key(seed)
    ks = jax.random.split(key, 24)
    f32 = jnp.float32
    nrm = lambda k, shape, s: jax.random.normal(k, shape, f32) * s
    gain = lambda k, shape: 1.0 + 0.05 * jax.random.normal(k, shape, f32)
    a0 = jax.random.uniform(ks[15], (DEPTH, RNN_WIDTH), f32, 0.9, 0.999)
    s0 = a0 ** (1.0 / LRU_C)
    lru_lambda = jnp.log(s0) - jnp.log1p(-s0)
    positions = jnp.broadcast_to(jnp.arange(SEQ, dtype=jnp.int32)[None, :], (BATCH, SEQ))
    return {
        'x': nrm(ks[0], (BATCH, SEQ, D_MODEL), 1.0),
        'p': nrm(ks[1], (DEPTH, BATCH, SEQ, PLE_DIM), 1.0),
        'positions': positions,
        'w_in': nrm(ks[2], (DEPTH, D_MODEL, IN_WIDTH), D_MODEL ** -0.5),
        'w_out': nrm(ks[3], (DEPTH, D_MODEL, D_MODEL), D_MODEL ** -0.5),
        'g_mix': gain(ks[4], (DEPTH, D_MODEL)),
        'g_subln': gain(ks[5], (DEPTH, V_HEAD_DIM)),
        'lam_q': nrm(ks[6], (DEPTH, 2, DIFF_HEAD_DIM), 0.1),
        'lam_k': nrm(ks[7], (DEPTH, 2, DIFF_HEAD_DIM), 0.1),
        'conv_w': nrm(ks[8], (DEPTH, CONV_WIDTH, RNN_WIDTH), CONV_WIDTH ** -0.5),
        'conv_b': nrm(ks[9], (DEPTH, RNN_WIDTH), 0.01),
        'w_gate_a': nrm(ks[10], (DEPTH, N_LRU_BLOCKS, LRU_BLOCK, LRU_BLOCK), LRU_BLOCK ** -0.5),
        'b_gate_a': nrm(ks[11], (DEPTH, RNN_WIDTH), 0.01),
        'w_gate_x': nrm(ks[12], (DEPTH, N_LRU_BLOCKS, LRU_BLOCK, LRU_BLOCK), LRU_BLOCK ** -0.5),
        'b_gate_x': nrm(ks[13], (DEPTH, RNN_WIDTH), 0.01),
        'lru_lambda': lru_lambda,
        'g_mlp': gain(ks[14], (DEPTH, D_MODEL)),
        'w_mlp_in': nrm(ks[16], (DEPTH, D_MODEL, D_FF), D_MODEL ** -0.5),
        'w_mlp_out': nrm(ks[17], (DEPTH, D_FF, D_MODEL), D_FF ** -0.5),
        'g_ple': gain(ks[18], (DEPTH, D_MODEL)),
        'w_ple_gate': nrm(ks[19], (DEPTH, D_MODEL, D_MODEL), D_MODEL ** -0.5),
        'w_ple_proj': nrm(ks[20], (DEPTH, PLE_DIM, D_MODEL), PLE_DIM ** -0.5),
        'g_final': gain(ks[21], (D_MODEL,)),
    }


def reference(x, p, positions, w_in, w_out, g_mix, g_subln, lam_q, lam_k, conv_w, conv_b,
              w_gate_a, b_gate_a, w_gate_x, b_gate_x, lru_lambda, g_mlp, w_mlp_in, w_mlp_out,
              g_ple, w_ple_gate, w_ple_proj, g_final):
    B, S, _ = x.shape
    h = x
    A, R = ATTN_WIDTH, RNN_WIDTH
    for l in range(DEPTH):
        hn = rms_norm(h, g_mix[l])
        proj = hn @ w_in[l]
        q, k, v, xr, gr = jnp.split(proj, [A, 2 * A, 3 * A, 3 * A + R], axis=-1)
        q = partial_rope(q.reshape(B, S, N_DIFF_HEADS, 2, DIFF_HEAD_DIM), positions)
        k = partial_rope(k.reshape(B, S, N_DIFF_HEADS, 2, DIFF_HEAD_DIM), positions)
        v = v.reshape(B, S, N_DIFF_HEADS, V_HEAD_DIM)
        lam_init = 0.8 - 0.6 * math.exp(-0.3 * l)
        dots = jnp.sum(lam_q[l].astype(jnp.float32) * lam_k[l].astype(jnp.float32), axis=-1)
        lam = jnp.exp(dots[0]) - jnp.exp(dots[1]) + lam_init
        o = diff_attention(q, k, v, lam)
        o = (rms_norm(o, g_subln[l]) * (1.0 - lam_init)).reshape(B, S, A)
        xc = causal_depthwise_conv(xr, conv_w[l], conv_b[l])
        y = rg_lru(xc, w_gate_a[l], b_gate_a[l], w_gate_x[l], b_gate_x[l], lru_lambda[l])
        y = y * jax.nn.gelu(gr)
        h = h + jnp.concatenate([o, y], axis=-1) @ w_out[l]
        hm = rms_norm(h, g_mlp[l])
        h = h + jnp.square(jax.nn.relu(hm @ w_mlp_in[l])) @ w_mlp_out[l]
        gate = jax.nn.sigmoid(rms_norm(h, g_ple[l]) @ w_ple_gate[l])
        h = h + gate * (p[l] @ w_ple_proj[l])
    return rms_norm(h, g_final)
```

```python
import math
import os
from contextlib import ExitStack

import numpy as np
import concourse.bass as bass
import concourse.mybir as mybir
from concourse.bass_utils import run_bass_kernel_spmd

F32 = mybir.dt.float32
BF16 = mybir.dt.bfloat16
I32 = mybir.dt.int32
AF = mybir.ActivationFunctionType
ALU = mybir.AluOpType
AX = mybir.AxisListType

D = 1024
T = 2048
L = 4
NC = 8
TB = 512
NTB = 4
NTT = 16
EPS = 1e-6
SCALE = 0.125
GM, GL, GP, CW, CB, BA, BX, LL, GS, NSM = 0, 8, 16, 24, 40, 44, 48, 52, 56, 64
PI = math.pi


class Sched:
    ENG = ("pe", "act", "dve", "pool", "sp")

    def __init__(self):
        self.q = {e: [] for e in self.ENG}
        self.cnt = {e: 0 for e in self.ENG}
        self.waited = {e: {} for e in self.ENG}
        self.lastw = {}
        self.readers = {}
        self.dcnt = {}

    def _waits(self, eng, reads, writes):
        need = {}
        for k in reads:
            ev = self.lastw.get(k)
            if ev is not None:
                need[ev[0]] = max(need.get(ev[0], 0), ev[1])
        for k in writes:
            ev = self.lastw.get(k)
            if ev is not None:
                need[ev[0]] = max(need.get(ev[0], 0), ev[1])
            for ev in self.readers.get(k, ()):
                need[ev[0]] = max(need.get(ev[0], 0), ev[1])
        waits = []
        for s, v in need.items():
            if eng == "pe" and s == "pe":
                continue
            if self.waited[eng].get(s, 0) < v:
                self.waited[eng][s] = v
                waits.append((s, v))
        return waits

    def _commit(self, ev, reads, writes):
        for k in reads:
            self.readers.setdefault(k, []).append(ev)
        for k in writes:
            self.lastw[k] = ev
            self.readers[k] = []

    def op(self, eng, fn, reads=(), writes=()):
        waits = self._waits(eng, reads, writes)
        self.cnt[eng] += 1
        ev = (eng, self.cnt[eng])
        self.q[eng].append((waits, fn, (eng, 1)))
        self._commit(ev, reads, writes)
        return ev

    def dma(self, queue, dsem, fns, reads=(), writes=()):
        waits = self._waits(queue, reads, writes)
        first = True
        for fn in fns:
            self.dcnt[dsem] = self.dcnt.get(dsem, 0) + 16
            self.q[queue].append((waits if first else [], fn, (dsem, 16)))
            first = False
        ev = (dsem, self.dcnt[dsem])
        self._commit(ev, reads, writes)
        return ev


class _Stop(Exception):
    pass


def build_program(n_layers=L, n_seq=2, lim=None):
    nc = bass.Bass("TRN2", target_bir_lowering=False)
    dt_in = lambda name, shape, dt=F32: nc.dram_tensor(name, shape, dt, kind="ExternalInput").ap()
    xT = dt_in("xT", [2, D, T])
    pT = dt_in("pT", [L, 2, 256, T])
    pos = dt_in("pos", [2, 128, NTT], I32)
    w_in = dt_in("w_in", [L, D, 2560])
    w_out = dt_in("w_out", [L, D, D])
    w1 = dt_in("w_mlp_in", [L, D, 4096])
    w2 = dt_in("w_mlp_out", [L, 4096, D])
    wg = dt_in("w_ple_gate", [L, D, D])
    wp = dt_in("w_ple_proj", [L, 256, D])
    smalls = dt_in("smalls", [128, L, NSM])
    gatew = dt_in("gatew", [L, 128, 1024])
    lqk = dt_in("lqk", [L, 128, 256])
    gfin = dt_in("gfin", [128, 8])
    cst = dt_in("cst", [128, 3 * 128])
    invf = dt_in("invf", [128, 8])
    outT = nc.dram_tensor("outT", [2, D, T], F32, kind="ExternalOutput").ap()

    S = Sched()
    with ExitStack() as es:
        sb = lambda name, shape, dt: es.enter_context(nc.sbuf_tensor(name, shape, dt))
        hT = sb("hT", [128, 8, T], F32)
        Bt = sb("Bt", [128, 8, T], BF16)
        QK = sb("QK", [128, 32, TB], BF16)
        V = sb("V", [128, NTT, 512], BF16)
        ring = sb("ring", [128, 2, 8, 512], BF16)
        Yt = sb("Yt", [128, 4, T], BF16)
        sm = sb("sm", [128, L, NSM], F32)
        gW = sb("gW", [128, 2, 4, 128], BF16)
        gf = sb("gf", [128, 8], F32)
        cb = sb("cb", [128, 3, 128], BF16)
        ivf = sb("ivf", [128, 8], F32)
        idf = sb("idf", [128, 128], F32)
        posi = sb("posi", [128, NTT], I32)
        posf = sb("posf", [128, NTT], F32)
        ang = sb("ang", [128, NTT, 8], F32)
        cosT = sb("cosT", [128, NTT, 8], F32)
        sinT = sb("sinT", [128, NTT, 8], F32)
        der = sb("der", [128, 16], F32)
        kint = sb("kint", [128, 128], I32)
        cst_c = sb("cst_c", [128, 4], F32)
        Pb = [[sb(f"Pb{i}{c}", [128, 512], BF16) for c in range(2)] for i in range(2)]
        E0 = sb("E0", [128, 512], F32)
        E1 = sb("E1", [128, 512], F32)
        E2 = sb("E2", [128, 512], BF16)
        xr_ext = sb("xr_ext", [128, 516], F32)
        Rt = {n: sb("R_" + n, [128, 512], F32) for n in ("gg", "xc", "r", "i", "hs")}
        xcb = sb("xcb", [128, 512], BF16)
        tcs_t = sb("tcs_t", [128, 512], F32)
        cos2T = sb("cos2T", [128, NTT, 16], F32)
        n2pi = tcs_t[:, 0:128]
        rtmp = tcs_t[:, 128:256]
        psA = [es.enter_context(nc.psum_tensor(f"psA{i}", [128, 1024], F32)) for i in range(2)]
        psB = [es.enter_context(nc.psum_tensor(f"psB{i}", [128, 512], F32)) for i in range(4)]
        bank = [psA[0][:, 0:512], psA[0][:, 512:1024], psA[1][:, 0:512], psA[1][:, 512:1024],
                psB[0][:], psB[1][:], psB[2][:], psB[3][:]]
        bkey = [f"ps{i}" for i in range(8)]

        ident = cb[:, 0, :]
        ones = cb[:, 1, :]
        cmask = cb[:, 2, :]
        eps_c = cst_c[:, 0:1]
        npi_c = cst_c[:, 1:2]

        loads = []
        st = {"next_load": 0, "n_emitted": 0, "rr": 0, "done": -1, "pending_done": []}

        def ring_key(n):
            return f"ring{n % 2}"

        def emit_load(n):
            fns = loads[n]
            S.dma("pool", f"dring{n % 2}", fns, reads=(), writes=(ring_key(n),))

        def _emit_upto(m):
            m = min(m, len(loads) - 1)
            while st["n_emitted"] <= m:
                emit_load(st["n_emitted"])
                st["n_emitted"] += 1

        def use_w(n, paired=False):
            if not paired:
                st["done"] = max(st["done"], n - 1)
            _emit_upto(max(n, st["done"] + 2))
            return ring[:, n % 2], ring_key(n)

        def done_w(n):
            st["done"] = max(st["done"], n)
            _emit_upto(st["done"] + 2)

        def mk_load(src_ap_fn, n_index_holder):
            pass

        def add_load(src):
            n = len(loads)
            slot = n % 2
            loads.append([lambda e, src=src, slot=slot: e.dma_start(out=ring[:, slot], in_=src)])
            return n

        def add_load_multi(pairs):
            n = len(loads)
            slot = n % 2
            fns = []
            for (osl, src) in pairs:
                fns.append(lambda e, src=src, osl=osl, slot=slot: e.dma_start(out=ring[:, slot, osl[0]:osl[1]], in_=src))
            loads.append(fns)
            return n

        def kpc(ap2d):
            return ap2d.rearrange("(k p) c -> p k c", p=128)

        plan = {}
        for s in range(n_seq):
            for l in range(n_layers):
                d = {}
                d["win"] = {g: add_load(kpc(w_in[l][:, g * 512:(g + 1) * 512])) for g in (2, 0, 1, 3, 4)}
                d["wo"] = [add_load(kpc(w_out[l][:, g * 512:(g + 1) * 512])) for g in range(2)]
                d["w1"], d["w2"] = [], []
                for g in range(4):
                    d["w1"].append([add_load(kpc(w1[l][:, g * 1024 + a * 512: g * 1024 + (a + 1) * 512])) for a in range(2)])
                    d["w2"].append([add_load(kpc(w2[l][g * 1024:(g + 1) * 1024, dh * 512:(dh + 1) * 512])) for dh in range(2)])
                d["wg0"] = add_load(kpc(wg[l][:, 0:512]))
                d["wp"] = add_load_multi([((0, 2), kpc(wp[l][:, 0:512])), ((2, 4), kpc(wp[l][:, 512:1024]))])
                d["wg1"] = add_load(kpc(wg[l][:, 512:1024]))
                plan[(s, l)] = d

        def nextbank(cands):
            b = cands[st["rr"] % len(cands)]
            st["rr"] += 1
            return b

        S.dma("sp", "dsm", [lambda e: e.dma_start(out=sm[:], in_=smalls)], writes=("sm",))
        S.dma("sp", "dgf", [lambda e: e.dma_start(out=gf[:], in_=gfin)], writes=("gf",))
        S.dma("sp", "divf", [lambda e: e.dma_start(out=ivf[:], in_=invf)], writes=("ivf",))
        S.dma("sp", "didf", [lambda e: e.dma_start(out=idf[:], in_=cst[:, 0:128])], writes=("idf",))
        S.dma("pool", "dcb", [lambda e: e.dma_start(out=cb[:], in_=cst.rearrange("p (a b) -> p a b", a=3))], writes=("cb",))
        S.op("dve", lambda e: e.memset(cst_c[:, 0:1], EPS), writes=("cstc0",))
        S.op("dve", lambda e: e.memset(cst_c[:, 1:2], -PI), writes=("cstc1",))
        S.op("dve", lambda e: e.memset(xr_ext[:, 0:4], 0.0), writes=("xr_ext",))

        def tbs(tb):
            return slice(tb * TB, (tb + 1) * TB)

        def rmsnorm_to_B(gcol_fn, tag):
            for tb in range(NTB):
                bk = nextbank([4, 5, 6, 7])
                for c in range(8):
                    sq = Pb[0][c % 2]
                    sqk = f"Pb0{c % 2}"
                    S.op("act", lambda e, sq=sq, c=c, tb=tb: e.activation(sq[:], hT[:, c, tbs(tb)], AF.Square),
                         reads=(f"h{c}_{tb}",), writes=(sqk,))
                    S.op("pe", lambda e, sq=sq, c=c, bk=bk: e.matmul(bank[bk], ones, sq[:], start=(c == 0), stop=(c == 7)),
                         reads=(sqk, "cb"), writes=(bkey[bk],))
                Er = (E0, E1)[tb % 2]
                Ek = ("E0", "E1")[tb % 2]
                S.op("act", lambda e, Er=Er, bk=bk: e.activation(Er[:], bank[bk], AF.Sqrt, bias=eps_c, scale=1.0 / D),
                     reads=(bkey[bk], "cstc0"), writes=(Ek,))
                S.op("dve", lambda e, Er=Er: e.reciprocal(Er[:], Er[:]), reads=(Ek,), writes=(Ek,))
                for c in range(8):
                    S.op("dve", lambda e, Er=Er, c=c, tb=tb: e.scalar_tensor_tensor(
                        Bt[:, c, tbs(tb)], hT[:, c, tbs(tb)], gcol_fn(c), Er[:], ALU.mult, ALU.mult),
                        reads=(f"h{c}_{tb}", Ek, "sm", "gf"), writes=(f"B{c}_{tb}",))

        def qk_idx(h, tb):
            return h * 4 + tb

        def chk(k):
            if lim is not None and k > lim:
                raise _Stop()

        for s in range(n_seq):
            S.dma("sp", "dx", [(lambda e, s=s, c=c: e.dma_start(out=hT[:, c, :], in_=xT[s, c * 128:(c + 1) * 128, :])) for c in range(8)],
                  writes=[f"h{c}_{tb}" for c in range(8) for tb in range(NTB)])
            S.dma("sp", "dpos", [lambda e, s=s: e.dma_start(out=posi[:], in_=pos[s])], writes=("posi",))
            S.op("dve", lambda e: e.tensor_copy(posf[:], posi[:]), reads=("posi",), writes=("posf",))
            S.op("dve", lambda e: e.tensor_tensor(ang[:], posf[:].unsqueeze(2).to_broadcast([128, NTT, 8]),
                                                  ivf[:].unsqueeze(1).to_broadcast([128, NTT, 8]), ALU.mult),
                 reads=("posf", "ivf"), writes=("ang",))
            S.op("dve", lambda e: e.memset(n2pi, -2 * PI), writes=("xrot",))
            af, cf, sf = ang[:].rearrange("p a b -> p (a b)"), cosT[:].rearrange("p a b -> p (a b)"), sinT[:].rearrange("p a b -> p (a b)")
            S.op("dve", lambda e: e.tensor_scalar(cf, af, 1.0 / (2 * PI), None, ALU.mult), reads=("ang",), writes=("cosT",))
            S.op("dve", lambda e: e.tensor_copy(kint[:], cf), reads=("cosT",), writes=("kint",))
            S.op("dve", lambda e: e.tensor_copy(cf, kint[:]), reads=("kint",), writes=("cosT",))
            S.op("dve", lambda e: e.scalar_tensor_tensor(af, cf, -6.28125, af, ALU.mult, ALU.add), reads=("cosT", "ang"), writes=("ang",))
            S.op("dve", lambda e: e.scalar_tensor_tensor(af, cf, -(2 * PI - 6.28125), af, ALU.mult, ALU.add), reads=("cosT", "ang"), writes=("ang",))
            S.op("dve", lambda e: e.scalar_tensor_tensor(sf, af, PI, n2pi, ALU.is_gt, ALU.mult), reads=("ang", "xrot"), writes=("sinT",))
            S.op("dve", lambda e: e.tensor_tensor(af, af, sf, ALU.add), reads=("ang", "sinT"), writes=("ang",))
            S.op("dve", lambda e: e.tensor_scalar(af, af, -PI, PI, ALU.max, ALU.min), reads=("ang",), writes=("ang",))
            S.op("act", lambda e: e.activation(sf, af, AF.Sin), reads=("ang",), writes=("sinT",))
            S.op("dve", lambda e: e.tensor_scalar(cf, af, 0.5 * PI, None, ALU.add), reads=("ang",), writes=("cosT",))
            S.op("dve", lambda e: e.scalar_tensor_tensor(rtmp, cf, PI, n2pi, ALU.is_gt, ALU.mult), reads=("cosT", "xrot"), writes=("tcv",))
            S.op("dve", lambda e: e.tensor_tensor(cf, cf, rtmp, ALU.add), reads=("cosT", "tcv"), writes=("cosT",))
            S.op("dve", lambda e: e.tensor_scalar(cf, cf, -PI, PI, ALU.max, ALU.min), reads=("cosT",), writes=("cosT",))
            S.op("act", lambda e: e.activation(cf, cf, AF.Sin), reads=("cosT",), writes=("cosT",))
            S.op("dve", lambda e: e.tensor_copy(cos2T[:, :, 0:8], cosT[:]), reads=("cosT",), writes=("cos2T",))
            S.op("dve", lambda e: e.tensor_copy(cos2T[:, :, 8:16], cosT[:]), reads=("cosT", "cos2T"), writes=("cos2T",))

            def emit_layer(s, l):
                pl = plan[(s, l)]
                chk(0)
                lam_init = 0.8 - 0.6 * math.exp(-0.3 * l)
                S.dma("pool", "dgw", [lambda e, l=l: e.dma_start(out=gW[:], in_=gatew[l].rearrange("p (g c j) -> p g c j", g=2, c=4))],
                      writes=("gW",))
                S.dma("sp", "dlqk", [lambda e, l=l: e.dma_start(out=E0[:, 0:256], in_=lqk[l])], writes=("E0",))
                S.op("dve", lambda e: e.tensor_tensor(E0[:, 0:128], E0[:, 0:128], E0[:, 128:256], ALU.mult), reads=("E0",), writes=("E0",))
                S.op("dve", lambda e: e.tensor_reduce(der[:, 0:2], E0[:, 0:128].rearrange("p (a b) -> p a b", a=2), AX.X, ALU.add),
                     reads=("E0",), writes=("der01",))
                S.op("act", lambda e: e.activation(der[:, 0:2], der[:, 0:2], AF.Exp), reads=("der01",), writes=("der01",))
                S.op("dve", lambda e: e.tensor_tensor(der[:, 2:3], der[:, 0:1], der[:, 1:2], ALU.subtract), reads=("der01",), writes=("der2",))
                S.op("dve", lambda e, li=lam_init: e.tensor_scalar(der[:, 3:4], der[:, 2:3], li, -1.0, ALU.add, ALU.mult),
                     reads=("der2",), writes=("neglam",))
                S.op("dve", lambda e, l=l, li=lam_init: e.tensor_scalar(der[:, 4:5], sm[:, l, GS:GS + 1], 1.0 - li, None, ALU.mult),
                     reads=("sm",), writes=("gsub",))
                S.op("act", lambda e, l=l: e.activation(der[:, 5:9], sm[:, l, LL:LL + 4], AF.Exp, scale=-1.0), reads=("sm",), writes=("c1",))
                S.op("act", lambda e: e.activation(der[:, 5:9], der[:, 5:9], AF.Ln, bias=1.0, scale=1.0), reads=("c1",), writes=("c1",))
                S.op("dve", lambda e: e.tensor_scalar(der[:, 9:13], der[:, 5:9], -16.0, None, ALU.mult), reads=("c1",), writes=("c2",))
                S.op("dve", lambda e: e.tensor_scalar(der[:, 5:9], der[:, 5:9], -8.0, None, ALU.mult), reads=("c1", "c2"), writes=("c1",))
                neglam = der[:, 3:4]
                gsub = der[:, 4:5]

                rmsnorm_to_B(lambda c, l=l: sm[:, l, GM + c:GM + c + 1], "n1")

                chk(1)
                wv, wvk_ = use_w(pl["win"][2])
                for tt in range(NTT):
                    tb = tt // 4
                    tsl = slice(tt * 128, (tt + 1) * 128)
                    vb = 4 + (tt % 2)

                    def projv(e, tsl=tsl, vb=vb):
                        ins = None
                        for c in range(8):
                            ins = e.matmul(bank[vb], Bt[:, c, tsl], wv[:, c, :], start=(c == 0), stop=(c == 7))
                        return ins
                    S.op("pe", projv, reads=[f"B{c}_{tb}" for c in range(8)] + [wvk_], writes=(bkey[vb],))
                    if tt % 2 == 0:
                        S.op("dve", lambda e, tt=tt, vb=vb: e.tensor_copy(V[:, tt, :], bank[vb]), reads=(bkey[vb],), writes=(f"V{tt}",))
                    else:
                        S.op("act", lambda e, tt=tt, vb=vb: e.activation(V[:, tt, :], bank[vb], AF.Copy), reads=(bkey[vb],), writes=(f"V{tt}",))

                done_w(pl["win"][2])
                chk(2)
                wq, wqk_ = use_w(pl["win"][0])
                wk, wkk_ = use_w(pl["win"][1], paired=True)
                tcv = tcs_t[:, 128:256].rearrange("p (g d) -> p g d", d=16)
                tsv = tcs_t[:, 256:384].rearrange("p (g d) -> p g d", d=16)
                Fnames = ("gg", "xc")
                Tb = (4, 5)

                def emit_transposes(tt):
                    tb, off = tt // 4, (tt % 4) * 128
                    for hb in range(2):
                        Ft = Rt[Fnames[hb]]
                        pb = Tb[hb]
                        for g in range(4):
                            S.op("pe", lambda e, g=g, Ft=Ft, pb=pb: e.transpose(bank[pb][:, g * 128:(g + 1) * 128], Ft[:, g * 128:(g + 1) * 128], idf[:]),
                                 reads=(Fnames[hb], "idf"), writes=(bkey[pb],))
                        src = bank[pb].rearrange("p (h t) -> p h t", h=4)
                        if hb == 0:
                            S.op("act", lambda e, tb=tb, off=off, src=src: e.activation(QK[:, tb:16:4, off:off + 128], src, AF.Copy),
                                 reads=(bkey[pb],), writes=[f"qk{qk_idx(h, tb)}" for h in range(4)])
                        else:
                            S.op("dve", lambda e, tb=tb, off=off, src=src: e.tensor_copy(QK[:, 16 + tb:32:4, off:off + 128], src),
                                 reads=(bkey[pb],), writes=[f"qk{16 + qk_idx(h, tb)}" for h in range(4)])

                for tt in range(NTT):
                    tb = tt // 4
                    tsl = slice(tt * 128, (tt + 1) * 128)
                    A = psA[tt % 2]
                    Ak = (bkey[0], bkey[1]) if tt % 2 == 0 else (bkey[2], bkey[3])

                    def proj(e, A=A, tsl=tsl):
                        ins = None
                        for c in range(8):
                            e.matmul(A[:, 0:512], Bt[:, c, tsl], wq[:, c, :], start=(c == 0), stop=(c == 7))
                            ins = e.matmul(A[:, 512:1024], Bt[:, c, tsl], wk[:, c, :], start=(c == 0), stop=(c == 7))
                        return ins
                    S.op("pe", proj, reads=[f"B{c}_{tb}" for c in range(8)] + [wqk_, wkk_], writes=(Ak[0], Ak[1]))
                    if tt >= 1 and not os.environ.get('SKIP_TR'):
                        emit_transposes(tt - 1)
                    cos2B = cos2T[:, tt, :].unsqueeze(1).to_broadcast([128, 8, 16])
                    sinB8 = sinT[:, tt, :].unsqueeze(1).to_broadcast([128, 8, 8])
                    for hb in range(2):
                        Ah = A[:, hb * 512:(hb + 1) * 512]
                        Fn = Fnames[hb]
                        Ft = Rt[Fn]
                        if hb == 0:
                            S.op("act", lambda e, Ft=Ft, Ah=Ah: e.activation(Ft[:], Ah, AF.Copy), reads=(Ak[hb],), writes=(Fn,))
                        else:
                            S.op("dve", lambda e, Ft=Ft, Ah=Ah: e.tensor_copy(Ft[:], Ah), reads=(Ak[hb],), writes=(Fn,))
                        if os.environ.get('SKIP_ROPE'):
                            continue
                        F3 = Ft[:].rearrange("p (g d) -> p g d", d=64)
                        S.op("dve", lambda e, F3=F3, cos2B=cos2B: e.tensor_tensor(tcv, F3[:, :, 0:16], cos2B, ALU.mult),
                             reads=(Fn, "cos2T"), writes=("tcv",))
                        S.op("dve", lambda e, F3=F3, sinB8=sinB8: e.scalar_tensor_tensor(tsv[:, :, 0:8], F3[:, :, 8:16], -1.0, sinB8, ALU.mult, ALU.mult),
                             reads=(Fn, "sinT"), writes=("tsv",))
                        S.op("dve", lambda e, F3=F3, sinB8=sinB8: e.tensor_tensor(tsv[:, :, 8:16], F3[:, :, 0:8], sinB8, ALU.mult),
                             reads=(Fn, "sinT", "tsv"), writes=("tsv",))
                        S.op("dve", lambda e, F3=F3: e.tensor_tensor(F3[:, :, 0:16], tcv, tsv, ALU.add),
                             reads=("tcv", "tsv", Fn), writes=(Fn,))
                if not os.environ.get('SKIP_TR'):
                    emit_transposes(NTT - 1)

                done_w(pl["win"][0])
                done_w(pl["win"][1])
                chk(3)
                wxr, wxrk = use_w(pl["win"][3])
                wgr, wgrk = use_w(pl["win"][4], paired=True)

                def rnn_proj(u):
                    c, tb = u // 4, u % 4
                    def f(e, c=c, tb=tb):
                        ins = None
                        for k in range(8):
                            ins = e.matmul(bank[6], wxr[:, k, c * 128:(c + 1) * 128], Bt[:, k, tbs(tb)], start=(k == 0), stop=(k == 7))
                        return ins
                    S.op("pe", f, reads=[f"B{k}_{tb}" for k in range(8)] + [wxrk], writes=(bkey[6],))
                    if tb == 0:
                        S.op("dve", lambda e: e.memset(xr_ext[:, 0:4], 0.0), reads=("xr_ext",), writes=("xr_ext",))
                    else:
                        S.op("dve", lambda e: e.tensor_copy(xr_ext[:, 0:4], xr_ext[:, 512:516]), reads=("xr_ext",), writes=("xr_ext",))
                    S.op("act", lambda e: e.activation(xr_ext[:, 4:516], bank[6], AF.Copy), reads=(bkey[6],), writes=("xr_ext",))
                    def f2(e, c=c, tb=tb):
                        ins = None
                        for k in range(8):
                            ins = e.matmul(bank[6], wgr[:, k, c * 128:(c + 1) * 128], Bt[:, k, tbs(tb)], start=(k == 0), stop=(k == 7))
                        return ins
                    S.op("pe", f2, reads=[f"B{k}_{tb}" for k in range(8)] + [wgrk], writes=(bkey[6],))
                    S.op("act", lambda e: e.activation(Rt["gg"][:], bank[6], AF.Gelu_apprx_tanh), reads=(bkey[6],), writes=("gg",))
                    cw = lambda k, c=c: sm[:, l, CW + c * 4 + k:CW + c * 4 + k + 1]
                    S.op("dve", lambda e, c=c: e.tensor_scalar(Rt["xc"][:], xr_ext[:, 4:516], cw(3), sm[:, l, CB + c:CB + c + 1], ALU.mult, ALU.add),
                         reads=("xr_ext", "sm"), writes=("xc",))
                    for j in (1, 2, 3):
                        S.op("dve", lambda e, j=j: e.scalar_tensor_tensor(Rt["xc"][:], xr_ext[:, 4 - j:516 - j], cw(3 - j), Rt["xc"][:], ALU.mult, ALU.add),
                             reads=("xr_ext", "xc", "sm"), writes=("xc",))
                    S.op("act", lambda e: e.activation(xcb[:], Rt["xc"][:], AF.Copy), reads=("xc",), writes=("xcb",))

                def rnn_gates(u):
                    c, tb = u // 4, u % 4
                    S.op("pe", lambda e, c=c: e.matmul(bank[6], gW[:, 0, c, :], xcb[:], start=True, stop=True), reads=("xcb", "gW"), writes=(bkey[6],))
                    S.op("act", lambda e, c=c: e.activation(Rt["r"][:], bank[6], AF.Sigmoid, bias=sm[:, l, BA + c:BA + c + 1], scale=1.0),
                         reads=(bkey[6], "sm"), writes=("r",))
                    S.op("pe", lambda e, c=c: e.matmul(bank[6], gW[:, 1, c, :], xcb[:], start=True, stop=True), reads=("xcb", "gW"), writes=(bkey[6],))
                    S.op("act", lambda e, c=c: e.activation(Rt["i"][:], bank[6], AF.Sigmoid, bias=sm[:, l, BX + c:BX + c + 1], scale=1.0),
                         reads=(bkey[6], "sm"), writes=("i",))
                    S.op("dve", lambda e: e.tensor_tensor(Rt["i"][:], Rt["i"][:], Rt["xc"][:], ALU.mult), reads=("i", "xc"), writes=("i",))
                    S.op("act", lambda e, c=c: e.activation(Rt["xc"][:], Rt["r"][:], AF.Exp, scale=der[:, 9 + c:10 + c]),
                         reads=("r", "c2"), writes=("xc",))
                    S.op("act", lambda e, c=c: e.activation(Rt["r"][:], Rt["r"][:], AF.Exp, scale=der[:, 5 + c:6 + c]),
                         reads=("r", "c1"), writes=("r",))
                    S.op("act", lambda e: e.activation(Rt["xc"][:], Rt["xc"][:], AF.Sqrt, bias=1.0, scale=-1.0), reads=("xc",), writes=("xc",))
                    S.op("dve", lambda e: e.tensor_tensor(Rt["i"][:], Rt["i"][:], Rt["xc"][:], ALU.mult), reads=("i", "xc"), writes=("i",))
                    if tb == 0:
                        S.op("dve", lambda e: e.memset(der[:, 13:14], 0.0), reads=("carry",), writes=("carry",))
                    else:
                        S.op("dve", lambda e: e.tensor_copy(der[:, 13:14], Rt["hs"][:, 511:512]), reads=("hs",), writes=("carry",))
                    S.op("dve", lambda e: e.tensor_tensor_scan(Rt["hs"][:], Rt["r"][:], Rt["i"][:], der[:, 13:14], ALU.mult, ALU.add),
                         reads=("r", "i", "carry"), writes=("hs",))
                    S.op("dve", lambda e, c=c, tb=tb: e.tensor_tensor(Yt[:, c, tbs(tb)], Rt["hs"][:], Rt["gg"][:], ALU.mult),
                         reads=("hs", "gg"), writes=(f"Y{c}_{tb}",))

                def att_block(h, j, u_proj, u_gate):
                    nkt = 4 * j + 4
                    qi = qk_idx(h, j)
                    S0b, S1b, O1b, O2b, R1b, R2b = 0, 1, 2, 3, 4, 5
                    for kt in range(nkt):
                        m = kt - 4 * j
                        q0 = m * 128 if m > 0 else 0
                        tbk, off = kt // 4, (kt % 4) * 128
                        ki = 16 + qk_idx(h, tbk)
                        pbi = kt % 2

                        def qk(e, q0=q0, ki=ki, off=off, qi=qi):
                            e.matmul(bank[S0b][:, q0:512], QK[0:64, ki, off:off + 128], QK[0:64, qi, q0:512], start=True, stop=True, tile_position=(0, 0))
                            return e.matmul(bank[S1b][:, q0:512], QK[64:128, ki, off:off + 128], QK[64:128, qi, q0:512], start=True, stop=True, tile_position=(64, 0))
                        S.op("pe", qk, reads=(f"qk{ki}", f"qk{qi}"), writes=(bkey[S0b], bkey[S1b]))
                        for cc in range(2):
                            P = Pb[pbi][cc]
                            pk = f"Pb{pbi}{cc}"
                            S.op("act", lambda e, P=P, cc=cc, q0=q0: e.activation(P[:, q0:512], bank[cc][:, q0:512], AF.Exp, scale=SCALE),
                                 reads=(bkey[cc],), writes=(pk,))
                            if m >= 0:
                                S.op("dve", lambda e, P=P, q0=q0: e.tensor_tensor(P[:, q0:q0 + 128], P[:, q0:q0 + 128], cmask, ALU.mult),
                                     reads=(pk, "cb"), writes=(pk,))

                        def pv(e, q0=q0, kt=kt, pbi=pbi, h=h, first=(kt == 0), last=(kt == nkt - 1)):
                            vv = V[:, kt, h * 128:(h + 1) * 128]
                            e.matmul(bank[O1b][:, q0:512], vv, Pb[pbi][0][:, q0:512], start=first, stop=last)
                            e.matmul(bank[R1b][:, q0:512], ones, Pb[pbi][0][:, q0:512], start=first, stop=last)
                            e.matmul(bank[O2b][:, q0:512], vv, Pb[pbi][1][:, q0:512], start=first, stop=last)
                            return e.matmul(bank[R2b][:, q0:512], ones, Pb[pbi][1][:, q0:512], start=first, stop=last)
                        S.op("pe", pv, reads=(f"V{kt}", f"Pb{pbi}0", f"Pb{pbi}1", "cb"),
                             writes=(bkey[O1b], bkey[O2b], bkey[R1b], bkey[R2b]))
                        if kt == 0 and u_gate is not None:
                            rnn_gates(u_gate)
                    S.op("dve", lambda e: e.reciprocal(E0[:], bank[R1b]), reads=(bkey[R1b],), writes=("E0",))
                    S.op("dve", lambda e: e.reciprocal(E1[:], bank[R2b]), reads=(bkey[R2b],), writes=("E1",))
                    S.op("dve", lambda e: e.tensor_tensor(E0[:], bank[O1b], E0[:], ALU.mult), reads=(bkey[O1b], "E0"), writes=("E0",))
                    S.op("dve", lambda e: e.tensor_tensor(E1[:], bank[O2b], E1[:], ALU.mult), reads=(bkey[O2b], "E1"), writes=("E1",))
                    S.op("dve", lambda e: e.scalar_tensor_tensor(E0[:], E1[:], neglam, E0[:], ALU.mult, ALU.add),
                         reads=("E0", "E1", "neglam"), writes=("E0",))
                    S.op("act", lambda e: e.activation(E2[:], E0[:], AF.Square), reads=("E0",), writes=("E2",))
                    S.op("pe", lambda e: e.matmul(bank[S0b], ones, E2[:], start=True, stop=True), reads=("E2", "cb"), writes=(bkey[S0b],))
                    S.op("act", lambda e: e.activation(E1[:], bank[S0b], AF.Sqrt, bias=eps_c, scale=1.0 / 128), reads=(bkey[S0b], "cstc0"), writes=("E1",))
                    S.op("dve", lambda e: e.reciprocal(E1[:], E1[:]), reads=("E1",), writes=("E1",))
                    S.op("dve", lambda e, qi=qi: e.scalar_tensor_tensor(QK[:, qi, :], E0[:], gsub, E1[:], ALU.mult, ALU.mult),
                         reads=("E0", "E1", "gsub"), writes=(f"qk{qi}",))
                    if u_proj is not None:
                        rnn_proj(u_proj)

                blocks = [(h, j) for h in range(4) for j in range(NTB)]
                rnn_proj(0)
                for bi, (h, j) in enumerate(blocks):
                    att_block(h, j, bi + 1 if bi + 1 < 16 else None, bi)

                done_w(pl["win"][3])
                done_w(pl["win"][4])
                chk(4)
                for g2 in range(2):
                    if g2 == 1:
                        done_w(pl["wo"][0])
                    wo_, wok = use_w(pl["wo"][g2])
                    for d4 in range(4):
                        dtc = g2 * 4 + d4
                        for tb in range(NTB):
                            bk = nextbank([0, 1, 2, 3, 4, 5, 6, 7])
                            def f(e, d4=d4, tb=tb, bk=bk, wo_=wo_):
                                ins = None
                                for c in range(4):
                                    e.matmul(bank[bk], wo_[:, c, d4 * 128:(d4 + 1) * 128], QK[:, c * 4 + tb, :], start=(c == 0), stop=False)
                                for c in range(4):
                                    ins = e.matmul(bank[bk], wo_[:, 4 + c, d4 * 128:(d4 + 1) * 128], Yt[:, c, tbs(tb)], start=False, stop=(c == 3))
                                return ins
                            S.op("pe", f, reads=[f"qk{c * 4 + tb}" for c in range(4)] + [f"Y{c}_{tb}" for c in range(4)] + [wok], writes=(bkey[bk],))
                            S.op("dve", lambda e, dtc=dtc, tb=tb, bk=bk: e.tensor_tensor(hT[:, dtc, tbs(tb)], bank[bk], hT[:, dtc, tbs(tb)], ALU.add),
                                 reads=(bkey[bk], f"h{dtc}_{tb}"), writes=(f"h{dtc}_{tb}",))

                chk(5)
                done_w(pl["wo"][1])
                rmsnorm_to_B(lambda c, l=l: sm[:, l, GL + c:GL + c + 1], "n2")
                pTb = V[:, 0:8, :].rearrange("p a b -> p (a b)").rearrange("p (k t) -> p k t", k=2)
                S.dma("pool", "dpt", [(lambda e, l=l, s=s, k=k: e.dma_start(out=pTb[:, k, :], in_=pT[l, s, k * 128:(k + 1) * 128, :])) for k in range(2)],
                      writes=[f"V{t}" for t in range(8)])
                sqt = [Rt["gg"], Rt["xc"]]
                sqk = ["gg", "xc"]
                cnt = 0
                for g in range(4):
                    for a in range(2):
                        w1_, w1k = use_w(pl["w1"][g][a])
                        for f4 in range(4):
                            fc = a * 4 + f4
                            for tb in range(NTB):
                                bk = nextbank([0, 1, 2, 3, 4, 5, 6, 7])
                                hk = f"qk{fc * 4 + tb}"
                                def f(e, f4=f4, tb=tb, bk=bk, w1_=w1_):
                                    ins = None
                                    for c in range(8):
                                        ins = e.matmul(bank[bk], w1_[:, c, f4 * 128:(f4 + 1) * 128], Bt[:, c, tbs(tb)], start=(c == 0), stop=(c == 7))
                                    return ins
                                S.op("pe", f, reads=[f"B{c}_{tb}" for c in range(8)] + [w1k], writes=(bkey[bk],))
                                ti = cnt % 2
                                cnt += 1
                                S.op("act", lambda e, bk=bk, ti=ti: e.activation(sqt[ti][:], bank[bk], AF.Square), reads=(bkey[bk],), writes=(sqk[ti],))
                                S.op("dve", lambda e, bk=bk, ti=ti, fc=fc, tb=tb: e.scalar_tensor_tensor(
                                    QK[:, fc * 4 + tb, :], bank[bk], 0.0, sqt[ti][:], ALU.is_gt, ALU.mult),
                                    reads=(bkey[bk], sqk[ti]), writes=(hk,))
                    for dh in range(2):
                        w2_, w2k = use_w(pl["w2"][g][dh])
                        for d4 in range(4):
                            dtc = dh * 4 + d4
                            for tb in range(NTB):
                                bk = nextbank([0, 1, 2, 3, 4, 5, 6, 7])
                                def f(e, d4=d4, tb=tb, bk=bk, w2_=w2_):
                                    ins = None
                                    for fc in range(8):
                                        ins = e.matmul(bank[bk], w2_[:, fc, d4 * 128:(d4 + 1) * 128], QK[:, fc * 4 + tb, :], start=(fc == 0), stop=(fc == 7))
                                    return ins
                                S.op("pe", f, reads=[f"qk{fc * 4 + tb}" for fc in range(8)] + [w2k], writes=(bkey[bk],))
                                S.op("dve", lambda e, dtc=dtc, tb=tb, bk=bk: e.tensor_tensor(hT[:, dtc, tbs(tb)], bank[bk], hT[:, dtc, tbs(tb)], ALU.add),
                                     reads=(bkey[bk], f"h{dtc}_{tb}"), writes=(f"h{dtc}_{tb}",))

                chk(6)
                rmsnorm_to_B(lambda c, l=l: sm[:, l, GP + c:GP + c + 1], "n3")
                sgt = [Rt["r"], Rt["i"]]
                sgk = ["r", "i"]
                t2t = [Rt["hs"], Rt["gg"]]
                t2k = ["hs", "gg"]
                cnt = 0
                wp_, wpk = None, None
                for dh in range(2):
                    if dh == 0:
                        wg_, wgk_ = use_w(pl["wg0"])
                        wp_, wpk = use_w(pl["wp"], paired=True)
                    else:
                        done_w(pl["wg0"])
                        wg_, wgk_ = use_w(pl["wg1"], paired=True)
                    for d4 in range(4):
                        dtc = dh * 4 + d4
                        for tb in range(NTB):
                            bg = nextbank([0, 1, 2, 3, 4, 5, 6, 7])
                            bp = nextbank([0, 1, 2, 3, 4, 5, 6, 7])
                            def f(e, d4=d4, tb=tb, bg=bg, wg_=wg_):
                                ins = None
                                for c in range(8):
                                    ins = e.matmul(bank[bg], wg_[:, c, d4 * 128:(d4 + 1) * 128], Bt[:, c, tbs(tb)], start=(c == 0), stop=(c == 7))
                                return ins
                            S.op("pe", f, reads=[f"B{c}_{tb}" for c in range(8)] + [wgk_], writes=(bkey[bg],))
                            def f2(e, d4=d4, tb=tb, bp=bp, dh=dh, wp_=wp_):
                                ins = None
                                for k in range(2):
                                    ins = e.matmul(bank[bp], wp_[:, dh * 2 + k, d4 * 128:(d4 + 1) * 128], pTb[:, k, tbs(tb)], start=(k == 0), stop=(k == 1))
                                return ins
                            S.op("pe", f2, reads=[f"V{t}" for t in range(8)] + [wpk], writes=(bkey[bp],))
                            ti = cnt % 2
                            cnt += 1
                            S.op("act", lambda e, bg=bg, ti=ti: e.activation(sgt[ti][:], bank[bg], AF.Sigmoid), reads=(bkey[bg],), writes=(sgk[ti],))
                            S.op("dve", lambda e, bp=bp, ti=ti: e.tensor_tensor(t2t[ti][:], bank[bp], sgt[ti][:], ALU.mult),
                                 reads=(bkey[bp], sgk[ti]), writes=(t2k[ti],))
                            S.op("dve", lambda e, dtc=dtc, tb=tb, ti=ti: e.tensor_tensor(hT[:, dtc, tbs(tb)], t2t[ti][:], hT[:, dtc, tbs(tb)], ALU.add),
                                 reads=(t2k[ti], f"h{dtc}_{tb}"), writes=(f"h{dtc}_{tb}",))

            for l in range(n_layers):
                try:
                    emit_layer(s, l)
                except _Stop:
                    pass
            for tb in range(NTB):
                bk = nextbank([4, 5, 6, 7])
                for c in range(8):
                    sq = Pb[0][c % 2]
                    sqk_ = f"Pb0{c % 2}"
                    S.op("act", lambda e, sq=sq, c=c, tb=tb: e.activation(sq[:], hT[:, c, tbs(tb)], AF.Square),
                         reads=(f"h{c}_{tb}",), writes=(sqk_,))
                    S.op("pe", lambda e, sq=sq, c=c, bk=bk: e.matmul(bank[bk], ones, sq[:], start=(c == 0), stop=(c == 7)),
                         reads=(sqk_, "cb"), writes=(bkey[bk],))
                Er = (E0, E1)[tb % 2]
                Ek = ("E0", "E1")[tb % 2]
                S.op("act", lambda e, Er=Er, bk=bk: e.activation(Er[:], bank[bk], AF.Sqrt, bias=eps_c, scale=1.0 / D),
                     reads=(bkey[bk], "cstc0"), writes=(Ek,))
                S.op("dve", lambda e, Er=Er: e.reciprocal(Er[:], Er[:]), reads=(Ek,), writes=(Ek,))
                for c in range(8):
                    names = ("gg", "xc", "r", "i", "hs")
                    nm = names[c % 5]
                    ot = Rt[nm]
                    S.op("dve", lambda e, Er=Er, c=c, tb=tb, ot=ot: e.scalar_tensor_tensor(
                        ot[:], hT[:, c, tbs(tb)], gf[:, c:c + 1], Er[:], ALU.mult, ALU.mult),
                        reads=(f"h{c}_{tb}", Ek, "gf"), writes=(nm,))
                    S.dma("sp", "dout_" + nm, [lambda e, ot=ot, c=c, tb=tb, s=s: e.dma_start(out=outT[s, c * 128:(c + 1) * 128, tbs(tb)], in_=ot[:])],
                          reads=(nm,))

        sem_names = list(S.ENG) + sorted(S.dcnt.keys())
        sems = {n: es.enter_context(nc.semaphore("s_" + n)) for n in sem_names}
        final_waits = [(n, S.cnt[n]) for n in S.ENG if n != "sp" and S.cnt[n] > 0] + [(n, v) for n, v in S.dcnt.items()]
        block = es.enter_context(nc.Block())

        def replay(name, e):
            for waits, fn, (sn, inc) in S.q[name]:
                for (ws, wv) in waits:
                    e.wait_ge(sems[ws], wv)
                ins = fn(e)
                ins.then_inc(sems[sn], inc)
            if name == "sp":
                for (ws, wv) in final_waits:
                    e.wait_ge(sems[ws], wv)

        @block.tensor
        def _(e):
            replay("pe", e)

        @block.scalar
        def _(e):
            replay("act", e)

        @block.vector
        def _(e):
            replay("dve", e)

        @block.gpsimd
        def _(e):
            replay("pool", e)

        @block.sync
        def _(e):
            replay("sp", e)
    return nc, S


def _host_consts():
    ident = np.eye(128, dtype=np.float32)
    ones = np.ones((128, 128), np.float32)
    k = np.arange(128)[:, None]
    q = np.arange(128)[None, :]
    mask = (q >= k).astype(np.float32)
    cst = np.concatenate([ident, ones, mask], axis=1)
    half = 8
    inv_freq = (np.float32(500000.0) ** (-np.arange(half, dtype=np.float32) * np.float32(2.0) / np.float32(16))).astype(np.float32)
    invf = np.broadcast_to(inv_freq[None, :], (128, 8)).copy()
    return cst, invf


def _layout_inputs(inp, n_layers=L):
    f = lambda a: np.ascontiguousarray(np.asarray(a, dtype=np.float32))
    col8 = lambda v: v.reshape(8, 128).T
    col4 = lambda v: v.reshape(4, 128).T
    smalls = np.zeros((128, L, NSM), np.float32)
    gatew = np.zeros((L, 128, 2, 4, 128), np.float32)
    lqk = np.zeros((L, 128, 256), np.float32)
    for l in range(L):
        smalls[:, l, GM:GM + 8] = col8(f(inp["g_mix"][l]))
        smalls[:, l, GL:GL + 8] = col8(f(inp["g_mlp"][l]))
        smalls[:, l, GP:GP + 8] = col8(f(inp["g_ple"][l]))
        cw = f(inp["conv_w"][l])
        for c in range(4):
            for k in range(4):
                smalls[:, l, CW + c * 4 + k] = cw[k, c * 128:(c + 1) * 128]
        smalls[:, l, CB:CB + 4] = col4(f(inp["conv_b"][l]))
        smalls[:, l, BA:BA + 4] = col4(f(inp["b_gate_a"][l]))
        smalls[:, l, BX:BX + 4] = col4(f(inp["b_gate_x"][l]))
        smalls[:, l, LL:LL + 4] = col4(f(inp["lru_lambda"][l]))
        smalls[:, l, GS] = f(inp["g_subln"][l])
        for gi, nm in enumerate(("w_gate_a", "w_gate_x")):
            w = f(inp[nm][l])
            for c in range(4):
                for b in range(2):
                    gatew[l, b * 64:(b + 1) * 64, gi, c, b * 64:(b + 1) * 64] = w[2 * c + b]
        lqk[l, :, 0:128] = f(inp["lam_q"][l]).reshape(1, 128)
        lqk[l, :, 128:256] = f(inp["lam_k"][l]).reshape(1, 128)
    gatew = gatew.reshape(L, 128, 1024)
    gfin = np.ascontiguousarray(col8(f(inp["g_final"])))
    cst, invf = _host_consts()
    x = f(inp["x"])
    p = f(inp["p"])
    posn = np.asarray(inp["positions"]).astype(np.int32)
    shared = dict(w_in=f(inp["w_in"]), w_out=f(inp["w_out"]), w_mlp_in=f(inp["w_mlp_in"]), w_mlp_out=f(inp["w_mlp_out"]),
                  w_ple_gate=f(inp["w_ple_gate"]), w_ple_proj=f(inp["w_ple_proj"]), smalls=smalls, gatew=gatew, lqk=lqk,
                  gfin=gfin, cst=cst, invf=invf)
    maps = []
    for i in range(NC):
        xs = x[2 * i:2 * i + 2]
        m = dict(shared)
        m["xT"] = np.ascontiguousarray(xs.transpose(0, 2, 1))
        m["pT"] = np.ascontiguousarray(p[:, 2 * i:2 * i + 2].transpose(0, 1, 3, 2))
        m["pos"] = np.ascontiguousarray(posn[2 * i:2 * i + 2].reshape(2, NTT, 128).transpose(0, 2, 1))
        maps.append(m)
    return maps


_CACHE = {}


def kernel(**inputs):
    maps = _layout_inputs(inputs)
    if "nc" not in _CACHE:
        _CACHE["nc"] = build_program()[0]
    nc = _CACHE["nc"]
    res = run_bass_kernel_spmd(nc, maps, core_ids=list(range(NC)))
    out = np.empty((16, T, D), np.float32)
    for i in range(NC):
        o = res.results[i]["outT"]
        out[2 * i:2 * i + 2] = o.transpose(0, 2, 1)
    return out
```

```python
import math
import os
from contextlib import ExitStack

import numpy as np
import concourse.bass as bass
import concourse.mybir as mybir
from concourse.bass_utils import run_bass_kernel_spmd

F32 = mybir.dt.float32
BF16 = mybir.dt.bfloat16
I32 = mybir.dt.int32
AF = mybir.ActivationFunctionType
ALU = mybir.AluOpType
AX = mybir.AxisListType

D = 1024
T = 2048
L = 4
NC = 8
TB = 512
NTB = 4
NTT = 16
EPS = 1e-6
SCALE = 0.125
GM, GL, GP, CW, CB, BA, BX, LL, GS, NSM = 0, 8, 16, 24, 40, 44, 48, 52, 56, 64
PI = math.pi


class Sched:
    ENG = ("pe", "act", "dve", "pool", "sp")

    def __init__(self):
        self.q = {e: [] for e in self.ENG}
        self.cnt = {e: 0 for e in self.ENG}
        self.waited = {e: {} for e in self.ENG}
        self.lastw = {}
        self.readers = {}
        self.dcnt = {}

    def _waits(self, eng, reads, writes):
        need = {}
        for k in reads:
            ev = self.lastw.get(k)
            if ev is not None:
                need[ev[0]] = max(need.get(ev[0], 0), ev[1])
        for k in writes:
            ev = self.lastw.get(k)
            if ev is not None:
                need[ev[0]] = max(need.get(ev[0], 0), ev[1])
            for ev in self.readers.get(k, ()):
                need[ev[0]] = max(need.get(ev[0], 0), ev[1])
        waits = []
        for s, v in need.items():
            if eng == "pe" and s == "pe":
                continue
            if self.waited[eng].get(s, 0) < v:
                self.waited[eng][s] = v
                waits.append((s, v))
        return waits

    def _commit(self, ev, reads, writes):
        for k in reads:
            self.readers.setdefault(k, []).append(ev)
        for k in writes:
            self.lastw[k] = ev
            self.readers[k] = []

    def op(self, eng, fn, reads=(), writes=()):
        waits = self._waits(eng, reads, writes)
        self.cnt[eng] += 1
        ev = (eng, self.cnt[eng])
        self.q[eng].append((waits, fn, (eng, 1)))
        self._commit(ev, reads, writes)
        return ev

    def dma(self, queue, dsem, fns, reads=(), writes=()):
        waits = self._waits(queue, reads, writes)
        first = True
        for fn in fns:
            self.dcnt[dsem] = self.dcnt.get(dsem, 0) + 16
            self.q[queue].append((waits if first else [], fn, (dsem, 16)))
            first = False
        ev = (dsem, self.dcnt[dsem])
        self._commit(ev, reads, writes)
        return ev


class _Stop(Exception):
    pass


def build_program(n_layers=L, n_seq=2, lim=None):
    nc = bass.Bass("TRN2", target_bir_lowering=False)
    dt_in = lambda name, shape, dt=F32: nc.dram_tensor(name, shape, dt, kind="ExternalInput").ap()
    xT = dt_in("xT", [2, D, T])
    pT = dt_in("pT", [L, 2, 256, T])
    pos = dt_in("pos", [2, 128, NTT], I32)
    w_in = dt_in("w_in", [L, D, 2560])
    w_out = dt_in("w_out", [L, D, D])
    w1 = dt_in("w_mlp_in", [L, D, 4096])
    w2 = dt_in("w_mlp_out", [L, 4096, D])
    wg = dt_in("w_ple_gate", [L, D, D])
    wp = dt_in("w_ple_proj", [L, 256, D])
    smalls = dt_in("smalls", [128, L, NSM])
    gatew = dt_in("gatew", [L, 128, 1024])
    lqk = dt_in("lqk", [L, 128, 256])
    gfin = dt_in("gfin", [128, 8])
    cst = dt_in("cst", [128, 3 * 128])
    invf = dt_in("invf", [128, 8])
    outT = nc.dram_tensor("outT", [2, D, T], F32, kind="ExternalOutput").ap()

    S = Sched()
    with ExitStack() as es:
        sb = lambda name, shape, dt: es.enter_context(nc.sbuf_tensor(name, shape, dt))
        hT = sb("hT", [128, 8, T], F32)
        Bt = sb("Bt", [128, 8, T], BF16)
        QK = sb("QK", [128, 32, TB], BF16)
        V = sb("V", [128, NTT, 512], BF16)
        ring = sb("ring", [128, 2, 8, 512], BF16)
        Yt = sb("Yt", [128, 4, T], BF16)
        sm = sb("sm", [128, L, NSM], F32)
        gW = sb("gW", [128, 2, 4, 128], BF16)
        gf = sb("gf", [128, 8], F32)
        cb = sb("cb", [128, 3, 128], BF16)
        ivf = sb("ivf", [128, 8], F32)
        idf = sb("idf", [128, 128], F32)
        posi = sb("posi", [128, NTT], I32)
        posf = sb("posf", [128, NTT], F32)
        ang = sb("ang", [128, NTT, 8], F32)
        cosT = sb("cosT", [128, NTT, 8], F32)
        sinT = sb("sinT", [128, NTT, 8], F32)
        der = sb("der", [128, 16], F32)
        kint = sb("kint", [128, 128], I32)
        cst_c = sb("cst_c", [128, 4], F32)
        Pb = [[sb(f"Pb{i}{c}", [128, 512], BF16) for c in range(2)] for i in range(2)]
        E0 = sb("E0", [128, 512], F32)
        E1 = sb("E1", [128, 512], F32)
        E2 = sb("E2", [128, 512], BF16)
        xr_ext = sb("xr_ext", [128, 516], F32)
        Rt = {n: sb("R_" + n, [128, 512], F32) for n in ("gg", "xc", "r", "i", "hs")}
        xcb = sb("xcb", [128, 512], BF16)
        tcs_t = sb("tcs_t", [128, 512], F32)
        cos2T = sb("cos2T", [128, NTT, 16], F32)
        n2pi = tcs_t[:, 0:128]
        rtmp = tcs_t[:, 128:256]
        psA = [es.enter_context(nc.psum_tensor(f"psA{i}", [128, 1024], F32)) for i in range(2)]
        psB = [es.enter_context(nc.psum_tensor(f"psB{i}", [128, 512], F32)) for i in range(4)]
        bank = [psA[0][:, 0:512], psA[0][:, 512:1024], psA[1][:, 0:512], psA[1][:, 512:1024],
                psB[0][:], psB[1][:], psB[2][:], psB[3][:]]
        bkey = [f"ps{i}" for i in range(8)]

        ident = cb[:, 0, :]
        ones = cb[:, 1, :]
        cmask = cb[:, 2, :]
        eps_c = cst_c[:, 0:1]
        npi_c = cst_c[:, 1:2]

        loads = []
        st = {"next_load": 0, "n_emitted": 0, "rr": 0, "done": -1, "pending_done": []}

        def ring_key(n):
            return f"ring{n % 2}"

        def emit_load(n):
            fns = loads[n]
            S.dma("pool", f"dring{n % 2}", fns, reads=(), writes=(ring_key(n),))

        def _emit_upto(m):
            m = min(m, len(loads) - 1)
            while st["n_emitted"] <= m:
                emit_load(st["n_emitted"])
                st["n_emitted"] += 1

        def use_w(n, paired=False):
            if not paired:
                st["done"] = max(st["done"], n - 1)
            _emit_upto(max(n, st["done"] + 2))
            return ring[:, n % 2], ring_key(n)

        def done_w(n):
            st["done"] = max(st["done"], n)
            _emit_upto(st["done"] + 2)

        def mk_load(src_ap_fn, n_index_holder):
            pass

        def add_load(src):
            n = len(loads)
            slot = n % 2
            loads.append([lambda e, src=src, slot=slot: e.dma_start(out=ring[:, slot], in_=src)])
            return n

        def add_load_multi(pairs):
            n = len(loads)
            slot = n % 2
            fns = []
            for (osl, src) in pairs:
                fns.append(lambda e, src=src, osl=osl, slot=slot: e.dma_start(out=ring[:, slot, osl[0]:osl[1]], in_=src))
            loads.append(fns)
            return n

        def kpc(ap2d):
            return ap2d.rearrange("(k p) c -> p k c", p=128)

        plan = {}
        for s in range(n_seq):
            for l in range(n_layers):
                d = {}
                d["win"] = {g: add_load(kpc(w_in[l][:, g * 512:(g + 1) * 512])) for g in (2, 0, 1, 3, 4)}
                d["wo"] = [add_load(kpc(w_out[l][:, g * 512:(g + 1) * 512])) for g in range(2)]
                d["w1"], d["w2"] = [], []
                for g in range(4):
                    d["w1"].append([add_load(kpc(w1[l][:, g * 1024 + a * 512: g * 1024 + (a + 1) * 512])) for a in range(2)])
                    d["w2"].append([add_load(kpc(w2[l][g * 1024:(g + 1) * 1024, dh * 512:(dh + 1) * 512])) for dh in range(2)])
                d["wg0"] = add_load(kpc(wg[l][:, 0:512]))
                d["wp"] = add_load_multi([((0, 2), kpc(wp[l][:, 0:512])), ((2, 4), kpc(wp[l][:, 512:1024]))])
                d["wg1"] = add_load(kpc(wg[l][:, 512:1024]))
                plan[(s, l)] = d

        def nextbank(cands):
            b = cands[st["rr"] % len(cands)]
            st["rr"] += 1
            return b

        S.dma("sp", "dsm", [lambda e: e.dma_start(out=sm[:], in_=smalls)], writes=("sm",))
        S.dma("sp", "dgf", [lambda e: e.dma_start(out=gf[:], in_=gfin)], writes=("gf",))
        S.dma("sp", "divf", [lambda e: e.dma_start(out=ivf[:], in_=invf)], writes=("ivf",))
        S.dma("sp", "didf", [lambda e: e.dma_start(out=idf[:], in_=cst[:, 0:128])], writes=("idf",))
        S.dma("pool", "dcb", [lambda e: e.dma_start(out=cb[:], in_=cst.rearrange("p (a b) -> p a b", a=3))], writes=("cb",))
        S.op("dve", lambda e: e.memset(cst_c[:, 0:1], EPS), writes=("cstc0",))
        S.op("dve", lambda e: e.memset(cst_c[:, 1:2], -PI), writes=("cstc1",))
        S.op("dve", lambda e: e.memset(xr_ext[:, 0:4], 0.0), writes=("xr_ext",))

        def tbs(tb):
            return slice(tb * TB, (tb + 1) * TB)

        def rmsnorm_to_B(gcol_fn, tag):
            for tb in range(NTB):
                bk = nextbank([4, 5, 6, 7])
                for c in range(8):
                    sq = Pb[0][c % 2]
                    sqk = f"Pb0{c % 2}"
                    S.op("act", lambda e, sq=sq, c=c, tb=tb: e.activation(sq[:], hT[:, c, tbs(tb)], AF.Square),
                         reads=(f"h{c}_{tb}",), writes=(sqk,))
                    S.op("pe", lambda e, sq=sq, c=c, bk=bk: e.matmul(bank[bk], ones, sq[:], start=(c == 0), stop=(c == 7)),
                         reads=(sqk, "cb"), writes=(bkey[bk],))
                Er = (E0, E1)[tb % 2]
                Ek = ("E0", "E1")[tb % 2]
                S.op("act", lambda e, Er=Er, bk=bk: e.activation(Er[:], bank[bk], AF.Sqrt, bias=eps_c, scale=1.0 / D),
                     reads=(bkey[bk], "cstc0"), writes=(Ek,))
                S.op("dve", lambda e, Er=Er: e.reciprocal(Er[:], Er[:]), reads=(Ek,), writes=(Ek,))
                for c in range(8):
                    S.op("dve", lambda e, Er=Er, c=c, tb=tb: e.scalar_tensor_tensor(
                        Bt[:, c, tbs(tb)], hT[:, c, tbs(tb)], gcol_fn(c), Er[:], ALU.mult, ALU.mult),
                        reads=(f"h{c}_{tb}", Ek, "sm", "gf"), writes=(f"B{c}_{tb}",))

        def qk_idx(h, tb):
            return h * 4 + tb

        def chk(k):
            if lim is not None and k > lim:
                raise _Stop()

        for s in range(n_seq):
            S.dma("sp", "dx", [(lambda e, s=s, c=c: e.dma_start(out=hT[:, c, :], in_=xT[s, c * 128:(c + 1) * 128, :])) for c in range(8)],
                  writes=[f"h{c}_{tb}" for c in range(8) for tb in range(NTB)])
            S.dma("sp", "dpos", [lambda e, s=s: e.dma_start(out=posi[:], in_=pos[s])], writes=("posi",))
            S.op("dve", lambda e: e.tensor_copy(posf[:], posi[:]), reads=("posi",), writes=("posf",))
            S.op("dve", lambda e: e.tensor_tensor(ang[:], posf[:].unsqueeze(2).to_broadcast([128, NTT, 8]),
                                                  ivf[:].unsqueeze(1).to_broadcast([128, NTT, 8]), ALU.mult),
                 reads=("posf", "ivf"), writes=("ang",))
            S.op("dve", lambda e: e.memset(n2pi, -2 * PI), writes=("xrot",))
            af, cf, sf = ang[:].rearrange("p a b -> p (a b)"), cosT[:].rearrange("p a b -> p (a b)"), sinT[:].rearrange("p a b -> p (a b)")
            S.op("dve", lambda e: e.tensor_scalar(cf, af, 1.0 / (2 * PI), None, ALU.mult), reads=("ang",), writes=("cosT",))
            S.op("dve", lambda e: e.tensor_copy(kint[:], cf), reads=("cosT",), writes=("kint",))
            S.op("dve", lambda e: e.tensor_copy(cf, kint[:]), reads=("kint",), writes=("cosT",))
            S.op("dve", lambda e: e.scalar_tensor_tensor(af, cf, -6.28125, af, ALU.mult, ALU.add), reads=("cosT", "ang"), writes=("ang",))
            S.op("dve", lambda e: e.scalar_tensor_tensor(af, cf, -(2 * PI - 6.28125), af, ALU.mult, ALU.add), reads=("cosT", "ang"), writes=("ang",))
            S.op("dve", lambda e: e.scalar_tensor_tensor(sf, af, PI, n2pi, ALU.is_gt, ALU.mult), reads=("ang", "xrot"), writes=("sinT",))
            S.op("dve", lambda e: e.tensor_tensor(af, af, sf, ALU.add), reads=("ang", "sinT"), writes=("ang",))
            S.op("dve", lambda e: e.tensor_scalar(af, af, -PI, PI, ALU.max, ALU.min), reads=("ang",), writes=("ang",))
            S.op("act", lambda e: e.activation(sf, af, AF.Sin), reads=("ang",), writes=("sinT",))
            S.op("dve", lambda e: e.tensor_scalar(cf, af, 0.5 * PI, None, ALU.add), reads=("ang",), writes=("cosT",))
            S.op("dve", lambda e: e.scalar_tensor_tensor(rtmp, cf, PI, n2pi, ALU.is_gt, ALU.mult), reads=("cosT", "xrot"), writes=("tcv",))
            S.op("dve", lambda e: e.tensor_tensor(cf, cf, rtmp, ALU.add), reads=("cosT", "tcv"), writes=("cosT",))
            S.op("dve", lambda e: e.tensor_scalar(cf, cf, -PI, PI, ALU.max, ALU.min), reads=("cosT",), writes=("cosT",))
            S.op("act", lambda e: e.activation(cf, cf, AF.Sin), reads=("cosT",), writes=("cosT",))
            S.op("dve", lambda e: e.tensor_copy(cos2T[:, :, 0:8], cosT[:]), reads=("cosT",), writes=("cos2T",))
            S.op("dve", lambda e: e.tensor_copy(cos2T[:, :, 8:16], cosT[:]), reads=("cosT", "cos2T"), writes=("cos2T",))

            def emit_layer(s, l):
                pl = plan[(s, l)]
                chk(0)
                lam_init = 0.8 - 0.6 * math.exp(-0.3 * l)
                S.dma("pool", "dgw", [lambda e, l=l: e.dma_start(out=gW[:], in_=gatew[l].rearrange("p (g c j) -> p g c j", g=2, c=4))],
                      writes=("gW",))
                S.dma("sp", "dlqk", [lambda e, l=l: e.dma_start(out=E0[:, 0:256], in_=lqk[l])], writes=("E0",))
                S.op("dve", lambda e: e.tensor_tensor(E0[:, 0:128], E0[:, 0:128], E0[:, 128:256], ALU.mult), reads=("E0",), writes=("E0",))
                S.op("dve", lambda e: e.tensor_reduce(der[:, 0:2], E0[:, 0:128].rearrange("p (a b) -> p a b", a=2), AX.X, ALU.add),
                     reads=("E0",), writes=("der01",))
                S.op("act", lambda e: e.activation(der[:, 0:2], der[:, 0:2], AF.Exp), reads=("der01",), writes=("der01",))
                S.op("dve", lambda e: e.tensor_tensor(der[:, 2:3], der[:, 0:1], der[:, 1:2], ALU.subtract), reads=("der01",), writes=("der2",))
                S.op("dve", lambda e, li=lam_init: e.tensor_scalar(der[:, 3:4], der[:, 2:3], li, -1.0, ALU.add, ALU.mult),
                     reads=("der2",), writes=("neglam",))
                S.op("dve", lambda e, l=l, li=lam_init: e.tensor_scalar(der[:, 4:5], sm[:, l, GS:GS + 1], 1.0 - li, None, ALU.mult),
                     reads=("sm",), writes=("gsub",))
                S.op("act", lambda e, l=l: e.activation(der[:, 5:9], sm[:, l, LL:LL + 4], AF.Exp, scale=-1.0), reads=("sm",), writes=("c1",))
                S.op("act", lambda e: e.activation(der[:, 5:9], der[:, 5:9], AF.Ln, bias=1.0, scale=1.0), reads=("c1",), writes=("c1",))
                S.op("dve", lambda e: e.tensor_scalar(der[:, 9:13], der[:, 5:9], -16.0, None, ALU.mult), reads=("c1",), writes=("c2",))
                S.op("dve", lambda e: e.tensor_scalar(der[:, 5:9], der[:, 5:9], -8.0, None, ALU.mult), reads=("c1", "c2"), writes=("c1",))
                neglam = der[:, 3:4]
                gsub = der[:, 4:5]

                rmsnorm_to_B(lambda c, l=l: sm[:, l, GM + c:GM + c + 1], "n1")

                chk(1)
                wv, wvk_ = use_w(pl["win"][2])
                for tt in range(NTT):
                    tb = tt // 4
                    tsl = slice(tt * 128, (tt + 1) * 128)
                    vb = 4 + (tt % 2)

                    def projv(e, tsl=tsl, vb=vb):
                        ins = None
                        for c in range(8):
                            ins = e.matmul(bank[vb], Bt[:, c, tsl], wv[:, c, :], start=(c == 0), stop=(c == 7))
                        return ins
                    S.op("pe", projv, reads=[f"B{c}_{tb}" for c in range(8)] + [wvk_], writes=(bkey[vb],))
                    if tt % 2 == 0:
                        S.op("dve", lambda e, tt=tt, vb=vb: e.tensor_copy(V[:, tt, :], bank[vb]), reads=(bkey[vb],), writes=(f"V{tt}",))
                    else:
                        S.op("act", lambda e, tt=tt, vb=vb: e.activation(V[:, tt, :], bank[vb], AF.Copy), reads=(bkey[vb],), writes=(f"V{tt}",))

                done_w(pl["win"][2])
                chk(2)
                wq, wqk_ = use_w(pl["win"][0])
                wk, wkk_ = use_w(pl["win"][1], paired=True)
                tcv = tcs_t[:, 128:256].rearrange("p (g d) -> p g d", d=16)
                tsv = tcs_t[:, 256:384].rearrange("p (g d) -> p g d", d=16)
                Fnames = ("gg", "xc")
                Tb = (4, 5)

                def emit_transposes(tt):
                    tb, off = tt // 4, (tt % 4) * 128
                    for hb in range(2):
                        Ft = Rt[Fnames[hb]]
                        pb = Tb[hb]
                        for g in range(4):
                            S.op("pe", lambda e, g=g, Ft=Ft, pb=pb: e.transpose(bank[pb][:, g * 128:(g + 1) * 128], Ft[:, g * 128:(g + 1) * 128], idf[:]),
                                 reads=(Fnames[hb], "idf"), writes=(bkey[pb],))
                        src = bank[pb].rearrange("p (h t) -> p h t", h=4)
                        if hb == 0:
                            S.op("act", lambda e, tb=tb, off=off, src=src: e.activation(QK[:, tb:16:4, off:off + 128], src, AF.Copy),
                                 reads=(bkey[pb],), writes=[f"qk{qk_idx(h, tb)}" for h in range(4)])
                        else:
                            S.op("dve", lambda e, tb=tb, off=off, src=src: e.tensor_copy(QK[:, 16 + tb:32:4, off:off + 128], src),
                                 reads=(bkey[pb],), writes=[f"qk{16 + qk_idx(h, tb)}" for h in range(4)])

                for tt in range(NTT):
                    tb = tt // 4
                    tsl = slice(tt * 128, (tt + 1) * 128)
                    A = psA[tt % 2]
                    Ak = (bkey[0], bkey[1]) if tt % 2 == 0 else (bkey[2], bkey[3])

                    def proj(e, A=A, tsl=tsl):
                        ins = None
                        for c in range(8):
                            e.matmul(A[:, 0:512], Bt[:, c, tsl], wq[:, c, :], start=(c == 0), stop=(c == 7))
                            ins = e.matmul(A[:, 512:1024], Bt[:, c, tsl], wk[:, c, :], start=(c == 0), stop=(c == 7))
                        return ins
                    S.op("pe", proj, reads=[f"B{c}_{tb}" for c in range(8)] + [wqk_, wkk_], writes=(Ak[0], Ak[1]))
                    if tt >= 1 and not os.environ.get('SKIP_TR'):
                        emit_transposes(tt - 1)
                    cos2B = cos2T[:, tt, :].unsqueeze(1).to_broadcast([128, 8, 16])
                    sinB8 = sinT[:, tt, :].unsqueeze(1).to_broadcast([128, 8, 8])
                    for hb in range(2):
                        Ah = A[:, hb * 512:(hb + 1) * 512]
                        Fn = Fnames[hb]
                        Ft = Rt[Fn]
                        if hb == 0:
                            S.op("act", lambda e, Ft=Ft, Ah=Ah: e.activation(Ft[:], Ah, AF.Copy), reads=(Ak[hb],), writes=(Fn,))
                        else:
                            S.op("dve", lambda e, Ft=Ft, Ah=Ah: e.tensor_copy(Ft[:], Ah), reads=(Ak[hb],), writes=(Fn,))
                        if os.environ.get('SKIP_ROPE'):
                            continue
                        F3 = Ft[:].rearrange("p (g d) -> p g d", d=64)
                        S.op("dve", lambda e, F3=F3, cos2B=cos2B: e.tensor_tensor(tcv, F3[:, :, 0:16], cos2B, ALU.mult),
                             reads=(Fn, "cos2T"), writes=("tcv",))
                        S.op("dve", lambda e, F3=F3, sinB8=sinB8: e.scalar_tensor_tensor(tsv[:, :, 0:8], F3[:, :, 8:16], -1.0, sinB8, ALU.mult, ALU.mult),
                             reads=(Fn, "sinT"), writes=("tsv",))
                        S.op("dve", lambda e, F3=F3, sinB8=sinB8: e.tensor_tensor(tsv[:, :, 8:16], F3[:, :, 0:8], sinB8, ALU.mult),
                             reads=(Fn, "sinT", "tsv"), writes=("tsv",))
                        S.op("dve", lambda e, F3=F3: e.tensor_tensor(F3[:, :, 0:16], tcv, tsv, ALU.add),
                             reads=("tcv", "tsv", Fn), writes=(Fn,))
                if not os.environ.get('SKIP_TR'):
                    emit_transposes(NTT - 1)

                done_w(pl["win"][0])
                done_w(pl["win"][1])
                chk(3)
                wxr, wxrk = use_w(pl["win"][3])
                wgr, wgrk = use_w(pl["win"][4], paired=True)

                def rnn_proj1(u):
                    c, tb = u // 4, u % 4
                    def f(e, c=c, tb=tb):
                        ins = None
                        for k in range(8):
                            ins = e.matmul(bank[6], wxr[:, k, c * 128:(c + 1) * 128], Bt[:, k, tbs(tb)], start=(k == 0), stop=(k == 7))
                        return ins
                    S.op("pe", f, reads=[f"B{k}_{tb}" for k in range(8)] + [wxrk], writes=(bkey[6],))
                    if tb == 0:
                        S.op("dve", lambda e: e.memset(xr_ext[:, 0:4], 0.0), reads=("xr_ext",), writes=("xr_ext",))
                    else:
                        S.op("dve", lambda e: e.tensor_copy(xr_ext[:, 0:4], xr_ext[:, 512:516]), reads=("xr_ext",), writes=("xr_ext",))
                    S.op("act", lambda e: e.activation(xr_ext[:, 4:516], bank[6], AF.Copy), reads=(bkey[6],), writes=("xr_ext",))

                def rnn_proj2(u):
                    c, tb = u // 4, u % 4
                    def f2(e, c=c, tb=tb):
                        ins = None
                        for k in range(8):
                            ins = e.matmul(bank[6], wgr[:, k, c * 128:(c + 1) * 128], Bt[:, k, tbs(tb)], start=(k == 0), stop=(k == 7))
                        return ins
                    S.op("pe", f2, reads=[f"B{k}_{tb}" for k in range(8)] + [wgrk], writes=(bkey[6],))
                    S.op("act", lambda e: e.activation(Rt["gg"][:], bank[6], AF.Gelu_apprx_tanh), reads=(bkey[6],), writes=("gg",))
                    cw = lambda k, c=c: sm[:, l, CW + c * 4 + k:CW + c * 4 + k + 1]
                    S.op("dve", lambda e, c=c: e.tensor_scalar(Rt["xc"][:], xr_ext[:, 4:516], cw(3), sm[:, l, CB + c:CB + c + 1], ALU.mult, ALU.add),
                         reads=("xr_ext", "sm"), writes=("xc",))
                    for j in (1, 2, 3):
                        S.op("dve", lambda e, j=j: e.scalar_tensor_tensor(Rt["xc"][:], xr_ext[:, 4 - j:516 - j], cw(3 - j), Rt["xc"][:], ALU.mult, ALU.add),
                             reads=("xr_ext", "xc", "sm"), writes=("xc",))
                    S.op("act", lambda e: e.activation(xcb[:], Rt["xc"][:], AF.Copy), reads=("xc",), writes=("xcb",))

                def rnn_gates1(u):
                    c, tb = u // 4, u % 4
                    S.op("pe", lambda e, c=c: e.matmul(bank[6], gW[:, 0, c, :], xcb[:], start=True, stop=True), reads=("xcb", "gW"), writes=(bkey[6],))
                    S.op("act", lambda e, c=c: e.activation(Rt["r"][:], bank[6], AF.Sigmoid, bias=sm[:, l, BA + c:BA + c + 1], scale=1.0),
                         reads=(bkey[6], "sm"), writes=("r",))

                def rnn_gates2(u):
                    c, tb = u // 4, u % 4
                    S.op("pe", lambda e, c=c: e.matmul(bank[6], gW[:, 1, c, :], xcb[:], start=True, stop=True), reads=("xcb", "gW"), writes=(bkey[6],))
                    S.op("act", lambda e, c=c: e.activation(Rt["i"][:], bank[6], AF.Sigmoid, bias=sm[:, l, BX + c:BX + c + 1], scale=1.0),
                         reads=(bkey[6], "sm"), writes=("i",))
                    S.op("dve", lambda e: e.tensor_tensor(Rt["i"][:], Rt["i"][:], Rt["xc"][:], ALU.mult), reads=("i", "xc"), writes=("i",))
                    S.op("act", lambda e, c=c: e.activation(Rt["xc"][:], Rt["r"][:], AF.Exp, scale=der[:, 9 + c:10 + c]),
                         reads=("r", "c2"), writes=("xc",))
                    S.op("act", lambda e, c=c: e.activation(Rt["r"][:], Rt["r"][:], AF.Exp, scale=der[:, 5 + c:6 + c]),
                         reads=("r", "c1"), writes=("r",))
                    S.op("act", lambda e: e.activation(Rt["xc"][:], Rt["xc"][:], AF.Sqrt, bias=1.0, scale=-1.0), reads=("xc",), writes=("xc",))
                    S.op("dve", lambda e: e.tensor_tensor(Rt["i"][:], Rt["i"][:], Rt["xc"][:], ALU.mult), reads=("i", "xc"), writes=("i",))
                    if tb == 0:
                        S.op("dve", lambda e: e.memset(der[:, 13:14], 0.0), reads=("carry",), writes=("carry",))
                    else:
                        S.op("dve", lambda e: e.tensor_copy(der[:, 13:14], Rt["hs"][:, 511:512]), reads=("hs",), writes=("carry",))
                    S.op("dve", lambda e: e.tensor_tensor_scan(Rt["hs"][:], Rt["r"][:], Rt["i"][:], der[:, 13:14], ALU.mult, ALU.add),
                         reads=("r", "i", "carry"), writes=("hs",))
                    S.op("dve", lambda e, c=c, tb=tb: e.tensor_tensor(Yt[:, c, tbs(tb)], Rt["hs"][:], Rt["gg"][:], ALU.mult),
                         reads=("hs", "gg"), writes=(f"Y{c}_{tb}",))

                pend = {"ss": None}

                def att_block(h, j, u_proj, u_gate):
                    nkt = 4 * j + 4
                    qi = qk_idx(h, j)
                    S0b, S1b, O1b, O2b, R1b, R2b = 0, 1, 2, 3, 4, 5

                    def geo(kt):
                        m = kt - 4 * j
                        q0 = m * 128 if m > 0 else 0
                        tbk, off = kt // 4, (kt % 4) * 128
                        return m, q0, 16 + qk_idx(h, tbk), off

                    def emit_qk_exp(kt):
                        m, q0, ki, off = geo(kt)
                        pbi = kt % 2

                        def qk(e, q0=q0, ki=ki, off=off, qi=qi):
                            e.matmul(bank[S0b][:, q0:512], QK[0:64, ki, off:off + 128], QK[0:64, qi, q0:512], start=True, stop=True, tile_position=(0, 0))
                            return e.matmul(bank[S1b][:, q0:512], QK[64:128, ki, off:off + 128], QK[64:128, qi, q0:512], start=True, stop=True, tile_position=(64, 0))
                        S.op("pe", qk, reads=(f"qk{ki}", f"qk{qi}"), writes=(bkey[S0b], bkey[S1b]))
                        for cc in range(2):
                            P = Pb[pbi][cc]
                            pk = f"Pb{pbi}{cc}"
                            S.op("act", lambda e, P=P, cc=cc, q0=q0: e.activation(P[:, q0:512], bank[cc][:, q0:512], AF.Exp, scale=SCALE),
                                 reads=(bkey[cc],), writes=(pk,))
                            if m >= 0:
                                S.op("dve", lambda e, P=P, q0=q0: e.tensor_tensor(P[:, q0:q0 + 128], P[:, q0:q0 + 128], cmask, ALU.mult),
                                     reads=(pk, "cb"), writes=(pk,))

                    def emit_pv(kt):
                        m, q0, ki, off = geo(kt)
                        pbi = kt % 2

                        def pv(e, q0=q0, kt=kt, pbi=pbi, h=h, first=(kt == 0), last=(kt == nkt - 1)):
                            vv = V[:, kt, h * 128:(h + 1) * 128]
                            e.matmul(bank[O1b][:, q0:512], vv, Pb[pbi][0][:, q0:512], start=first, stop=last)
                            e.matmul(bank[R1b][:, q0:512], ones, Pb[pbi][0][:, q0:512], start=first, stop=last)
                            e.matmul(bank[O2b][:, q0:512], vv, Pb[pbi][1][:, q0:512], start=first, stop=last)
                            return e.matmul(bank[R2b][:, q0:512], ones, Pb[pbi][1][:, q0:512], start=first, stop=last)
                        S.op("pe", pv, reads=(f"V{kt}", f"Pb{pbi}0", f"Pb{pbi}1", "cb"),
                             writes=(bkey[O1b], bkey[O2b], bkey[R1b], bkey[R2b]))

                    emit_qk_exp(0)
                    for kt in range(nkt):
                        if kt + 1 < nkt:
                            emit_qk_exp(kt + 1)
                        emit_pv(kt)
                        if kt == 0:
                            if pend["ss"] is not None:
                                pend["ss"]()
                                pend["ss"] = None
                            if u_gate is not None:
                                rnn_gates1(u_gate)
                        if kt == 1 and u_gate is not None:
                            rnn_gates2(u_gate)
                        if kt == nkt - 2 and u_proj is not None:
                            rnn_proj1(u_proj)
                    S.op("dve", lambda e: e.reciprocal(E0[:], bank[R1b]), reads=(bkey[R1b],), writes=("E0",))
                    S.op("dve", lambda e: e.reciprocal(E1[:], bank[R2b]), reads=(bkey[R2b],), writes=("E1",))
                    S.op("dve", lambda e: e.tensor_tensor(E0[:], bank[O1b], E0[:], ALU.mult), reads=(bkey[O1b], "E0"), writes=("E0",))
                    S.op("dve", lambda e: e.tensor_tensor(E1[:], bank[O2b], E1[:], ALU.mult), reads=(bkey[O2b], "E1"), writes=("E1",))
                    S.op("dve", lambda e: e.scalar_tensor_tensor(E0[:], E1[:], neglam, E0[:], ALU.mult, ALU.add),
                         reads=("E0", "E1", "neglam"), writes=("E0",))
                    S.op("act", lambda e: e.activation(E2[:], E0[:], AF.Square), reads=("E0",), writes=("E2",))

                    def part_b(qi=qi):
                        S.op("pe", lambda e: e.matmul(bank[7], ones, E2[:], start=True, stop=True), reads=("E2", "cb"), writes=(bkey[7],))
                        S.op("act", lambda e: e.activation(E1[:], bank[7], AF.Sqrt, bias=eps_c, scale=1.0 / 128), reads=(bkey[7], "cstc0"), writes=("E1",))
                        S.op("dve", lambda e: e.reciprocal(E1[:], E1[:]), reads=("E1",), writes=("E1",))
                        S.op("dve", lambda e, qi=qi: e.scalar_tensor_tensor(QK[:, qi, :], E0[:], gsub, E1[:], ALU.mult, ALU.mult),
                             reads=("E0", "E1", "gsub"), writes=(f"qk{qi}",))
                    pend["ss"] = part_b
                    if u_proj is not None:
                        rnn_proj2(u_proj)

                blocks = [(h, j) for h in range(4) for j in range(NTB)]
                rnn_proj1(0)
                rnn_proj2(0)
                for bi, (h, j) in enumerate(blocks):
                    att_block(h, j, bi + 1 if bi + 1 < 16 else None, bi)
                pend["ss"]()
                pend["ss"] = None

                done_w(pl["win"][3])
                done_w(pl["win"][4])
                chk(4)
                for g2 in range(2):
                    if g2 == 1:
                        done_w(pl["wo"][0])
                    wo_, wok = use_w(pl["wo"][g2])
                    for d4 in range(4):
                        dtc = g2 * 4 + d4
                        for tb in range(NTB):
                            bk = nextbank([0, 1, 2, 3, 4, 5, 6, 7])
                            def f(e, d4=d4, tb=tb, bk=bk, wo_=wo_):
                                ins = None
                                for c in range(4):
                                    e.matmul(bank[bk], wo_[:, c, d4 * 128:(d4 + 1) * 128], QK[:, c * 4 + tb, :], start=(c == 0), stop=False)
                                for c in range(4):
                                    ins = e.matmul(bank[bk], wo_[:, 4 + c, d4 * 128:(d4 + 1) * 128], Yt[:, c, tbs(tb)], start=False, stop=(c == 3))
                                return ins
                            S.op("pe", f, reads=[f"qk{c * 4 + tb}" for c in range(4)] + [f"Y{c}_{tb}" for c in range(4)] + [wok], writes=(bkey[bk],))
                            S.op("dve", lambda e, dtc=dtc, tb=tb, bk=bk: e.tensor_tensor(hT[:, dtc, tbs(tb)], bank[bk], hT[:, dtc, tbs(tb)], ALU.add),
                                 reads=(bkey[bk], f"h{dtc}_{tb}"), writes=(f"h{dtc}_{tb}",))

                chk(5)
                done_w(pl["wo"][1])
                rmsnorm_to_B(lambda c, l=l: sm[:, l, GL + c:GL + c + 1], "n2")
                pTb = V[:, 0:8, :].rearrange("p a b -> p (a b)").rearrange("p (k t) -> p k t", k=2)
                S.dma("pool", "dpt", [(lambda e, l=l, s=s, k=k: e.dma_start(out=pTb[:, k, :], in_=pT[l, s, k * 128:(k + 1) * 128, :])) for k in range(2)],
                      writes=[f"V{t}" for t in range(8)])
                sqt = [Rt["gg"], Rt["xc"]]
                sqk = ["gg", "xc"]
                cnt = 0
                for g in range(4):
                    for a in range(2):
                        w1_, w1k = use_w(pl["w1"][g][a])
                        for f4 in range(4):
                            fc = a * 4 + f4
                            for tb in range(NTB):
                                bk = nextbank([0, 1, 2, 3, 4, 5, 6, 7])
                                hk = f"qk{fc * 4 + tb}"
                                def f(e, f4=f4, tb=tb, bk=bk, w1_=w1_):
                                    ins = None
                                    for c in range(8):
                                        ins = e.matmul(bank[bk], w1_[:, c, f4 * 128:(f4 + 1) * 128], Bt[:, c, tbs(tb)], start=(c == 0), stop=(c == 7))
                                    return ins
                                S.op("pe", f, reads=[f"B{c}_{tb}" for c in range(8)] + [w1k], writes=(bkey[bk],))
                                ti = cnt % 2
                                cnt += 1
                                S.op("act", lambda e, bk=bk, ti=ti: e.activation(sqt[ti][:], bank[bk], AF.Square), reads=(bkey[bk],), writes=(sqk[ti],))
                                S.op("dve", lambda e, bk=bk, ti=ti, fc=fc, tb=tb: e.scalar_tensor_tensor(
                                    QK[:, fc * 4 + tb, :], bank[bk], 0.0, sqt[ti][:], ALU.is_gt, ALU.mult),
                                    reads=(bkey[bk], sqk[ti]), writes=(hk,))
                    for dh in range(2):
                        w2_, w2k = use_w(pl["w2"][g][dh])
                        for d4 in range(4):
                            dtc = dh * 4 + d4
                            for tb in range(NTB):
                                bk = nextbank([0, 1, 2, 3, 4, 5, 6, 7])
                                def f(e, d4=d4, tb=tb, bk=bk, w2_=w2_):
                                    ins = None
                                    for fc in range(8):
                                        ins = e.matmul(bank[bk], w2_[:, fc, d4 * 128:(d4 + 1) * 128], QK[:, fc * 4 + tb, :], start=(fc == 0), stop=(fc == 7))
                                    return ins
                                S.op("pe", f, reads=[f"qk{fc * 4 + tb}" for fc in range(8)] + [w2k], writes=(bkey[bk],))
                                S.op("dve", lambda e, dtc=dtc, tb=tb, bk=bk: e.tensor_tensor(hT[:, dtc, tbs(tb)], bank[bk], hT[:, dtc, tbs(tb)], ALU.add),
                                     reads=(bkey[bk], f"h{dtc}_{tb}"), writes=(f"h{dtc}_{tb}",))

                chk(6)
                rmsnorm_to_B(lambda c, l=l: sm[:, l, GP + c:GP + c + 1], "n3")
                sgt = [Rt["r"], Rt["i"]]
                sgk = ["r", "i"]
                t2t = [Rt["hs"], Rt["gg"]]
                t2k = ["hs", "gg"]
                cnt = 0
                wp_, wpk = None, None
                for dh in range(2):
                    if dh == 0:
                        wg_, wgk_ = use_w(pl["wg0"])
                        wp_, wpk = use_w(pl["wp"], paired=True)
                    else:
                        done_w(pl["wg0"])
                        wg_, wgk_ = use_w(pl["wg1"], paired=True)
                    for d4 in range(4):
                        dtc = dh * 4 + d4
                        for tb in range(NTB):
                            bg = nextbank([0, 1, 2, 3, 4, 5, 6, 7])
                            bp = nextbank([0, 1, 2, 3, 4, 5, 6, 7])
                            def f(e, d4=d4, tb=tb, bg=bg, wg_=wg_):
                                ins = None
                                for c in range(8):
                                    ins = e.matmul(bank[bg], wg_[:, c, d4 * 128:(d4 + 1) * 128], Bt[:, c, tbs(tb)], start=(c == 0), stop=(c == 7))
                                return ins
                            S.op("pe", f, reads=[f"B{c}_{tb}" for c in range(8)] + [wgk_], writes=(bkey[bg],))
                            def f2(e, d4=d4, tb=tb, bp=bp, dh=dh, wp_=wp_):
                                ins = None
                                for k in range(2):
                                    ins = e.matmul(bank[bp], wp_[:, dh * 2 + k, d4 * 128:(d4 + 1) * 128], pTb[:, k, tbs(tb)], start=(k == 0), stop=(k == 1))
                                return ins
                            S.op("pe", f2, reads=[f"V{t}" for t in range(8)] + [wpk], writes=(bkey[bp],))
                            ti = cnt % 2
                            cnt += 1
                            S.op("act", lambda e, bg=bg, ti=ti: e.activation(sgt[ti][:], bank[bg], AF.Sigmoid), reads=(bkey[bg],), writes=(sgk[ti],))
                            S.op("dve", lambda e, bp=bp, ti=ti: e.tensor_tensor(t2t[ti][:], bank[bp], sgt[ti][:], ALU.mult),
                                 reads=(bkey[bp], sgk[ti]), writes=(t2k[ti],))
                            S.op("dve", lambda e, dtc=dtc, tb=tb, ti=ti: e.tensor_tensor(hT[:, dtc, tbs(tb)], t2t[ti][:], hT[:, dtc, tbs(tb)], ALU.add),
                                 reads=(t2k[ti], f"h{dtc}_{tb}"), writes=(f"h{dtc}_{tb}",))

            for l in range(n_layers):
                try:
                    emit_layer(s, l)
                except _Stop:
                    pass
            for tb in range(NTB):
                bk = nextbank([4, 5, 6, 7])
                for c in range(8):
                    sq = Pb[0][c % 2]
                    sqk_ = f"Pb0{c % 2}"
                    S.op("act", lambda e, sq=sq, c=c, tb=tb: e.activation(sq[:], hT[:, c, tbs(tb)], AF.Square),
                         reads=(f"h{c}_{tb}",), writes=(sqk_,))
                    S.op("pe", lambda e, sq=sq, c=c, bk=bk: e.matmul(bank[bk], ones, sq[:], start=(c == 0), stop=(c == 7)),
                         reads=(sqk_, "cb"), writes=(bkey[bk],))
                Er = (E0, E1)[tb % 2]
                Ek = ("E0", "E1")[tb % 2]
                S.op("act", lambda e, Er=Er, bk=bk: e.activation(Er[:], bank[bk], AF.Sqrt, bias=eps_c, scale=1.0 / D),
                     reads=(bkey[bk], "cstc0"), writes=(Ek,))
                S.op("dve", lambda e, Er=Er: e.reciprocal(Er[:], Er[:]), reads=(Ek,), writes=(Ek,))
                for c in range(8):
                    names = ("gg", "xc", "r", "i", "hs")
                    nm = names[c % 5]
                    ot = Rt[nm]
                    S.op("dve", lambda e, Er=Er, c=c, tb=tb, ot=ot: e.scalar_tensor_tensor(
                        ot[:], hT[:, c, tbs(tb)], gf[:, c:c + 1], Er[:], ALU.mult, ALU.mult),
                        reads=(f"h{c}_{tb}", Ek, "gf"), writes=(nm,))
                    S.dma("sp", "dout_" + nm, [lambda e, ot=ot, c=c, tb=tb, s=s: e.dma_start(out=outT[s, c * 128:(c + 1) * 128, tbs(tb)], in_=ot[:])],
                          reads=(nm,))

        sem_names = list(S.ENG) + sorted(S.dcnt.keys())
        sems = {n: es.enter_context(nc.semaphore("s_" + n)) for n in sem_names}
        final_waits = [(n, S.cnt[n]) for n in S.ENG if n != "sp" and S.cnt[n] > 0] + [(n, v) for n, v in S.dcnt.items()]
        block = es.enter_context(nc.Block())

        def replay(name, e):
            for waits, fn, (sn, inc) in S.q[name]:
                for (ws, wv) in waits:
                    e.wait_ge(sems[ws], wv)
                ins = fn(e)
                ins.then_inc(sems[sn], inc)
            if name == "sp":
                for (ws, wv) in final_waits:
                    e.wait_ge(sems[ws], wv)

        @block.tensor
        def _(e):
            replay("pe", e)

        @block.scalar
        def _(e):
            replay("act", e)

        @block.vector
        def _(e):
            replay("dve", e)

        @block.gpsimd
        def _(e):
            replay("pool", e)

        @block.sync
        def _(e):
            replay("sp", e)
    return nc, S


def _host_consts():
    ident = np.eye(128, dtype=np.float32)
    ones = np.ones((128, 128), np.float32)
    k = np.arange(128)[:, None]
    q = np.arange(128)[None, :]
    mask = (q >= k).astype(np.float32)
    cst = np.concatenate([ident, ones, mask], axis=1)
    half = 8
    inv_freq = (np.float32(500000.0) ** (-np.arange(half, dtype=np.float32) * np.float32(2.0) / np.float32(16))).astype(np.float32)
    invf = np.broadcast_to(inv_freq[None, :], (128, 8)).copy()
    return cst, invf


def _layout_inputs(inp, n_layers=L):
    f = lambda a: np.ascontiguousarray(np.asarray(a, dtype=np.float32))
    col8 = lambda v: v.reshape(8, 128).T
    col4 = lambda v: v.reshape(4, 128).T
    smalls = np.zeros((128, L, NSM), np.float32)
    gatew = np.zeros((L, 128, 2, 4, 128), np.float32)
    lqk = np.zeros((L, 128, 256), np.float32)
    for l in range(L):
        smalls[:, l, GM:GM + 8] = col8(f(inp["g_mix"][l]))
        smalls[:, l, GL:GL + 8] = col8(f(inp["g_mlp"][l]))
        smalls[:, l, GP:GP + 8] = col8(f(inp["g_ple"][l]))
        cw = f(inp["conv_w"][l])
        for c in range(4):
            for k in range(4):
                smalls[:, l, CW + c * 4 + k] = cw[k, c * 128:(c + 1) * 128]
        smalls[:, l, CB:CB + 4] = col4(f(inp["conv_b"][l]))
        smalls[:, l, BA:BA + 4] = col4(f(inp["b_gate_a"][l]))
        smalls[:, l, BX:BX + 4] = col4(f(inp["b_gate_x"][l]))
        smalls[:, l, LL:LL + 4] = col4(f(inp["lru_lambda"][l]))
        smalls[:, l, GS] = f(inp["g_subln"][l])
        for gi, nm in enumerate(("w_gate_a", "w_gate_x")):
            w = f(inp[nm][l])
            for c in range(4):
                for b in range(2):
                    gatew[l, b * 64:(b + 1) * 64, gi, c, b * 64:(b + 1) * 64] = w[2 * c + b]
        lqk[l, :, 0:128] = f(inp["lam_q"][l]).reshape(1, 128)
        lqk[l, :, 128:256] = f(inp["lam_k"][l]).reshape(1, 128)
    gatew = gatew.reshape(L, 128, 1024)
    gfin = np.ascontiguousarray(col8(f(inp["g_final"])))
    cst, invf = _host_consts()
    x = f(inp["x"])
    p = f(inp["p"])
    posn = np.asarray(inp["positions"]).astype(np.int32)
    shared = dict(w_in=f(inp["w_in"]), w_out=f(inp["w_out"]), w_mlp_in=f(inp["w_mlp_in"]), w_mlp_out=f(inp["w_mlp_out"]),
                  w_ple_gate=f(inp["w_ple_gate"]), w_ple_proj=f(inp["w_ple_proj"]), smalls=smalls, gatew=gatew, lqk=lqk,
                  gfin=gfin, cst=cst, invf=invf)
    maps = []
    for i in range(NC):
        xs = x[2 * i:2 * i + 2]
        m = dict(shared)
        m["xT"] = np.ascontiguousarray(xs.transpose(0, 2, 1))
        m["pT"] = np.ascontiguousarray(p[:, 2 * i:2 * i + 2].transpose(0, 1, 3, 2))
        m["pos"] = np.ascontiguousarray(posn[2 * i:2 * i + 2].reshape(2, NTT, 128).transpose(0, 2, 1))
        maps.append(m)
    return maps


_CACHE = {}


def kernel(**inputs):
    maps = _layout_inputs(inputs)
    if "nc" not in _CACHE:
        _CACHE["nc"] = build_program()[0]
    nc = _CACHE["nc"]
    res = run_bass_kernel_spmd(nc, maps, core_ids=list(range(NC)))
    out = np.empty((16, T, D), np.float32)
    for i in range(NC):
        o = res.results[i]["outT"]
        out[2 * i:2 * i + 2] = o.transpose(0, 2, 1)
    return out
```

```python
import math
import os
from contextlib import ExitStack

import numpy as np
import concourse.bass as bass
import concourse.mybir as mybir
from concourse.bass_utils import run_bass_kernel_spmd

F32 = mybir.dt.float32
BF16 = mybir.dt.bfloat16
I32 = mybir.dt.int32
AF = mybir.ActivationFunctionType
ALU = mybir.AluOpType
AX = mybir.AxisListType

D = 1024
T = 2048
L = 4
NC = 8
TB = 512
NTB = 4
NTT = 16
EPS = 1e-6
SCALE = 0.125
GM, GL, GP, CW, CB, BA, BX, LL, GS, NSM = 0, 8, 16, 24, 40, 44, 48, 52, 56, 64
PI = math.pi


class Sched:
    ENG = ("pe", "act", "dve", "pool", "sp")

    def __init__(self):
        self.q = {e: [] for e in self.ENG}
        self.cnt = {e: 0 for e in self.ENG}
        self.waited = {e: {} for e in self.ENG}
        self.lastw = {}
        self.readers = {}
        self.dcnt = {}

    def _waits(self, eng, reads, writes):
        need = {}
        for k in reads:
            ev = self.lastw.get(k)
            if ev is not None:
                need[ev[0]] = max(need.get(ev[0], 0), ev[1])
        for k in writes:
            ev = self.lastw.get(k)
            if ev is not None:
                need[ev[0]] = max(need.get(ev[0], 0), ev[1])
            for ev in self.readers.get(k, ()):
                need[ev[0]] = max(need.get(ev[0], 0), ev[1])
        waits = []
        for s, v in need.items():
            if eng == "pe" and s == "pe":
                continue
            if self.waited[eng].get(s, 0) < v:
                self.waited[eng][s] = v
                waits.append((s, v))
        return waits

    def _commit(self, ev, reads, writes):
        for k in reads:
            self.readers.setdefault(k, []).append(ev)
        for k in writes:
            self.lastw[k] = ev
            self.readers[k] = []

    def op(self, eng, fn, reads=(), writes=()):
        waits = self._waits(eng, reads, writes)
        self.cnt[eng] += 1
        ev = (eng, self.cnt[eng])
        self.q[eng].append((waits, fn, (eng, 1)))
        self._commit(ev, reads, writes)
        return ev

    def dma(self, queue, dsem, fns, reads=(), writes=()):
        waits = self._waits(queue, reads, writes)
        first = True
        for fn in fns:
            self.dcnt[dsem] = self.dcnt.get(dsem, 0) + 16
            self.q[queue].append((waits if first else [], fn, (dsem, 16)))
            first = False
        ev = (dsem, self.dcnt[dsem])
        self._commit(ev, reads, writes)
        return ev


class _Stop(Exception):
    pass


def build_program(n_layers=L, n_seq=2, lim=None):
    nc = bass.Bass("TRN2", target_bir_lowering=False)
    dt_in = lambda name, shape, dt=F32: nc.dram_tensor(name, shape, dt, kind="ExternalInput").ap()
    xT = dt_in("xT", [2, D, T])
    pT = dt_in("pT", [L, 2, 256, T])
    pos = dt_in("pos", [2, 128, NTT], I32)
    w_in = dt_in("w_in", [L, D, 2560])
    w_out = dt_in("w_out", [L, D, D])
    w1 = dt_in("w_mlp_in", [L, D, 4096])
    w2 = dt_in("w_mlp_out", [L, 4096, D])
    wg = dt_in("w_ple_gate", [L, D, D])
    wp = dt_in("w_ple_proj", [L, 256, D])
    smalls = dt_in("smalls", [128, L, NSM])
    gatew = dt_in("gatew", [L, 128, 1024])
    lqk = dt_in("lqk", [L, 128, 256])
    gfin = dt_in("gfin", [128, 8])
    cst = dt_in("cst", [128, 3 * 128])
    invf = dt_in("invf", [128, 8])
    outT = nc.dram_tensor("outT", [2, D, T], F32, kind="ExternalOutput").ap()

    S = Sched()
    with ExitStack() as es:
        sb = lambda name, shape, dt: es.enter_context(nc.sbuf_tensor(name, shape, dt))
        hT = sb("hT", [128, 8, T], F32)
        Bt = sb("Bt", [128, 8, T], BF16)
        QK = sb("QK", [128, 32, TB], BF16)
        V = sb("V", [128, NTT, 512], BF16)
        ring = sb("ring", [128, 2, 8, 512], BF16)
        Yt = sb("Yt", [128, 4, T], BF16)
        sm = sb("sm", [128, L, NSM], F32)
        gW = sb("gW", [128, 2, 4, 128], BF16)
        gf = sb("gf", [128, 8], F32)
        cb = sb("cb", [128, 3, 128], BF16)
        ivf = sb("ivf", [128, 8], F32)
        idf = sb("idf", [128, 128], F32)
        posi = sb("posi", [128, NTT], I32)
        posf = sb("posf", [128, NTT], F32)
        ang = sb("ang", [128, NTT, 8], F32)
        cosT = sb("cosT", [128, NTT, 8], F32)
        sinT = sb("sinT", [128, NTT, 8], F32)
        der = sb("der", [128, 16], F32)
        kint = sb("kint", [128, 128], I32)
        cst_c = sb("cst_c", [128, 4], F32)
        Pb = [[sb(f"Pb{i}{c}", [128, 512], BF16) for c in range(2)] for i in range(2)]
        E0 = sb("E0", [128, 512], F32)
        E1 = sb("E1", [128, 512], F32)
        E2 = sb("E2", [128, 512], BF16)
        xr_ext = sb("xr_ext", [128, 516], F32)
        Rt = {n: sb("R_" + n, [128, 512], F32) for n in ("gg", "xc", "r", "i", "hs")}
        xcb = sb("xcb", [128, 512], BF16)
        tcs_t = sb("tcs_t", [128, 512], F32)
        cos2T = sb("cos2T", [128, NTT, 16], F32)
        n2pi = tcs_t[:, 0:128]
        rtmp = tcs_t[:, 128:256]
        psA = [es.enter_context(nc.psum_tensor(f"psA{i}", [128, 1024], F32)) for i in range(2)]
        psB = [es.enter_context(nc.psum_tensor(f"psB{i}", [128, 512], F32)) for i in range(4)]
        bank = [psA[0][:, 0:512], psA[0][:, 512:1024], psA[1][:, 0:512], psA[1][:, 512:1024],
                psB[0][:], psB[1][:], psB[2][:], psB[3][:]]
        bkey = [f"ps{i}" for i in range(8)]

        ident = cb[:, 0, :]
        ones = cb[:, 1, :]
        cmask = cb[:, 2, :]
        eps_c = cst_c[:, 0:1]
        npi_c = cst_c[:, 1:2]

        loads = []
        st = {"next_load": 0, "n_emitted": 0, "rr": 0, "done": -1, "pending_done": []}

        def ring_key(n):
            return f"ring{n % 2}"

        def emit_load(n):
            fns = loads[n]
            S.dma("pool", f"dring{n % 2}", fns, reads=(), writes=(ring_key(n),))

        def _emit_upto(m):
            m = min(m, len(loads) - 1)
            while st["n_emitted"] <= m:
                emit_load(st["n_emitted"])
                st["n_emitted"] += 1

        def use_w(n, paired=False):
            if not paired:
                st["done"] = max(st["done"], n - 1)
            _emit_upto(max(n, st["done"] + 2))
            return ring[:, n % 2], ring_key(n)

        def done_w(n):
            st["done"] = max(st["done"], n)
            _emit_upto(st["done"] + 2)

        def mk_load(src_ap_fn, n_index_holder):
            pass

        def add_load(src):
            n = len(loads)
            slot = n % 2
            loads.append([lambda e, src=src, slot=slot: e.dma_start(out=ring[:, slot], in_=src)])
            return n

        def add_load_multi(pairs):
            n = len(loads)
            slot = n % 2
            fns = []
            for (osl, src) in pairs:
                fns.append(lambda e, src=src, osl=osl, slot=slot: e.dma_start(out=ring[:, slot, osl[0]:osl[1]], in_=src))
            loads.append(fns)
            return n

        def kpc(ap2d):
            return ap2d.rearrange("(k p) c -> p k c", p=128)

        plan = {}
        for s in range(n_seq):
            for l in range(n_layers):
                d = {}
                d["win"] = {g: add_load(kpc(w_in[l][:, g * 512:(g + 1) * 512])) for g in (2, 0, 1, 3, 4)}
                d["wo"] = [add_load(kpc(w_out[l][:, g * 512:(g + 1) * 512])) for g in range(2)]
                d["w1"], d["w2"] = [], []
                for g in range(4):
                    d["w1"].append([add_load(kpc(w1[l][:, g * 1024 + a * 512: g * 1024 + (a + 1) * 512])) for a in range(2)])
                    d["w2"].append([add_load(kpc(w2[l][g * 1024:(g + 1) * 1024, dh * 512:(dh + 1) * 512])) for dh in range(2)])
                d["wg0"] = add_load(kpc(wg[l][:, 0:512]))
                d["wp"] = add_load_multi([((0, 2), kpc(wp[l][:, 0:512])), ((2, 4), kpc(wp[l][:, 512:1024]))])
                d["wg1"] = add_load(kpc(wg[l][:, 512:1024]))
                plan[(s, l)] = d

        def nextbank(cands):
            b = cands[st["rr"] % len(cands)]
            st["rr"] += 1
            return b

        S.dma("sp", "dsm", [lambda e: e.dma_start(out=sm[:], in_=smalls)], writes=("sm",))
        S.dma("sp", "dgf", [lambda e: e.dma_start(out=gf[:], in_=gfin)], writes=("gf",))
        S.dma("sp", "divf", [lambda e: e.dma_start(out=ivf[:], in_=invf)], writes=("ivf",))
        S.dma("sp", "didf", [lambda e: e.dma_start(out=idf[:], in_=cst[:, 0:128])], writes=("idf",))
        S.dma("pool", "dcb", [lambda e: e.dma_start(out=cb[:], in_=cst.rearrange("p (a b) -> p a b", a=3))], writes=("cb",))
        S.op("dve", lambda e: e.memset(cst_c[:, 0:1], EPS), writes=("cstc0",))
        S.op("dve", lambda e: e.memset(cst_c[:, 1:2], -PI), writes=("cstc1",))
        S.op("dve", lambda e: e.memset(xr_ext[:, 0:4], 0.0), writes=("xr_ext",))

        def tbs(tb):
            return slice(tb * TB, (tb + 1) * TB)

        def rmsnorm_to_B(gcol_fn, tag):
            for tb in range(NTB):
                rmsnorm_tb(gcol_fn, tb)

        def rmsnorm_tb(gcol_fn, tb):
            if True:
                bk = nextbank([4, 5, 6, 7])
                for c in range(8):
                    sq = Pb[0][c % 2]
                    sqk = f"Pb0{c % 2}"
                    S.op("act", lambda e, sq=sq, c=c, tb=tb: e.activation(sq[:], hT[:, c, tbs(tb)], AF.Square),
                         reads=(f"h{c}_{tb}",), writes=(sqk,))
                    S.op("pe", lambda e, sq=sq, c=c, bk=bk: e.matmul(bank[bk], ones, sq[:], start=(c == 0), stop=(c == 7)),
                         reads=(sqk, "cb"), writes=(bkey[bk],))
                Er = (E0, E1)[tb % 2]
                Ek = ("E0", "E1")[tb % 2]
                S.op("act", lambda e, Er=Er, bk=bk: e.activation(Er[:], bank[bk], AF.Sqrt, bias=eps_c, scale=1.0 / D),
                     reads=(bkey[bk], "cstc0"), writes=(Ek,))
                S.op("dve", lambda e, Er=Er: e.reciprocal(Er[:], Er[:]), reads=(Ek,), writes=(Ek,))
                for c in range(8):
                    S.op("dve", lambda e, Er=Er, c=c, tb=tb: e.scalar_tensor_tensor(
                        Bt[:, c, tbs(tb)], hT[:, c, tbs(tb)], gcol_fn(c), Er[:], ALU.mult, ALU.mult),
                        reads=(f"h{c}_{tb}", Ek, "sm", "gf"), writes=(f"B{c}_{tb}",))

        def qk_idx(h, tb):
            return h * 4 + tb

        def chk(k):
            if lim is not None and k > lim:
                raise _Stop()

        for s in range(n_seq):
            S.dma("sp", "dx", [(lambda e, s=s, c=c: e.dma_start(out=hT[:, c, :], in_=xT[s, c * 128:(c + 1) * 128, :])) for c in range(8)],
                  writes=[f"h{c}_{tb}" for c in range(8) for tb in range(NTB)])
            S.dma("sp", "dpos", [lambda e, s=s: e.dma_start(out=posi[:], in_=pos[s])], writes=("posi",))
            S.op("dve", lambda e: e.tensor_copy(posf[:], posi[:]), reads=("posi",), writes=("posf",))
            S.op("dve", lambda e: e.tensor_tensor(ang[:], posf[:].unsqueeze(2).to_broadcast([128, NTT, 8]),
                                                  ivf[:].unsqueeze(1).to_broadcast([128, NTT, 8]), ALU.mult),
                 reads=("posf", "ivf"), writes=("ang",))
            S.op("dve", lambda e: e.memset(n2pi, -2 * PI), writes=("xrot",))
            af, cf, sf = ang[:].rearrange("p a b -> p (a b)"), cosT[:].rearrange("p a b -> p (a b)"), sinT[:].rearrange("p a b -> p (a b)")
            S.op("dve", lambda e: e.tensor_scalar(cf, af, 1.0 / (2 * PI), None, ALU.mult), reads=("ang",), writes=("cosT",))
            S.op("dve", lambda e: e.tensor_copy(kint[:], cf), reads=("cosT",), writes=("kint",))
            S.op("dve", lambda e: e.tensor_copy(cf, kint[:]), reads=("kint",), writes=("cosT",))
            S.op("dve", lambda e: e.scalar_tensor_tensor(af, cf, -6.28125, af, ALU.mult, ALU.add), reads=("cosT", "ang"), writes=("ang",))
            S.op("dve", lambda e: e.scalar_tensor_tensor(af, cf, -(2 * PI - 6.28125), af, ALU.mult, ALU.add), reads=("cosT", "ang"), writes=("ang",))
            S.op("dve", lambda e: e.scalar_tensor_tensor(sf, af, PI, n2pi, ALU.is_gt, ALU.mult), reads=("ang", "xrot"), writes=("sinT",))
            S.op("dve", lambda e: e.tensor_tensor(af, af, sf, ALU.add), reads=("ang", "sinT"), writes=("ang",))
            S.op("dve", lambda e: e.tensor_scalar(af, af, -PI, PI, ALU.max, ALU.min), reads=("ang",), writes=("ang",))
            S.op("act", lambda e: e.activation(sf, af, AF.Sin), reads=("ang",), writes=("sinT",))
            S.op("dve", lambda e: e.tensor_scalar(cf, af, 0.5 * PI, None, ALU.add), reads=("ang",), writes=("cosT",))
            S.op("dve", lambda e: e.scalar_tensor_tensor(rtmp, cf, PI, n2pi, ALU.is_gt, ALU.mult), reads=("cosT", "xrot"), writes=("tcv",))
            S.op("dve", lambda e: e.tensor_tensor(cf, cf, rtmp, ALU.add), reads=("cosT", "tcv"), writes=("cosT",))
            S.op("dve", lambda e: e.tensor_scalar(cf, cf, -PI, PI, ALU.max, ALU.min), reads=("cosT",), writes=("cosT",))
            S.op("act", lambda e: e.activation(cf, cf, AF.Sin), reads=("cosT",), writes=("cosT",))
            S.op("dve", lambda e: e.tensor_copy(cos2T[:, :, 0:8], cosT[:]), reads=("cosT",), writes=("cos2T",))
            S.op("dve", lambda e: e.tensor_copy(cos2T[:, :, 8:16], cosT[:]), reads=("cosT", "cos2T"), writes=("cos2T",))

            def emit_layer(s, l):
                pl = plan[(s, l)]
                chk(0)
                lam_init = 0.8 - 0.6 * math.exp(-0.3 * l)
                S.dma("pool", "dgw", [lambda e, l=l: e.dma_start(out=gW[:], in_=gatew[l].rearrange("p (g c j) -> p g c j", g=2, c=4))],
                      writes=("gW",))
                S.dma("sp", "dlqk", [lambda e, l=l: e.dma_start(out=E0[:, 0:256], in_=lqk[l])], writes=("E0",))
                S.op("dve", lambda e: e.tensor_tensor(E0[:, 0:128], E0[:, 0:128], E0[:, 128:256], ALU.mult), reads=("E0",), writes=("E0",))
                S.op("dve", lambda e: e.tensor_reduce(der[:, 0:2], E0[:, 0:128].rearrange("p (a b) -> p a b", a=2), AX.X, ALU.add),
                     reads=("E0",), writes=("der01",))
                S.op("act", lambda e: e.activation(der[:, 0:2], der[:, 0:2], AF.Exp), reads=("der01",), writes=("der01",))
                S.op("dve", lambda e: e.tensor_tensor(der[:, 2:3], der[:, 0:1], der[:, 1:2], ALU.subtract), reads=("der01",), writes=("der2",))
                S.op("dve", lambda e, li=lam_init: e.tensor_scalar(der[:, 3:4], der[:, 2:3], li, -1.0, ALU.add, ALU.mult),
                     reads=("der2",), writes=("neglam",))
                S.op("dve", lambda e, l=l, li=lam_init: e.tensor_scalar(der[:, 4:5], sm[:, l, GS:GS + 1], 1.0 - li, None, ALU.mult),
                     reads=("sm",), writes=("gsub",))
                S.op("act", lambda e, l=l: e.activation(der[:, 5:9], sm[:, l, LL:LL + 4], AF.Exp, scale=-1.0), reads=("sm",), writes=("c1",))
                S.op("act", lambda e: e.activation(der[:, 5:9], der[:, 5:9], AF.Ln, bias=1.0, scale=1.0), reads=("c1",), writes=("c1",))
                S.op("dve", lambda e: e.tensor_scalar(der[:, 9:13], der[:, 5:9], -16.0, None, ALU.mult), reads=("c1",), writes=("c2",))
                S.op("dve", lambda e: e.tensor_scalar(der[:, 5:9], der[:, 5:9], -8.0, None, ALU.mult), reads=("c1", "c2"), writes=("c1",))
                neglam = der[:, 3:4]
                gsub = der[:, 4:5]

                rmsnorm_to_B(lambda c, l=l: sm[:, l, GM + c:GM + c + 1], "n1")

                chk(1)
                wv, wvk_ = use_w(pl["win"][2])
                for tt in range(NTT):
                    tb = tt // 4
                    tsl = slice(tt * 128, (tt + 1) * 128)
                    vb = 4 + (tt % 2)

                    def projv(e, tsl=tsl, vb=vb):
                        ins = None
                        for c in range(8):
                            ins = e.matmul(bank[vb], Bt[:, c, tsl], wv[:, c, :], start=(c == 0), stop=(c == 7))
                        return ins
                    S.op("pe", projv, reads=[f"B{c}_{tb}" for c in range(8)] + [wvk_], writes=(bkey[vb],))
                    if tt % 2 == 0:
                        S.op("dve", lambda e, tt=tt, vb=vb: e.tensor_copy(V[:, tt, :], bank[vb]), reads=(bkey[vb],), writes=(f"V{tt}",))
                    else:
                        S.op("act", lambda e, tt=tt, vb=vb: e.activation(V[:, tt, :], bank[vb], AF.Copy), reads=(bkey[vb],), writes=(f"V{tt}",))

                done_w(pl["win"][2])
                chk(2)
                wq, wqk_ = use_w(pl["win"][0])
                wk, wkk_ = use_w(pl["win"][1], paired=True)
                tcv = tcs_t[:, 128:256].rearrange("p (g d) -> p g d", d=16)
                tsv = tcs_t[:, 256:384].rearrange("p (g d) -> p g d", d=16)
                Fnames = ("gg", "xc")
                Tb = (4, 5)

                def emit_transposes(tt):
                    tb, off = tt // 4, (tt % 4) * 128
                    for hb in range(2):
                        Ft = Rt[Fnames[hb]]
                        pb = Tb[hb]
                        for g in range(4):
                            S.op("pe", lambda e, g=g, Ft=Ft, pb=pb: e.transpose(bank[pb][:, g * 128:(g + 1) * 128], Ft[:, g * 128:(g + 1) * 128], idf[:]),
                                 reads=(Fnames[hb], "idf"), writes=(bkey[pb],))
                        src = bank[pb].rearrange("p (h t) -> p h t", h=4)
                        if hb == 0:
                            S.op("act", lambda e, tb=tb, off=off, src=src: e.activation(QK[:, tb:16:4, off:off + 128], src, AF.Copy),
                                 reads=(bkey[pb],), writes=[f"qk{qk_idx(h, tb)}" for h in range(4)])
                        else:
                            S.op("dve", lambda e, tb=tb, off=off, src=src: e.tensor_copy(QK[:, 16 + tb:32:4, off:off + 128], src),
                                 reads=(bkey[pb],), writes=[f"qk{16 + qk_idx(h, tb)}" for h in range(4)])

                for tt in range(NTT):
                    tb = tt // 4
                    tsl = slice(tt * 128, (tt + 1) * 128)
                    A = psA[tt % 2]
                    Ak = (bkey[0], bkey[1]) if tt % 2 == 0 else (bkey[2], bkey[3])

                    def proj(e, A=A, tsl=tsl):
                        ins = None
                        for c in range(8):
                            e.matmul(A[:, 0:512], Bt[:, c, tsl], wq[:, c, :], start=(c == 0), stop=(c == 7))
                            ins = e.matmul(A[:, 512:1024], Bt[:, c, tsl], wk[:, c, :], start=(c == 0), stop=(c == 7))
                        return ins
                    S.op("pe", proj, reads=[f"B{c}_{tb}" for c in range(8)] + [wqk_, wkk_], writes=(Ak[0], Ak[1]))
                    if tt >= 1 and not os.environ.get('SKIP_TR'):
                        emit_transposes(tt - 1)
                    cos2B = cos2T[:, tt, :].unsqueeze(1).to_broadcast([128, 8, 16])
                    sinB8 = sinT[:, tt, :].unsqueeze(1).to_broadcast([128, 8, 8])
                    for hb in range(2):
                        Ah = A[:, hb * 512:(hb + 1) * 512]
                        Fn = Fnames[hb]
                        Ft = Rt[Fn]
                        if hb == 0:
                            S.op("act", lambda e, Ft=Ft, Ah=Ah: e.activation(Ft[:], Ah, AF.Copy), reads=(Ak[hb],), writes=(Fn,))
                        else:
                            S.op("dve", lambda e, Ft=Ft, Ah=Ah: e.tensor_copy(Ft[:], Ah), reads=(Ak[hb],), writes=(Fn,))
                        if os.environ.get('SKIP_ROPE'):
                            continue
                        F3 = Ft[:].rearrange("p (g d) -> p g d", d=64)
                        S.op("dve", lambda e, F3=F3, cos2B=cos2B: e.tensor_tensor(tcv, F3[:, :, 0:16], cos2B, ALU.mult),
                             reads=(Fn, "cos2T"), writes=("tcv",))
                        S.op("dve", lambda e, F3=F3, sinB8=sinB8: e.scalar_tensor_tensor(tsv[:, :, 0:8], F3[:, :, 8:16], -1.0, sinB8, ALU.mult, ALU.mult),
                             reads=(Fn, "sinT"), writes=("tsv",))
                        S.op("dve", lambda e, F3=F3, sinB8=sinB8: e.tensor_tensor(tsv[:, :, 8:16], F3[:, :, 0:8], sinB8, ALU.mult),
                             reads=(Fn, "sinT", "tsv"), writes=("tsv",))
                        S.op("dve", lambda e, F3=F3: e.tensor_tensor(F3[:, :, 0:16], tcv, tsv, ALU.add),
                             reads=("tcv", "tsv", Fn), writes=(Fn,))
                if not os.environ.get('SKIP_TR'):
                    emit_transposes(NTT - 1)

                done_w(pl["win"][0])
                done_w(pl["win"][1])
                chk(3)
                wxr, wxrk = use_w(pl["win"][3])
                wgr, wgrk = use_w(pl["win"][4], paired=True)

                def rnn_proj1(u):
                    c, tb = u // 4, u % 4
                    def f(e, c=c, tb=tb):
                        ins = None
                        for k in range(8):
                            ins = e.matmul(bank[6], wxr[:, k, c * 128:(c + 1) * 128], Bt[:, k, tbs(tb)], start=(k == 0), stop=(k == 7))
                        return ins
                    S.op("pe", f, reads=[f"B{k}_{tb}" for k in range(8)] + [wxrk], writes=(bkey[6],))
                    if tb == 0:
                        S.op("dve", lambda e: e.memset(xr_ext[:, 0:4], 0.0), reads=("xr_ext",), writes=("xr_ext",))
                    else:
                        S.op("dve", lambda e: e.tensor_copy(xr_ext[:, 0:4], xr_ext[:, 512:516]), reads=("xr_ext",), writes=("xr_ext",))
                    S.op("act", lambda e: e.activation(xr_ext[:, 4:516], bank[6], AF.Copy), reads=(bkey[6],), writes=("xr_ext",))

                def rnn_proj2(u):
                    c, tb = u // 4, u % 4
                    def f2(e, c=c, tb=tb):
                        ins = None
                        for k in range(8):
                            ins = e.matmul(bank[7], wgr[:, k, c * 128:(c + 1) * 128], Bt[:, k, tbs(tb)], start=(k == 0), stop=(k == 7))
                        return ins
                    S.op("pe", f2, reads=[f"B{k}_{tb}" for k in range(8)] + [wgrk], writes=(bkey[7],))
                    S.op("act", lambda e: e.activation(Rt["gg"][:], bank[7], AF.Gelu_apprx_tanh), reads=(bkey[7],), writes=("gg",))
                    cw = lambda k, c=c: sm[:, l, CW + c * 4 + k:CW + c * 4 + k + 1]
                    S.op("dve", lambda e, c=c: e.tensor_scalar(Rt["xc"][:], xr_ext[:, 4:516], cw(3), sm[:, l, CB + c:CB + c + 1], ALU.mult, ALU.add),
                         reads=("xr_ext", "sm"), writes=("xc",))
                    for j in (1, 2, 3):
                        S.op("dve", lambda e, j=j: e.scalar_tensor_tensor(Rt["xc"][:], xr_ext[:, 4 - j:516 - j], cw(3 - j), Rt["xc"][:], ALU.mult, ALU.add),
                             reads=("xr_ext", "xc", "sm"), writes=("xc",))
                    S.op("act", lambda e: e.activation(xcb[:], Rt["xc"][:], AF.Copy), reads=("xc",), writes=("xcb",))

                def rnn_gates1(u):
                    c, tb = u // 4, u % 4
                    S.op("pe", lambda e, c=c: e.matmul(bank[6], gW[:, 0, c, :], xcb[:], start=True, stop=True), reads=("xcb", "gW"), writes=(bkey[6],))
                    S.op("act", lambda e, c=c: e.activation(Rt["r"][:], bank[6], AF.Sigmoid, bias=sm[:, l, BA + c:BA + c + 1], scale=1.0),
                         reads=(bkey[6], "sm"), writes=("r",))

                def rnn_gates2(u):
                    c, tb = u // 4, u % 4
                    S.op("pe", lambda e, c=c: e.matmul(bank[7], gW[:, 1, c, :], xcb[:], start=True, stop=True), reads=("xcb", "gW"), writes=(bkey[7],))
                    S.op("act", lambda e, c=c: e.activation(Rt["i"][:], bank[7], AF.Sigmoid, bias=sm[:, l, BX + c:BX + c + 1], scale=1.0),
                         reads=(bkey[7], "sm"), writes=("i",))
                    S.op("dve", lambda e: e.tensor_tensor(Rt["i"][:], Rt["i"][:], Rt["xc"][:], ALU.mult), reads=("i", "xc"), writes=("i",))
                    S.op("act", lambda e, c=c: e.activation(Rt["xc"][:], Rt["r"][:], AF.Exp, scale=der[:, 9 + c:10 + c]),
                         reads=("r", "c2"), writes=("xc",))
                    S.op("act", lambda e, c=c: e.activation(Rt["r"][:], Rt["r"][:], AF.Exp, scale=der[:, 5 + c:6 + c]),
                         reads=("r", "c1"), writes=("r",))
                    S.op("act", lambda e: e.activation(Rt["xc"][:], Rt["xc"][:], AF.Sqrt, bias=1.0, scale=-1.0), reads=("xc",), writes=("xc",))
                    S.op("dve", lambda e: e.tensor_tensor(Rt["i"][:], Rt["i"][:], Rt["xc"][:], ALU.mult), reads=("i", "xc"), writes=("i",))
                    if tb == 0:
                        S.op("dve", lambda e: e.memset(der[:, 13:14], 0.0), reads=("carry",), writes=("carry",))
                    else:
                        S.op("dve", lambda e: e.tensor_copy(der[:, 13:14], Rt["hs"][:, 511:512]), reads=("hs",), writes=("carry",))
                    S.op("dve", lambda e: e.tensor_tensor_scan(Rt["hs"][:], Rt["r"][:], Rt["i"][:], der[:, 13:14], ALU.mult, ALU.add),
                         reads=("r", "i", "carry"), writes=("hs",))
                    S.op("dve", lambda e, c=c, tb=tb: e.tensor_tensor(Yt[:, c, tbs(tb)], Rt["hs"][:], Rt["gg"][:], ALU.mult),
                         reads=("hs", "gg"), writes=(f"Y{c}_{tb}",))

                pend = {"ss": None}

                def att_block(h, j, u_proj, u_gate):
                    nkt = 4 * j + 4
                    qi = qk_idx(h, j)
                    S0b, S1b, O1b, O2b, R1b, R2b = 0, 1, 2, 3, 4, 5

                    def geo(kt):
                        m = kt - 4 * j
                        q0 = m * 128 if m > 0 else 0
                        tbk, off = kt // 4, (kt % 4) * 128
                        return m, q0, 16 + qk_idx(h, tbk), off

                    def emit_qk_exp(kt):
                        m, q0, ki, off = geo(kt)
                        pbi = kt % 2

                        def qk(e, q0=q0, ki=ki, off=off, qi=qi):
                            e.matmul(bank[S0b][:, q0:512], QK[0:64, ki, off:off + 128], QK[0:64, qi, q0:512], start=True, stop=True, tile_position=(0, 0))
                            return e.matmul(bank[S1b][:, q0:512], QK[64:128, ki, off:off + 128], QK[64:128, qi, q0:512], start=True, stop=True, tile_position=(64, 0))
                        S.op("pe", qk, reads=(f"qk{ki}", f"qk{qi}"), writes=(bkey[S0b], bkey[S1b]))
                        for cc in range(2):
                            P = Pb[pbi][cc]
                            pk = f"Pb{pbi}{cc}"
                            S.op("act", lambda e, P=P, cc=cc, q0=q0: e.activation(P[:, q0:512], bank[cc][:, q0:512], AF.Exp, scale=SCALE),
                                 reads=(bkey[cc],), writes=(pk,))
                            if m >= 0:
                                S.op("dve", lambda e, P=P, q0=q0: e.tensor_tensor(P[:, q0:q0 + 128], P[:, q0:q0 + 128], cmask, ALU.mult),
                                     reads=(pk, "cb"), writes=(pk,))

                    def emit_pv(kt):
                        m, q0, ki, off = geo(kt)
                        pbi = kt % 2

                        def pv(e, q0=q0, kt=kt, pbi=pbi, h=h, first=(kt == 0), last=(kt == nkt - 1)):
                            vv = V[:, kt, h * 128:(h + 1) * 128]
                            e.matmul(bank[O1b][:, q0:512], vv, Pb[pbi][0][:, q0:512], start=first, stop=last)
                            e.matmul(bank[R1b][:, q0:512], ones, Pb[pbi][0][:, q0:512], start=first, stop=last)
                            e.matmul(bank[O2b][:, q0:512], vv, Pb[pbi][1][:, q0:512], start=first, stop=last)
                            return e.matmul(bank[R2b][:, q0:512], ones, Pb[pbi][1][:, q0:512], start=first, stop=last)
                        S.op("pe", pv, reads=(f"V{kt}", f"Pb{pbi}0", f"Pb{pbi}1", "cb"),
                             writes=(bkey[O1b], bkey[O2b], bkey[R1b], bkey[R2b]))

                    emit_qk_exp(0)
                    for kt in range(nkt):
                        if kt + 1 < nkt:
                            emit_qk_exp(kt + 1)
                        emit_pv(kt)
                        if kt == 0:
                            if pend["ss"] is not None:
                                pend["ss"]()
                                pend["ss"] = None
                            if u_gate is not None:
                                rnn_gates1(u_gate)
                        if kt == 1 and u_gate is not None:
                            rnn_gates2(u_gate)
                        if kt == nkt - 2 and u_proj is not None:
                            rnn_proj1(u_proj)
                    S.op("dve", lambda e: e.reciprocal(E0[:], bank[R1b]), reads=(bkey[R1b],), writes=("E0",))
                    S.op("dve", lambda e: e.reciprocal(E1[:], bank[R2b]), reads=(bkey[R2b],), writes=("E1",))
                    S.op("dve", lambda e: e.tensor_tensor(E0[:], bank[O1b], E0[:], ALU.mult), reads=(bkey[O1b], "E0"), writes=("E0",))
                    S.op("dve", lambda e: e.tensor_tensor(E1[:], bank[O2b], E1[:], ALU.mult), reads=(bkey[O2b], "E1"), writes=("E1",))
                    S.op("dve", lambda e: e.scalar_tensor_tensor(E0[:], E1[:], neglam, E0[:], ALU.mult, ALU.add),
                         reads=("E0", "E1", "neglam"), writes=("E0",))
                    S.op("act", lambda e: e.activation(E2[:], E0[:], AF.Square), reads=("E0",), writes=("E2",))

                    def part_b(qi=qi):
                        S.op("pe", lambda e: e.matmul(bank[7], ones, E2[:], start=True, stop=True), reads=("E2", "cb"), writes=(bkey[7],))
                        S.op("act", lambda e: e.activation(E1[:], bank[7], AF.Sqrt, bias=eps_c, scale=1.0 / 128), reads=(bkey[7], "cstc0"), writes=("E1",))
                        S.op("dve", lambda e: e.reciprocal(E1[:], E1[:]), reads=("E1",), writes=("E1",))
                        S.op("dve", lambda e, qi=qi: e.scalar_tensor_tensor(QK[:, qi, :], E0[:], gsub, E1[:], ALU.mult, ALU.mult),
                             reads=("E0", "E1", "gsub"), writes=(f"qk{qi}",))
                    pend["ss"] = part_b
                    if u_proj is not None:
                        rnn_proj2(u_proj)

                blocks = [(h, j) for h in range(4) for j in range(NTB)]
                rnn_proj1(0)
                rnn_proj2(0)
                for bi, (h, j) in enumerate(blocks):
                    att_block(h, j, bi + 1 if bi + 1 < 16 else None, bi)
                pend["ss"]()
                pend["ss"] = None

                done_w(pl["win"][3])
                done_w(pl["win"][4])
                chk(4)
                wo0_, wok0 = use_w(pl["wo"][0])
                wo1_, wok1 = use_w(pl["wo"][1], paired=True)
                for tb in range(NTB):
                    for g2 in range(2):
                        wo_, wok = (wo0_, wok0) if g2 == 0 else (wo1_, wok1)
                        for d4 in range(4):
                            dtc = g2 * 4 + d4
                            bk = nextbank([0, 1, 2, 3, 4, 5, 6, 7])
                            def f(e, d4=d4, tb=tb, bk=bk, wo_=wo_):
                                ins = None
                                for c in range(4):
                                    e.matmul(bank[bk], wo_[:, c, d4 * 128:(d4 + 1) * 128], QK[:, c * 4 + tb, :], start=(c == 0), stop=False)
                                for c in range(4):
                                    ins = e.matmul(bank[bk], wo_[:, 4 + c, d4 * 128:(d4 + 1) * 128], Yt[:, c, tbs(tb)], start=False, stop=(c == 3))
                                return ins
                            S.op("pe", f, reads=[f"qk{c * 4 + tb}" for c in range(4)] + [f"Y{c}_{tb}" for c in range(4)] + [wok], writes=(bkey[bk],))
                            S.op("dve", lambda e, dtc=dtc, tb=tb, bk=bk: e.tensor_tensor(hT[:, dtc, tbs(tb)], bank[bk], hT[:, dtc, tbs(tb)], ALU.add),
                                 reads=(bkey[bk], f"h{dtc}_{tb}"), writes=(f"h{dtc}_{tb}",))
                    if tb >= 1:
                        rmsnorm_tb(lambda c, l=l: sm[:, l, GL + c:GL + c + 1], tb - 1)
                rmsnorm_tb(lambda c, l=l: sm[:, l, GL + c:GL + c + 1], NTB - 1)
                done_w(pl["wo"][0])
                done_w(pl["wo"][1])

                chk(5)
                pTb = V[:, 0:8, :].rearrange("p a b -> p (a b)").rearrange("p (k t) -> p k t", k=2)
                S.dma("pool", "dpt", [(lambda e, l=l, s=s, k=k: e.dma_start(out=pTb[:, k, :], in_=pT[l, s, k * 128:(k + 1) * 128, :])) for k in range(2)],
                      writes=[f"V{t}" for t in range(8)])
                sqt = [Rt["gg"], Rt["xc"]]
                sqk = ["gg", "xc"]
                cnt = 0
                for g in range(4):
                    for a in range(2):
                        w1_, w1k = use_w(pl["w1"][g][a])
                        for f4 in range(4):
                            fc = a * 4 + f4
                            for tb in range(NTB):
                                bk = nextbank([0, 1, 2, 3, 4, 5, 6, 7])
                                hk = f"qk{fc * 4 + tb}"
                                def f(e, f4=f4, tb=tb, bk=bk, w1_=w1_):
                                    ins = None
                                    for c in range(8):
                                        ins = e.matmul(bank[bk], w1_[:, c, f4 * 128:(f4 + 1) * 128], Bt[:, c, tbs(tb)], start=(c == 0), stop=(c == 7))
                                    return ins
                                S.op("pe", f, reads=[f"B{c}_{tb}" for c in range(8)] + [w1k], writes=(bkey[bk],))
                                ti = cnt % 2
                                cnt += 1
                                S.op("act", lambda e, bk=bk, ti=ti: e.activation(sqt[ti][:], bank[bk], AF.Square), reads=(bkey[bk],), writes=(sqk[ti],))
                                S.op("dve", lambda e, bk=bk, ti=ti, fc=fc, tb=tb: e.scalar_tensor_tensor(
                                    QK[:, fc * 4 + tb, :], bank[bk], 0.0, sqt[ti][:], ALU.is_gt, ALU.mult),
                                    reads=(bkey[bk], sqk[ti]), writes=(hk,))
                    def mlp_out_tile(w2_, w2k, dh, d4, tb):
                        dtc = dh * 4 + d4
                        bk = nextbank([0, 1, 2, 3, 4, 5, 6, 7])
                        def f(e, d4=d4, tb=tb, bk=bk, w2_=w2_):
                            ins = None
                            for fc in range(8):
                                ins = e.matmul(bank[bk], w2_[:, fc, d4 * 128:(d4 + 1) * 128], QK[:, fc * 4 + tb, :], start=(fc == 0), stop=(fc == 7))
                            return ins
                        S.op("pe", f, reads=[f"qk{fc * 4 + tb}" for fc in range(8)] + [w2k], writes=(bkey[bk],))
                        S.op("dve", lambda e, dtc=dtc, tb=tb, bk=bk: e.tensor_tensor(hT[:, dtc, tbs(tb)], bank[bk], hT[:, dtc, tbs(tb)], ALU.add),
                             reads=(bkey[bk], f"h{dtc}_{tb}"), writes=(f"h{dtc}_{tb}",))
                    if g < 3:
                        for dh in range(2):
                            w2_, w2k = use_w(pl["w2"][g][dh])
                            for d4 in range(4):
                                for tb in range(NTB):
                                    mlp_out_tile(w2_, w2k, dh, d4, tb)
                    else:
                        w2a = use_w(pl["w2"][g][0])
                        w2b = use_w(pl["w2"][g][1], paired=True)
                        for tb in range(NTB):
                            for dh in range(2):
                                w2_, w2k = w2a if dh == 0 else w2b
                                for d4 in range(4):
                                    mlp_out_tile(w2_, w2k, dh, d4, tb)
                            if tb >= 1:
                                rmsnorm_tb(lambda c, l=l: sm[:, l, GP + c:GP + c + 1], tb - 1)
                        rmsnorm_tb(lambda c, l=l: sm[:, l, GP + c:GP + c + 1], NTB - 1)
                        done_w(pl["w2"][g][0])
                        done_w(pl["w2"][g][1])

                chk(6)
                sgt = [Rt["r"], Rt["i"]]
                sgk = ["r", "i"]
                t2t = [Rt["hs"], Rt["gg"]]
                t2k = ["hs", "gg"]
                cnt = 0
                wp_, wpk = None, None
                for dh in range(2):
                    if dh == 0:
                        wg_, wgk_ = use_w(pl["wg0"])
                        wp_, wpk = use_w(pl["wp"], paired=True)
                    else:
                        done_w(pl["wg0"])
                        wg_, wgk_ = use_w(pl["wg1"], paired=True)
                    for d4 in range(4):
                        dtc = dh * 4 + d4
                        for tb in range(NTB):
                            bg = nextbank([0, 1, 2, 3, 4, 5, 6, 7])
                            bp = nextbank([0, 1, 2, 3, 4, 5, 6, 7])
                            def f(e, d4=d4, tb=tb, bg=bg, wg_=wg_):
                                ins = None
                                for c in range(8):
                                    ins = e.matmul(bank[bg], wg_[:, c, d4 * 128:(d4 + 1) * 128], Bt[:, c, tbs(tb)], start=(c == 0), stop=(c == 7))
                                return ins
                            S.op("pe", f, reads=[f"B{c}_{tb}" for c in range(8)] + [wgk_], writes=(bkey[bg],))
                            def f2(e, d4=d4, tb=tb, bp=bp, dh=dh, wp_=wp_):
                                ins = None
                                for k in range(2):
                                    ins = e.matmul(bank[bp], wp_[:, dh * 2 + k, d4 * 128:(d4 + 1) * 128], pTb[:, k, tbs(tb)], start=(k == 0), stop=(k == 1))
                                return ins
                            S.op("pe", f2, reads=[f"V{t}" for t in range(8)] + [wpk], writes=(bkey[bp],))
                            ti = cnt % 2
                            cnt += 1
                            S.op("act", lambda e, bg=bg, ti=ti: e.activation(sgt[ti][:], bank[bg], AF.Sigmoid), reads=(bkey[bg],), writes=(sgk[ti],))
                            S.op("dve", lambda e, bp=bp, ti=ti: e.tensor_tensor(t2t[ti][:], bank[bp], sgt[ti][:], ALU.mult),
                                 reads=(bkey[bp], sgk[ti]), writes=(t2k[ti],))
                            S.op("dve", lambda e, dtc=dtc, tb=tb, ti=ti: e.tensor_tensor(hT[:, dtc, tbs(tb)], t2t[ti][:], hT[:, dtc, tbs(tb)], ALU.add),
                                 reads=(t2k[ti], f"h{dtc}_{tb}"), writes=(f"h{dtc}_{tb}",))

            for l in range(n_layers):
                try:
                    emit_layer(s, l)
                except _Stop:
                    pass
            for tb in range(NTB):
                bk = nextbank([4, 5, 6, 7])
                for c in range(8):
                    sq = Pb[0][c % 2]
                    sqk_ = f"Pb0{c % 2}"
                    S.op("act", lambda e, sq=sq, c=c, tb=tb: e.activation(sq[:], hT[:, c, tbs(tb)], AF.Square),
                         reads=(f"h{c}_{tb}",), writes=(sqk_,))
                    S.op("pe", lambda e, sq=sq, c=c, bk=bk: e.matmul(bank[bk], ones, sq[:], start=(c == 0), stop=(c == 7)),
                         reads=(sqk_, "cb"), writes=(bkey[bk],))
                Er = (E0, E1)[tb % 2]
                Ek = ("E0", "E1")[tb % 2]
                S.op("act", lambda e, Er=Er, bk=bk: e.activation(Er[:], bank[bk], AF.Sqrt, bias=eps_c, scale=1.0 / D),
                     reads=(bkey[bk], "cstc0"), writes=(Ek,))
                S.op("dve", lambda e, Er=Er: e.reciprocal(Er[:], Er[:]), reads=(Ek,), writes=(Ek,))
                for c in range(8):
                    names = ("gg", "xc", "r", "i", "hs")
                    nm = names[c % 5]
                    ot = Rt[nm]
                    S.op("dve", lambda e, Er=Er, c=c, tb=tb, ot=ot: e.scalar_tensor_tensor(
                        ot[:], hT[:, c, tbs(tb)], gf[:, c:c + 1], Er[:], ALU.mult, ALU.mult),
                        reads=(f"h{c}_{tb}", Ek, "gf"), writes=(nm,))
                    S.dma("sp", "dout_" + nm, [lambda e, ot=ot, c=c, tb=tb, s=s: e.dma_start(out=outT[s, c * 128:(c + 1) * 128, tbs(tb)], in_=ot[:])],
                          reads=(nm,))

        sem_names = list(S.ENG) + sorted(S.dcnt.keys())
        sems = {n: es.enter_context(nc.semaphore("s_" + n)) for n in sem_names}
        final_waits = [(n, S.cnt[n]) for n in S.ENG if n != "sp" and S.cnt[n] > 0] + [(n, v) for n, v in S.dcnt.items()]
        block = es.enter_context(nc.Block())

        def replay(name, e):
            for waits, fn, (sn, inc) in S.q[name]:
                for (ws, wv) in waits:
                    e.wait_ge(sems[ws], wv)
                ins = fn(e)
                ins.then_inc(sems[sn], inc)
            if name == "sp":
                for (ws, wv) in final_waits:
                    e.wait_ge(sems[ws], wv)

        @block.tensor
        def _(e):
            replay("pe", e)

        @block.scalar
        def _(e):
            replay("act", e)

        @block.vector
        def _(e):
            replay("dve", e)

        @block.gpsimd
        def _(e):
            replay("pool", e)

        @block.sync
        def _(e):
            replay("sp", e)
    return nc, S


def _host_consts():
    ident = np.eye(128, dtype=np.float32)
    ones = np.ones((128, 128), np.float32)
    k = np.arange(128)[:, None]
    q = np.arange(128)[None, :]
    mask = (q >= k).astype(np.float32)
    cst = np.concatenate([ident, ones, mask], axis=1)
    half = 8
    inv_freq = (np.float32(500000.0) ** (-np.arange(half, dtype=np.float32) * np.float32(2.0) / np.float32(16))).astype(np.float32)
    invf = np.broadcast_to(inv_freq[None, :], (128, 8)).copy()
    return cst, invf


def _layout_inputs(inp, n_layers=L):
    f = lambda a: np.ascontiguousarray(np.asarray(a, dtype=np.float32))
    col8 = lambda v: v.reshape(8, 128).T
    col4 = lambda v: v.reshape(4, 128).T
    smalls = np.zeros((128, L, NSM), np.float32)
    gatew = np.zeros((L, 128, 2, 4, 128), np.float32)
    lqk = np.zeros((L, 128, 256), np.float32)
    for l in range(L):
        smalls[:, l, GM:GM + 8] = col8(f(inp["g_mix"][l]))
        smalls[:, l, GL:GL + 8] = col8(f(inp["g_mlp"][l]))
        smalls[:, l, GP:GP + 8] = col8(f(inp["g_ple"][l]))
        cw = f(inp["conv_w"][l])
        for c in range(4):
            for k in range(4):
                smalls[:, l, CW + c * 4 + k] = cw[k, c * 128:(c + 1) * 128]
        smalls[:, l, CB:CB + 4] = col4(f(inp["conv_b"][l]))
        smalls[:, l, BA:BA + 4] = col4(f(inp["b_gate_a"][l]))
        smalls[:, l, BX:BX + 4] = col4(f(inp["b_gate_x"][l]))
        smalls[:, l, LL:LL + 4] = col4(f(inp["lru_lambda"][l]))
        smalls[:, l, GS] = f(inp["g_subln"][l])
        for gi, nm in enumerate(("w_gate_a", "w_gate_x")):
            w = f(inp[nm][l])
            for c in range(4):
                for b in range(2):
                    gatew[l, b * 64:(b + 1) * 64, gi, c, b * 64:(b + 1) * 64] = w[2 * c + b]
        lqk[l, :, 0:128] = f(inp["lam_q"][l]).reshape(1, 128)
        lqk[l, :, 128:256] = f(inp["lam_k"][l]).reshape(1, 128)
    gatew = gatew.reshape(L, 128, 1024)
    gfin = np.ascontiguousarray(col8(f(inp["g_final"])))
    cst, invf = _host_consts()
    x = f(inp["x"])
    p = f(inp["p"])
    posn = np.asarray(inp["positions"]).astype(np.int32)
    shared = dict(w_in=f(inp["w_in"]), w_out=f(inp["w_out"]), w_mlp_in=f(inp["w_mlp_in"]), w_mlp_out=f(inp["w_mlp_out"]),
                  w_ple_gate=f(inp["w_ple_gate"]), w_ple_proj=f(inp["w_ple_proj"]), smalls=smalls, gatew=gatew, lqk=lqk,
                  gfin=gfin, cst=cst, invf=invf)
    maps = []
    for i in range(NC):
        xs = x[2 * i:2 * i + 2]
        m = dict(shared)
        m["xT"] = np.ascontiguousarray(xs.transpose(0, 2, 1))
        m["pT"] = np.ascontiguousarray(p[:, 2 * i:2 * i + 2].transpose(0, 1, 3, 2))
        m["pos"] = np.ascontiguousarray(posn[2 * i:2 * i + 2].reshape(2, NTT, 128).transpose(0, 2, 1))
        maps.append(m)
    return maps


_CACHE = {}


def kernel(**inputs):
    maps = _layout_inputs(inputs)
    if "nc" not in _CACHE:
        _CACHE["nc"] = build_program()[0]
    nc = _CACHE["nc"]
    res = run_bass_kernel_spmd(nc, maps, core_ids=list(range(NC)))
    out = np.empty((16, T, D), np.float32)
    for i in range(NC):
        o = res.results[i]["outT"]
        out[2 * i:2 * i + 2] = o.transpose(0, 2, 1)
    return out
```

```python
import math
import os
from contextlib import ExitStack

import numpy as np
import concourse.bass as bass
import concourse.mybir as mybir
from concourse.bass_utils import run_bass_kernel_spmd

F32 = mybir.dt.float32
BF16 = mybir.dt.bfloat16
I32 = mybir.dt.int32
AF = mybir.ActivationFunctionType
ALU = mybir.AluOpType
AX = mybir.AxisListType

D = 1024
T = 2048
L = 4
NC = 8
TB = 512
NTB = 4
NTT = 16
EPS = 1e-6
SCALE = 0.125
GM, GL, GP, CW, CB, BA, BX, LL, GS, NSM = 0, 8, 16, 24, 40, 44, 48, 52, 56, 64
PI = math.pi


class Sched:
    ENG = ("pe", "act", "dve", "pool", "sp")

    def __init__(self):
        self.q = {e: [] for e in self.ENG}
        self.cnt = {e: 0 for e in self.ENG}
        self.waited = {e: {} for e in self.ENG}
        self.lastw = {}
        self.readers = {}
        self.dcnt = {}

    def _waits(self, eng, reads, writes):
        need = {}
        for k in reads:
            ev = self.lastw.get(k)
            if ev is not None:
                need[ev[0]] = max(need.get(ev[0], 0), ev[1])
        for k in writes:
            ev = self.lastw.get(k)
            if ev is not None:
                need[ev[0]] = max(need.get(ev[0], 0), ev[1])
            for ev in self.readers.get(k, ()):
                need[ev[0]] = max(need.get(ev[0], 0), ev[1])
        waits = []
        for s, v in need.items():
            if eng == "pe" and s == "pe":
                continue
            if self.waited[eng].get(s, 0) < v:
                self.waited[eng][s] = v
                waits.append((s, v))
        return waits

    def _commit(self, ev, reads, writes):
        for k in reads:
            self.readers.setdefault(k, []).append(ev)
        for k in writes:
            self.lastw[k] = ev
            self.readers[k] = []

    def op(self, eng, fn, reads=(), writes=()):
        waits = self._waits(eng, reads, writes)
        self.cnt[eng] += 1
        ev = (eng, self.cnt[eng])
        self.q[eng].append((waits, fn, (eng, 1)))
        self._commit(ev, reads, writes)
        return ev

    def dma(self, queue, dsem, fns, reads=(), writes=()):
        waits = self._waits(queue, reads, writes)
        first = True
        for fn in fns:
            self.dcnt[dsem] = self.dcnt.get(dsem, 0) + 16
            self.q[queue].append((waits if first else [], fn, (dsem, 16)))
            first = False
        ev = (dsem, self.dcnt[dsem])
        self._commit(ev, reads, writes)
        return ev


class _Stop(Exception):
    pass


def build_program(n_layers=L, n_seq=2, lim=None):
    nc = bass.Bass("TRN2", target_bir_lowering=False)
    dt_in = lambda name, shape, dt=F32: nc.dram_tensor(name, shape, dt, kind="ExternalInput").ap()
    xT = dt_in("xT", [2, D, T])
    pT = dt_in("pT", [L, 2, 256, T])
    pos = dt_in("pos", [2, 128, NTT], I32)
    w_in = dt_in("w_in", [L, D, 2560])
    w_out = dt_in("w_out", [L, D, D])
    w1 = dt_in("w_mlp_in", [L, D, 4096])
    w2 = dt_in("w_mlp_out", [L, 4096, D])
    wg = dt_in("w_ple_gate", [L, D, D])
    wp = dt_in("w_ple_proj", [L, 256, D])
    smalls = dt_in("smalls", [128, L, NSM])
    gatew = dt_in("gatew", [L, 128, 1024])
    lqk = dt_in("lqk", [L, 128, 256])
    gfin = dt_in("gfin", [128, 8])
    cst = dt_in("cst", [128, 3 * 128])
    invf = dt_in("invf", [128, 8])
    outT = nc.dram_tensor("outT", [2, D, T], F32, kind="ExternalOutput").ap()

    S = Sched()
    with ExitStack() as es:
        sb = lambda name, shape, dt: es.enter_context(nc.sbuf_tensor(name, shape, dt))
        hT = sb("hT", [128, 8, T], F32)
        Bt = sb("Bt", [128, 8, T], BF16)
        QK = sb("QK", [128, 32, TB], BF16)
        V = sb("V", [128, NTT, 512], BF16)
        ring = sb("ring", [128, 2, 8, 512], BF16)
        Yt = sb("Yt", [128, 4, T], BF16)
        sm = sb("sm", [128, L, NSM], F32)
        gW = sb("gW", [128, 2, 4, 128], BF16)
        gf = sb("gf", [128, 8], F32)
        cb = sb("cb", [128, 3, 128], BF16)
        ivf = sb("ivf", [128, 8], F32)
        idf = sb("idf", [128, 128], F32)
        posi = sb("posi", [128, NTT], I32)
        posf = sb("posf", [128, NTT], F32)
        ang = sb("ang", [128, NTT, 8], F32)
        cosT = sb("cosT", [128, NTT, 8], F32)
        sinT = sb("sinT", [128, NTT, 8], F32)
        der = sb("der", [128, 16], F32)
        kint = sb("kint", [128, 128], I32)
        cst_c = sb("cst_c", [128, 4], F32)
        Pb = [[sb(f"Pb{i}{c}", [128, 512], BF16) for c in range(2)] for i in range(2)]
        E0 = sb("E0", [128, 512], F32)
        E1 = sb("E1", [128, 512], F32)
        E2 = sb("E2", [128, 512], BF16)
        xr_ext = sb("xr_ext", [128, 516], F32)
        Rt = {n: sb("R_" + n, [128, 512], F32) for n in ("gg", "xc", "r", "i", "hs")}
        xcb = sb("xcb", [128, 512], BF16)
        tcs_t = sb("tcs_t", [128, 512], F32)
        cos2T = sb("cos2T", [128, NTT, 16], F32)
        n2pi = tcs_t[:, 0:128]
        rtmp = tcs_t[:, 128:256]
        psA = [es.enter_context(nc.psum_tensor(f"psA{i}", [128, 1024], F32)) for i in range(2)]
        psB = [es.enter_context(nc.psum_tensor(f"psB{i}", [128, 512], F32)) for i in range(4)]
        bank = [psA[0][:, 0:512], psA[0][:, 512:1024], psA[1][:, 0:512], psA[1][:, 512:1024],
                psB[0][:], psB[1][:], psB[2][:], psB[3][:]]
        bkey = [f"ps{i}" for i in range(8)]

        ident = cb[:, 0, :]
        ones = cb[:, 1, :]
        cmask = cb[:, 2, :]
        eps_c = cst_c[:, 0:1]
        npi_c = cst_c[:, 1:2]

        loads = []
        st = {"next_load": 0, "n_emitted": 0, "rr": 0, "done": -1, "pending_done": []}

        def ring_key(n):
            return f"ring{n % 2}"

        def emit_load(n):
            fns = loads[n]
            S.dma("pool", f"dring{n % 2}", fns, reads=(), writes=(ring_key(n),))

        def _emit_upto(m):
            m = min(m, len(loads) - 1)
            while st["n_emitted"] <= m:
                emit_load(st["n_emitted"])
                st["n_emitted"] += 1

        def use_w(n, paired=False):
            if not paired:
                st["done"] = max(st["done"], n - 1)
            _emit_upto(max(n, st["done"] + 2))
            return ring[:, n % 2], ring_key(n)

        def done_w(n):
            st["done"] = max(st["done"], n)
            _emit_upto(st["done"] + 2)

        def mk_load(src_ap_fn, n_index_holder):
            pass

        def add_load(src):
            n = len(loads)
            slot = n % 2
            loads.append([lambda e, src=src, slot=slot: e.dma_start(out=ring[:, slot], in_=src)])
            return n

        def add_load_multi(pairs):
            n = len(loads)
            slot = n % 2
            fns = []
            for (osl, src) in pairs:
                fns.append(lambda e, src=src, osl=osl, slot=slot: e.dma_start(out=ring[:, slot, osl[0]:osl[1]], in_=src))
            loads.append(fns)
            return n

        def kpc(ap2d):
            return ap2d.rearrange("(k p) c -> p k c", p=128)

        plan = {}
        for s in range(n_seq):
            for l in range(n_layers):
                d = {}
                d["win"] = {g: add_load(kpc(w_in[l][:, g * 512:(g + 1) * 512])) for g in (2, 0, 1, 3, 4)}
                d["wo"] = [add_load(kpc(w_out[l][:, g * 512:(g + 1) * 512])) for g in range(2)]
                d["w1"], d["w2"] = [], []
                for g in range(4):
                    d["w1"].append([add_load(kpc(w1[l][:, g * 1024 + a * 512: g * 1024 + (a + 1) * 512])) for a in range(2)])
                    d["w2"].append([add_load(kpc(w2[l][g * 1024:(g + 1) * 1024, dh * 512:(dh + 1) * 512])) for dh in range(2)])
                d["wg0"] = add_load(kpc(wg[l][:, 0:512]))
                d["wp"] = add_load_multi([((0, 2), kpc(wp[l][:, 0:512])), ((2, 4), kpc(wp[l][:, 512:1024]))])
                d["wg1"] = add_load(kpc(wg[l][:, 512:1024]))
                plan[(s, l)] = d

        def nextbank(cands):
            b = cands[st["rr"] % len(cands)]
            st["rr"] += 1
            return b

        S.dma("sp", "dsm", [lambda e: e.dma_start(out=sm[:], in_=smalls)], writes=("sm",))
        S.dma("sp", "dgf", [lambda e: e.dma_start(out=gf[:], in_=gfin)], writes=("gf",))
        S.dma("sp", "divf", [lambda e: e.dma_start(out=ivf[:], in_=invf)], writes=("ivf",))
        S.dma("sp", "didf", [lambda e: e.dma_start(out=idf[:], in_=cst[:, 0:128])], writes=("idf",))
        S.dma("pool", "dcb", [lambda e: e.dma_start(out=cb[:], in_=cst.rearrange("p (a b) -> p a b", a=3))], writes=("cb",))
        S.op("dve", lambda e: e.memset(cst_c[:, 0:1], EPS), writes=("cstc0",))
        S.op("dve", lambda e: e.memset(cst_c[:, 1:2], -PI), writes=("cstc1",))
        S.op("dve", lambda e: e.memset(xr_ext[:, 0:4], 0.0), writes=("xr_ext",))

        def tbs(tb):
            return slice(tb * TB, (tb + 1) * TB)

        def rmsnorm_to_B(gcol_fn, tag):
            for tb in range(NTB):
                rmsnorm_tb(gcol_fn, tb)

        def rmsnorm_tb(gcol_fn, tb):
            if True:
                bk = nextbank([4, 5, 6, 7])
                for c in range(8):
                    sq = Pb[0][c % 2]
                    sqk = f"Pb0{c % 2}"
                    S.op("act", lambda e, sq=sq, c=c, tb=tb: e.activation(sq[:], hT[:, c, tbs(tb)], AF.Square),
                         reads=(f"h{c}_{tb}",), writes=(sqk,))
                    S.op("pe", lambda e, sq=sq, c=c, bk=bk: e.matmul(bank[bk], ones, sq[:], start=(c == 0), stop=(c == 7)),
                         reads=(sqk, "cb"), writes=(bkey[bk],))
                Er = (E0, E1)[tb % 2]
                Ek = ("E0", "E1")[tb % 2]
                S.op("act", lambda e, Er=Er, bk=bk: e.activation(Er[:], bank[bk], AF.Ln, bias=eps_c, scale=1.0 / D),
                     reads=(bkey[bk], "cstc0"), writes=(Ek,))
                S.op("act", lambda e, Er=Er: e.activation(Er[:], Er[:], AF.Exp, scale=-0.5), reads=(Ek,), writes=(Ek,))
                for c in range(8):
                    S.op("dve", lambda e, Er=Er, c=c, tb=tb: e.scalar_tensor_tensor(
                        Bt[:, c, tbs(tb)], hT[:, c, tbs(tb)], gcol_fn(c), Er[:], ALU.mult, ALU.mult),
                        reads=(f"h{c}_{tb}", Ek, "sm", "gf"), writes=(f"B{c}_{tb}",))

        def qk_idx(h, tb):
            return h * 4 + tb

        def chk(k):
            if lim is not None and k > lim:
                raise _Stop()

        for s in range(n_seq):
            S.dma("sp", "dx", [(lambda e, s=s, c=c: e.dma_start(out=hT[:, c, :], in_=xT[s, c * 128:(c + 1) * 128, :])) for c in range(8)],
                  writes=[f"h{c}_{tb}" for c in range(8) for tb in range(NTB)])
            S.dma("sp", "dpos", [lambda e, s=s: e.dma_start(out=posi[:], in_=pos[s])], writes=("posi",))
            S.op("dve", lambda e: e.tensor_copy(posf[:], posi[:]), reads=("posi",), writes=("posf",))
            S.op("dve", lambda e: e.tensor_tensor(ang[:], posf[:].unsqueeze(2).to_broadcast([128, NTT, 8]),
                                                  ivf[:].unsqueeze(1).to_broadcast([128, NTT, 8]), ALU.mult),
                 reads=("posf", "ivf"), writes=("ang",))
            S.op("dve", lambda e: e.memset(n2pi, -2 * PI), writes=("xrot",))
            af, cf, sf = ang[:].rearrange("p a b -> p (a b)"), cosT[:].rearrange("p a b -> p (a b)"), sinT[:].rearrange("p a b -> p (a b)")
            S.op("dve", lambda e: e.tensor_scalar(cf, af, 1.0 / (2 * PI), None, ALU.mult), reads=("ang",), writes=("cosT",))
            S.op("dve", lambda e: e.tensor_copy(kint[:], cf), reads=("cosT",), writes=("kint",))
            S.op("dve", lambda e: e.tensor_copy(cf, kint[:]), reads=("kint",), writes=("cosT",))
            S.op("dve", lambda e: e.scalar_tensor_tensor(af, cf, -6.28125, af, ALU.mult, ALU.add), reads=("cosT", "ang"), writes=("ang",))
            S.op("dve", lambda e: e.scalar_tensor_tensor(af, cf, -(2 * PI - 6.28125), af, ALU.mult, ALU.add), reads=("cosT", "ang"), writes=("ang",))
            S.op("dve", lambda e: e.scalar_tensor_tensor(sf, af, PI, n2pi, ALU.is_gt, ALU.mult), reads=("ang", "xrot"), writes=("sinT",))
            S.op("dve", lambda e: e.tensor_tensor(af, af, sf, ALU.add), reads=("ang", "sinT"), writes=("ang",))
            S.op("dve", lambda e: e.tensor_scalar(af, af, -PI, PI, ALU.max, ALU.min), reads=("ang",), writes=("ang",))
            S.op("act", lambda e: e.activation(sf, af, AF.Sin), reads=("ang",), writes=("sinT",))
            S.op("dve", lambda e: e.tensor_scalar(cf, af, 0.5 * PI, None, ALU.add), reads=("ang",), writes=("cosT",))
            S.op("dve", lambda e: e.scalar_tensor_tensor(rtmp, cf, PI, n2pi, ALU.is_gt, ALU.mult), reads=("cosT", "xrot"), writes=("tcv",))
            S.op("dve", lambda e: e.tensor_tensor(cf, cf, rtmp, ALU.add), reads=("cosT", "tcv"), writes=("cosT",))
            S.op("dve", lambda e: e.tensor_scalar(cf, cf, -PI, PI, ALU.max, ALU.min), reads=("cosT",), writes=("cosT",))
            S.op("act", lambda e: e.activation(cf, cf, AF.Sin), reads=("cosT",), writes=("cosT",))
            S.op("dve", lambda e: e.tensor_copy(cos2T[:, :, 0:8], cosT[:]), reads=("cosT",), writes=("cos2T",))
            S.op("dve", lambda e: e.tensor_copy(cos2T[:, :, 8:16], cosT[:]), reads=("cosT", "cos2T"), writes=("cos2T",))

            def emit_layer(s, l):
                pl = plan[(s, l)]
                chk(0)
                lam_init = 0.8 - 0.6 * math.exp(-0.3 * l)
                S.dma("pool", "dgw", [lambda e, l=l: e.dma_start(out=gW[:], in_=gatew[l].rearrange("p (g c j) -> p g c j", g=2, c=4))],
                      writes=("gW",))
                S.dma("sp", "dlqk", [lambda e, l=l: e.dma_start(out=E0[:, 0:256], in_=lqk[l])], writes=("E0",))
                S.op("dve", lambda e: e.tensor_tensor(E0[:, 0:128], E0[:, 0:128], E0[:, 128:256], ALU.mult), reads=("E0",), writes=("E0",))
                S.op("dve", lambda e: e.tensor_reduce(der[:, 0:2], E0[:, 0:128].rearrange("p (a b) -> p a b", a=2), AX.X, ALU.add),
                     reads=("E0",), writes=("der01",))
                S.op("act", lambda e: e.activation(der[:, 0:2], der[:, 0:2], AF.Exp), reads=("der01",), writes=("der01",))
                S.op("dve", lambda e: e.tensor_tensor(der[:, 2:3], der[:, 0:1], der[:, 1:2], ALU.subtract), reads=("der01",), writes=("der2",))
                S.op("dve", lambda e, li=lam_init: e.tensor_scalar(der[:, 3:4], der[:, 2:3], li, -1.0, ALU.add, ALU.mult),
                     reads=("der2",), writes=("neglam",))
                S.op("dve", lambda e, l=l, li=lam_init: e.tensor_scalar(der[:, 4:5], sm[:, l, GS:GS + 1], 1.0 - li, None, ALU.mult),
                     reads=("sm",), writes=("gsub",))
                S.op("act", lambda e, l=l: e.activation(der[:, 5:9], sm[:, l, LL:LL + 4], AF.Exp, scale=-1.0), reads=("sm",), writes=("c1",))
                S.op("act", lambda e: e.activation(der[:, 5:9], der[:, 5:9], AF.Ln, bias=1.0, scale=1.0), reads=("c1",), writes=("c1",))
                S.op("dve", lambda e: e.tensor_scalar(der[:, 9:13], der[:, 5:9], -16.0, None, ALU.mult), reads=("c1",), writes=("c2",))
                S.op("dve", lambda e: e.tensor_scalar(der[:, 5:9], der[:, 5:9], -8.0, None, ALU.mult), reads=("c1", "c2"), writes=("c1",))
                neglam = der[:, 3:4]
                gsub = der[:, 4:5]

                rmsnorm_to_B(lambda c, l=l: sm[:, l, GM + c:GM + c + 1], "n1")

                chk(1)
                wv, wvk_ = use_w(pl["win"][2])
                for tt in range(NTT):
                    tb = tt // 4
                    tsl = slice(tt * 128, (tt + 1) * 128)
                    vb = 4 + (tt % 2)

                    def projv(e, tsl=tsl, vb=vb):
                        ins = None
                        for c in range(8):
                            ins = e.matmul(bank[vb], Bt[:, c, tsl], wv[:, c, :], start=(c == 0), stop=(c == 7))
                        return ins
                    S.op("pe", projv, reads=[f"B{c}_{tb}" for c in range(8)] + [wvk_], writes=(bkey[vb],))
                    if tt % 2 == 0:
                        S.op("dve", lambda e, tt=tt, vb=vb: e.tensor_copy(V[:, tt, :], bank[vb]), reads=(bkey[vb],), writes=(f"V{tt}",))
                    else:
                        S.op("act", lambda e, tt=tt, vb=vb: e.activation(V[:, tt, :], bank[vb], AF.Copy), reads=(bkey[vb],), writes=(f"V{tt}",))

                done_w(pl["win"][2])
                chk(2)
                wq, wqk_ = use_w(pl["win"][0])
                wk, wkk_ = use_w(pl["win"][1], paired=True)
                tcv = tcs_t[:, 128:256].rearrange("p (g d) -> p g d", d=16)
                tsv = tcs_t[:, 256:384].rearrange("p (g d) -> p g d", d=16)
                Fnames = ("gg", "xc")
                Tb = (4, 5)

                def emit_transposes(tt):
                    tb, off = tt // 4, (tt % 4) * 128
                    for hb in range(2):
                        Ft = Rt[Fnames[hb]]
                        pb = Tb[hb]
                        for g in range(4):
                            S.op("pe", lambda e, g=g, Ft=Ft, pb=pb: e.transpose(bank[pb][:, g * 128:(g + 1) * 128], Ft[:, g * 128:(g + 1) * 128], idf[:]),
                                 reads=(Fnames[hb], "idf"), writes=(bkey[pb],))
                        src = bank[pb].rearrange("p (h t) -> p h t", h=4)
                        if hb == 0:
                            S.op("act", lambda e, tb=tb, off=off, src=src: e.activation(QK[:, tb:16:4, off:off + 128], src, AF.Copy),
                                 reads=(bkey[pb],), writes=[f"qk{qk_idx(h, tb)}" for h in range(4)])
                        else:
                            S.op("dve", lambda e, tb=tb, off=off, src=src: e.tensor_copy(QK[:, 16 + tb:32:4, off:off + 128], src),
                                 reads=(bkey[pb],), writes=[f"qk{16 + qk_idx(h, tb)}" for h in range(4)])

                for tt in range(NTT):
                    tb = tt // 4
                    tsl = slice(tt * 128, (tt + 1) * 128)
                    A = psA[tt % 2]
                    Ak = (bkey[0], bkey[1]) if tt % 2 == 0 else (bkey[2], bkey[3])

                    def proj(e, A=A, tsl=tsl):
                        ins = None
                        for c in range(8):
                            e.matmul(A[:, 0:512], Bt[:, c, tsl], wq[:, c, :], start=(c == 0), stop=(c == 7))
                            ins = e.matmul(A[:, 512:1024], Bt[:, c, tsl], wk[:, c, :], start=(c == 0), stop=(c == 7))
                        return ins
                    S.op("pe", proj, reads=[f"B{c}_{tb}" for c in range(8)] + [wqk_, wkk_], writes=(Ak[0], Ak[1]))
                    if tt >= 1 and not os.environ.get('SKIP_TR'):
                        emit_transposes(tt - 1)
                    cos2B = cos2T[:, tt, :].unsqueeze(1).to_broadcast([128, 8, 16])
                    sinB8 = sinT[:, tt, :].unsqueeze(1).to_broadcast([128, 8, 8])
                    for hb in range(2):
                        Ah = A[:, hb * 512:(hb + 1) * 512]
                        Fn = Fnames[hb]
                        Ft = Rt[Fn]
                        if hb == 0:
                            S.op("act", lambda e, Ft=Ft, Ah=Ah: e.activation(Ft[:], Ah, AF.Copy), reads=(Ak[hb],), writes=(Fn,))
                        else:
                            S.op("dve", lambda e, Ft=Ft, Ah=Ah: e.tensor_copy(Ft[:], Ah), reads=(Ak[hb],), writes=(Fn,))
                        if os.environ.get('SKIP_ROPE'):
                            continue
                        F3 = Ft[:].rearrange("p (g d) -> p g d", d=64)
                        S.op("dve", lambda e, F3=F3, cos2B=cos2B: e.tensor_tensor(tcv, F3[:, :, 0:16], cos2B, ALU.mult),
                             reads=(Fn, "cos2T"), writes=("tcv",))
                        S.op("dve", lambda e, F3=F3, sinB8=sinB8: e.scalar_tensor_tensor(tsv[:, :, 0:8], F3[:, :, 8:16], -1.0, sinB8, ALU.mult, ALU.mult),
                             reads=(Fn, "sinT"), writes=("tsv",))
                        S.op("dve", lambda e, F3=F3, sinB8=sinB8: e.tensor_tensor(tsv[:, :, 8:16], F3[:, :, 0:8], sinB8, ALU.mult),
                             reads=(Fn, "sinT", "tsv"), writes=("tsv",))
                        S.op("dve", lambda e, F3=F3: e.tensor_tensor(F3[:, :, 0:16], tcv, tsv, ALU.add),
                             reads=("tcv", "tsv", Fn), writes=(Fn,))
                if not os.environ.get('SKIP_TR'):
                    emit_transposes(NTT - 1)

                done_w(pl["win"][0])
                done_w(pl["win"][1])
                chk(3)
                wxr, wxrk = use_w(pl["win"][3])
                wgr, wgrk = use_w(pl["win"][4], paired=True)

                def rnn_proj1(u):
                    c, tb = u // 4, u % 4
                    def f(e, c=c, tb=tb):
                        ins = None
                        for k in range(8):
                            ins = e.matmul(bank[6], wxr[:, k, c * 128:(c + 1) * 128], Bt[:, k, tbs(tb)], start=(k == 0), stop=(k == 7))
                        return ins
                    S.op("pe", f, reads=[f"B{k}_{tb}" for k in range(8)] + [wxrk], writes=(bkey[6],))
                    if tb == 0:
                        S.op("dve", lambda e: e.memset(xr_ext[:, 0:4], 0.0), reads=("xr_ext",), writes=("xr_ext",))
                    else:
                        S.op("dve", lambda e: e.tensor_copy(xr_ext[:, 0:4], xr_ext[:, 512:516]), reads=("xr_ext",), writes=("xr_ext",))
                    S.op("act", lambda e: e.activation(xr_ext[:, 4:516], bank[6], AF.Copy), reads=(bkey[6],), writes=("xr_ext",))

                def rnn_proj2(u):
                    c, tb = u // 4, u % 4
                    def f2(e, c=c, tb=tb):
                        ins = None
                        for k in range(8):
                            ins = e.matmul(bank[7], wgr[:, k, c * 128:(c + 1) * 128], Bt[:, k, tbs(tb)], start=(k == 0), stop=(k == 7))
                        return ins
                    S.op("pe", f2, reads=[f"B{k}_{tb}" for k in range(8)] + [wgrk], writes=(bkey[7],))
                    S.op("act", lambda e: e.activation(Rt["gg"][:], bank[7], AF.Copy), reads=(bkey[7],), writes=("gg",))
                    S.op("act", lambda e: e.activation(Rt["hs"][:], bank[7], AF.Square), reads=(bkey[7],), writes=("hs",))
                    S.op("dve", lambda e: e.tensor_scalar(Rt["hs"][:], Rt["hs"][:], 0.044715, 1.0, ALU.mult, ALU.add), reads=("hs",), writes=("hs",))
                    S.op("dve", lambda e: e.tensor_tensor(Rt["hs"][:], Rt["hs"][:], Rt["gg"][:], ALU.mult), reads=("hs", "gg"), writes=("hs",))
                    S.op("act", lambda e: e.activation(Rt["hs"][:], Rt["hs"][:], AF.Sigmoid, scale=1.5957691216057308), reads=("hs",), writes=("hs",))
                    S.op("dve", lambda e: e.tensor_tensor(Rt["gg"][:], Rt["gg"][:], Rt["hs"][:], ALU.mult), reads=("hs", "gg"), writes=("gg",))
                    cw = lambda k, c=c: sm[:, l, CW + c * 4 + k:CW + c * 4 + k + 1]
                    S.op("dve", lambda e, c=c: e.tensor_scalar(Rt["xc"][:], xr_ext[:, 4:516], cw(3), sm[:, l, CB + c:CB + c + 1], ALU.mult, ALU.add),
                         reads=("xr_ext", "sm"), writes=("xc",))
                    for j in (1, 2, 3):
                        S.op("dve", lambda e, j=j: e.scalar_tensor_tensor(Rt["xc"][:], xr_ext[:, 4 - j:516 - j], cw(3 - j), Rt["xc"][:], ALU.mult, ALU.add),
                             reads=("xr_ext", "xc", "sm"), writes=("xc",))
                    S.op("act", lambda e: e.activation(xcb[:], Rt["xc"][:], AF.Copy), reads=("xc",), writes=("xcb",))

                def rnn_gates1(u):
                    c, tb = u // 4, u % 4
                    S.op("pe", lambda e, c=c: e.matmul(bank[6], gW[:, 0, c, :], xcb[:], start=True, stop=True), reads=("xcb", "gW"), writes=(bkey[6],))
                    S.op("act", lambda e, c=c: e.activation(Rt["r"][:], bank[6], AF.Sigmoid, bias=sm[:, l, BA + c:BA + c + 1], scale=1.0),
                         reads=(bkey[6], "sm"), writes=("r",))

                def rnn_gates2(u):
                    c, tb = u // 4, u % 4
                    S.op("pe", lambda e, c=c: e.matmul(bank[7], gW[:, 1, c, :], xcb[:], start=True, stop=True), reads=("xcb", "gW"), writes=(bkey[7],))
                    S.op("act", lambda e, c=c: e.activation(Rt["i"][:], bank[7], AF.Sigmoid, bias=sm[:, l, BX + c:BX + c + 1], scale=1.0),
                         reads=(bkey[7], "sm"), writes=("i",))
                    S.op("dve", lambda e: e.tensor_tensor(Rt["i"][:], Rt["i"][:], Rt["xc"][:], ALU.mult), reads=("i", "xc"), writes=("i",))
                    S.op("act", lambda e, c=c: e.activation(Rt["xc"][:], Rt["r"][:], AF.Exp, scale=der[:, 9 + c:10 + c]),
                         reads=("r", "c2"), writes=("xc",))
                    S.op("act", lambda e, c=c: e.activation(Rt["r"][:], Rt["r"][:], AF.Exp, scale=der[:, 5 + c:6 + c]),
                         reads=("r", "c1"), writes=("r",))
                    S.op("act", lambda e: e.activation(Rt["xc"][:], Rt["xc"][:], AF.Ln, bias=1.0, scale=-1.0), reads=("xc",), writes=("xc",))
                    S.op("act", lambda e: e.activation(Rt["xc"][:], Rt["xc"][:], AF.Exp, scale=0.5), reads=("xc",), writes=("xc",))
                    S.op("dve", lambda e: e.tensor_tensor(Rt["i"][:], Rt["i"][:], Rt["xc"][:], ALU.mult), reads=("i", "xc"), writes=("i",))
                    if tb == 0:
                        S.op("dve", lambda e: e.memset(der[:, 13:14], 0.0), reads=("carry",), writes=("carry",))
                    S.op("dve", lambda e: e.tensor_tensor_scan(Rt["hs"][:], Rt["r"][:], Rt["i"][:], der[:, 13:14], ALU.mult, ALU.add),
                         reads=("r", "i", "carry"), writes=("hs",))
                    S.op("dve", lambda e: e.tensor_copy(der[:, 13:14], Rt["hs"][:, 511:512]), reads=("hs", "carry"), writes=("carry",))
                    S.op("dve", lambda e, c=c, tb=tb: e.tensor_tensor(Yt[:, c, tbs(tb)], Rt["hs"][:], Rt["gg"][:], ALU.mult),
                         reads=("hs", "gg"), writes=(f"Y{c}_{tb}",))

                pend = {"ss": None}

                def att_block(h, j, u_proj, u_gate):
                    nkt = 4 * j + 4
                    qi = qk_idx(h, j)
                    S0b, S1b, O1b, O2b, R1b, R2b = 0, 1, 2, 3, 4, 5

                    def geo(kt):
                        m = kt - 4 * j
                        q0 = m * 128 if m > 0 else 0
                        tbk, off = kt // 4, (kt % 4) * 128
                        return m, q0, 16 + qk_idx(h, tbk), off

                    def emit_qk_exp(kt):
                        m, q0, ki, off = geo(kt)
                        pbi = kt % 2

                        def qk(e, q0=q0, ki=ki, off=off, qi=qi):
                            e.matmul(bank[S0b][:, q0:512], QK[0:64, ki, off:off + 128], QK[0:64, qi, q0:512], start=True, stop=True, tile_position=(0, 0))
                            return e.matmul(bank[S1b][:, q0:512], QK[64:128, ki, off:off + 128], QK[64:128, qi, q0:512], start=True, stop=True, tile_position=(64, 0))
                        S.op("pe", qk, reads=(f"qk{ki}", f"qk{qi}"), writes=(bkey[S0b], bkey[S1b]))
                        for cc in range(2):
                            P = Pb[pbi][cc]
                            pk = f"Pb{pbi}{cc}"
                            S.op("act", lambda e, P=P, cc=cc, q0=q0: e.activation(P[:, q0:512], bank[cc][:, q0:512], AF.Exp, scale=SCALE),
                                 reads=(bkey[cc],), writes=(pk,))
                            if m >= 0:
                                S.op("dve", lambda e, P=P, q0=q0: e.tensor_tensor(P[:, q0:q0 + 128], P[:, q0:q0 + 128], cmask, ALU.mult),
                                     reads=(pk, "cb"), writes=(pk,))

                    def emit_pv(kt):
                        m, q0, ki, off = geo(kt)
                        pbi = kt % 2

                        def pv(e, q0=q0, kt=kt, pbi=pbi, h=h, first=(kt == 0), last=(kt == nkt - 1)):
                            vv = V[:, kt, h * 128:(h + 1) * 128]
                            e.matmul(bank[O1b][:, q0:512], vv, Pb[pbi][0][:, q0:512], start=first, stop=last)
                            e.matmul(bank[R1b][:, q0:512], ones, Pb[pbi][0][:, q0:512], start=first, stop=last)
                            e.matmul(bank[O2b][:, q0:512], vv, Pb[pbi][1][:, q0:512], start=first, stop=last)
                            return e.matmul(bank[R2b][:, q0:512], ones, Pb[pbi][1][:, q0:512], start=first, stop=last)
                        S.op("pe", pv, reads=(f"V{kt}", f"Pb{pbi}0", f"Pb{pbi}1", "cb"),
                             writes=(bkey[O1b], bkey[O2b], bkey[R1b], bkey[R2b]))

                    emit_qk_exp(0)
                    for kt in range(nkt):
                        if kt + 1 < nkt:
                            emit_qk_exp(kt + 1)
                        emit_pv(kt)
                        if kt == 0:
                            if pend["ss"] is not None:
                                pend["ss"]()
                                pend["ss"] = None
                            if u_gate is not None:
                                rnn_gates1(u_gate)
                        if kt == 1 and u_gate is not None:
                            rnn_gates2(u_gate)
                        if kt == nkt - 2 and u_proj is not None:
                            rnn_proj1(u_proj)
                    S.op("act", lambda e: e.activation(E0[:], bank[R1b], AF.Ln), reads=(bkey[R1b],), writes=("E0",))
                    S.op("act", lambda e: e.activation(E1[:], bank[R2b], AF.Ln), reads=(bkey[R2b],), writes=("E1",))
                    S.op("act", lambda e: e.activation(E0[:], E0[:], AF.Exp, scale=-1.0), reads=("E0",), writes=("E0",))
                    S.op("act", lambda e: e.activation(E1[:], E1[:], AF.Exp, scale=-1.0), reads=("E1",), writes=("E1",))
                    S.op("dve", lambda e: e.tensor_tensor(E0[:], bank[O1b], E0[:], ALU.mult), reads=(bkey[O1b], "E0"), writes=("E0",))
                    S.op("dve", lambda e: e.tensor_tensor(E1[:], bank[O2b], E1[:], ALU.mult), reads=(bkey[O2b], "E1"), writes=("E1",))
                    S.op("dve", lambda e: e.scalar_tensor_tensor(E0[:], E1[:], neglam, E0[:], ALU.mult, ALU.add),
                         reads=("E0", "E1", "neglam"), writes=("E0",))
                    S.op("act", lambda e: e.activation(E2[:], E0[:], AF.Square), reads=("E0",), writes=("E2",))

                    def part_b(qi=qi):
                        S.op("pe", lambda e: e.matmul(bank[7], ones, E2[:], start=True, stop=True), reads=("E2", "cb"), writes=(bkey[7],))
                        S.op("act", lambda e: e.activation(E1[:], bank[7], AF.Ln, bias=eps_c, scale=1.0 / 128), reads=(bkey[7], "cstc0"), writes=("E1",))
                        S.op("act", lambda e: e.activation(E1[:], E1[:], AF.Exp, scale=-0.5), reads=("E1",), writes=("E1",))
                        S.op("dve", lambda e, qi=qi: e.scalar_tensor_tensor(QK[:, qi, :], E0[:], gsub, E1[:], ALU.mult, ALU.mult),
                             reads=("E0", "E1", "gsub"), writes=(f"qk{qi}",))
                    pend["ss"] = part_b
                    if u_proj is not None:
                        rnn_proj2(u_proj)

                blocks = [(h, j) for h in range(4) for j in range(NTB)]
                rnn_proj1(0)
                rnn_proj2(0)
                for bi, (h, j) in enumerate(blocks):
                    att_block(h, j, bi + 1 if bi + 1 < 16 else None, bi)
                pend["ss"]()
                pend["ss"] = None

                done_w(pl["win"][3])
                done_w(pl["win"][4])
                chk(4)
                wo0_, wok0 = use_w(pl["wo"][0])
                wo1_, wok1 = use_w(pl["wo"][1], paired=True)
                for tb in range(NTB):
                    for g2 in range(2):
                        wo_, wok = (wo0_, wok0) if g2 == 0 else (wo1_, wok1)
                        for d4 in range(4):
                            dtc = g2 * 4 + d4
                            bk = nextbank([0, 1, 2, 3, 4, 5, 6, 7])
                            def f(e, d4=d4, tb=tb, bk=bk, wo_=wo_):
                                ins = None
                                for c in range(4):
                                    e.matmul(bank[bk], wo_[:, c, d4 * 128:(d4 + 1) * 128], QK[:, c * 4 + tb, :], start=(c == 0), stop=False)
                                for c in range(4):
                                    ins = e.matmul(bank[bk], wo_[:, 4 + c, d4 * 128:(d4 + 1) * 128], Yt[:, c, tbs(tb)], start=False, stop=(c == 3))
                                return ins
                            S.op("pe", f, reads=[f"qk{c * 4 + tb}" for c in range(4)] + [f"Y{c}_{tb}" for c in range(4)] + [wok], writes=(bkey[bk],))
                            S.op("dve", lambda e, dtc=dtc, tb=tb, bk=bk: e.tensor_tensor(hT[:, dtc, tbs(tb)], bank[bk], hT[:, dtc, tbs(tb)], ALU.add),
                                 reads=(bkey[bk], f"h{dtc}_{tb}"), writes=(f"h{dtc}_{tb}",))
                    if tb >= 1:
                        rmsnorm_tb(lambda c, l=l: sm[:, l, GL + c:GL + c + 1], tb - 1)
                rmsnorm_tb(lambda c, l=l: sm[:, l, GL + c:GL + c + 1], NTB - 1)
                done_w(pl["wo"][0])
                done_w(pl["wo"][1])

                chk(5)
                pTb = V[:, 0:8, :].rearrange("p a b -> p (a b)").rearrange("p (k t) -> p k t", k=2)
                S.dma("pool", "dpt", [(lambda e, l=l, s=s, k=k: e.dma_start(out=pTb[:, k, :], in_=pT[l, s, k * 128:(k + 1) * 128, :])) for k in range(2)],
                      writes=[f"V{t}" for t in range(8)])
                sqt = [Rt["gg"], Rt["xc"]]
                sqk = ["gg", "xc"]
                cnt = 0
                for g in range(4):
                    for a in range(2):
                        w1_, w1k = use_w(pl["w1"][g][a])
                        for f4 in range(4):
                            fc = a * 4 + f4
                            for tb in range(NTB):
                                bk = nextbank([0, 1, 2, 3, 4, 5, 6, 7])
                                hk = f"qk{fc * 4 + tb}"
                                def f(e, f4=f4, tb=tb, bk=bk, w1_=w1_):
                                    ins = None
                                    for c in range(8):
                                        ins = e.matmul(bank[bk], w1_[:, c, f4 * 128:(f4 + 1) * 128], Bt[:, c, tbs(tb)], start=(c == 0), stop=(c == 7))
                                    return ins
                                S.op("pe", f, reads=[f"B{c}_{tb}" for c in range(8)] + [w1k], writes=(bkey[bk],))
                                ti = cnt % 2
                                cnt += 1
                                S.op("act", lambda e, bk=bk, ti=ti: e.activation(sqt[ti][:], bank[bk], AF.Square), reads=(bkey[bk],), writes=(sqk[ti],))
                                S.op("dve", lambda e, bk=bk, ti=ti, fc=fc, tb=tb: e.scalar_tensor_tensor(
                                    QK[:, fc * 4 + tb, :], bank[bk], 0.0, sqt[ti][:], ALU.is_gt, ALU.mult),
                                    reads=(bkey[bk], sqk[ti]), writes=(hk,))
                    def mlp_out_tile(w2_, w2k, dh, d4, tb):
                        dtc = dh * 4 + d4
                        bk = nextbank([0, 1, 2, 3, 4, 5, 6, 7])
                        def f(e, d4=d4, tb=tb, bk=bk, w2_=w2_):
                            ins = None
                            for fc in range(8):
                                ins = e.matmul(bank[bk], w2_[:, fc, d4 * 128:(d4 + 1) * 128], QK[:, fc * 4 + tb, :], start=(fc == 0), stop=(fc == 7))
                            return ins
                        S.op("pe", f, reads=[f"qk{fc * 4 + tb}" for fc in range(8)] + [w2k], writes=(bkey[bk],))
                        S.op("dve", lambda e, dtc=dtc, tb=tb, bk=bk: e.tensor_tensor(hT[:, dtc, tbs(tb)], bank[bk], hT[:, dtc, tbs(tb)], ALU.add),
                             reads=(bkey[bk], f"h{dtc}_{tb}"), writes=(f"h{dtc}_{tb}",))
                    if g < 3:
                        for dh in range(2):
                            w2_, w2k = use_w(pl["w2"][g][dh])
                            for d4 in range(4):
                                for tb in range(NTB):
                                    mlp_out_tile(w2_, w2k, dh, d4, tb)
                    else:
                        w2a = use_w(pl["w2"][g][0])
                        w2b = use_w(pl["w2"][g][1], paired=True)
                        for tb in range(NTB):
                            for dh in range(2):
                                w2_, w2k = w2a if dh == 0 else w2b
                                for d4 in range(4):
                                    mlp_out_tile(w2_, w2k, dh, d4, tb)
                            if tb >= 1:
                                rmsnorm_tb(lambda c, l=l: sm[:, l, GP + c:GP + c + 1], tb - 1)
                        rmsnorm_tb(lambda c, l=l: sm[:, l, GP + c:GP + c + 1], NTB - 1)
                        done_w(pl["w2"][g][0])
                        done_w(pl["w2"][g][1])

                chk(6)
                sgt = [Rt["r"], Rt["i"]]
                sgk = ["r", "i"]
                t2t = [Rt["hs"], Rt["gg"]]
                t2k = ["hs", "gg"]
                cnt = 0
                wp_, wpk = None, None
                for dh in range(2):
                    if dh == 0:
                        wg_, wgk_ = use_w(pl["wg0"])
                        wp_, wpk = use_w(pl["wp"], paired=True)
                    else:
                        done_w(pl["wg0"])
                        wg_, wgk_ = use_w(pl["wg1"], paired=True)
                    for d4 in range(4):
                        dtc = dh * 4 + d4
                        for tb in range(NTB):
                            bg = nextbank([0, 1, 2, 3, 4, 5, 6, 7])
                            bp = nextbank([0, 1, 2, 3, 4, 5, 6, 7])
                            def f(e, d4=d4, tb=tb, bg=bg, wg_=wg_):
                                ins = None
                                for c in range(8):
                                    ins = e.matmul(bank[bg], wg_[:, c, d4 * 128:(d4 + 1) * 128], Bt[:, c, tbs(tb)], start=(c == 0), stop=(c == 7))
                                return ins
                            S.op("pe", f, reads=[f"B{c}_{tb}" for c in range(8)] + [wgk_], writes=(bkey[bg],))
                            def f2(e, d4=d4, tb=tb, bp=bp, dh=dh, wp_=wp_):
                                ins = None
                                for k in range(2):
                                    ins = e.matmul(bank[bp], wp_[:, dh * 2 + k, d4 * 128:(d4 + 1) * 128], pTb[:, k, tbs(tb)], start=(k == 0), stop=(k == 1))
                                return ins
                            S.op("pe", f2, reads=[f"V{t}" for t in range(8)] + [wpk], writes=(bkey[bp],))
                            ti = cnt % 2
                            cnt += 1
                            S.op("act", lambda e, bg=bg, ti=ti: e.activation(sgt[ti][:], bank[bg], AF.Sigmoid), reads=(bkey[bg],), writes=(sgk[ti],))
                            S.op("dve", lambda e, bp=bp, ti=ti: e.tensor_tensor(t2t[ti][:], bank[bp], sgt[ti][:], ALU.mult),
                                 reads=(bkey[bp], sgk[ti]), writes=(t2k[ti],))
                            S.op("dve", lambda e, dtc=dtc, tb=tb, ti=ti: e.tensor_tensor(hT[:, dtc, tbs(tb)], t2t[ti][:], hT[:, dtc, tbs(tb)], ALU.add),
                                 reads=(t2k[ti], f"h{dtc}_{tb}"), writes=(f"h{dtc}_{tb}",))

            for l in range(n_layers):
                try:
                    emit_layer(s, l)
                except _Stop:
                    pass
            for tb in range(NTB):
                bk = nextbank([4, 5, 6, 7])
                for c in range(8):
                    sq = Pb[0][c % 2]
                    sqk_ = f"Pb0{c % 2}"
                    S.op("act", lambda e, sq=sq, c=c, tb=tb: e.activation(sq[:], hT[:, c, tbs(tb)], AF.Square),
                         reads=(f"h{c}_{tb}",), writes=(sqk_,))
                    S.op("pe", lambda e, sq=sq, c=c, bk=bk: e.matmul(bank[bk], ones, sq[:], start=(c == 0), stop=(c == 7)),
                         reads=(sqk_, "cb"), writes=(bkey[bk],))
                Er = (E0, E1)[tb % 2]
                Ek = ("E0", "E1")[tb % 2]
                S.op("act", lambda e, Er=Er, bk=bk: e.activation(Er[:], bank[bk], AF.Ln, bias=eps_c, scale=1.0 / D),
                     reads=(bkey[bk], "cstc0"), writes=(Ek,))
                S.op("act", lambda e, Er=Er: e.activation(Er[:], Er[:], AF.Exp, scale=-0.5), reads=(Ek,), writes=(Ek,))
                for c in range(8):
                    names = ("gg", "xc", "r", "i", "hs")
                    nm = names[c % 5]
                    ot = Rt[nm]
                    S.op("dve", lambda e, Er=Er, c=c, tb=tb, ot=ot: e.scalar_tensor_tensor(
                        ot[:], hT[:, c, tbs(tb)], gf[:, c:c + 1], Er[:], ALU.mult, ALU.mult),
                        reads=(f"h{c}_{tb}", Ek, "gf"), writes=(nm,))
                    S.dma("sp", "dout_" + nm, [lambda e, ot=ot, c=c, tb=tb, s=s: e.dma_start(out=outT[s, c * 128:(c + 1) * 128, tbs(tb)], in_=ot[:])],
                          reads=(nm,))

        sem_names = list(S.ENG) + sorted(S.dcnt.keys())
        sems = {n: es.enter_context(nc.semaphore("s_" + n)) for n in sem_names}
        final_waits = [(n, S.cnt[n]) for n in S.ENG if n != "sp" and S.cnt[n] > 0] + [(n, v) for n, v in S.dcnt.items()]
        block = es.enter_context(nc.Block())

        def replay(name, e):
            for waits, fn, (sn, inc) in S.q[name]:
                for (ws, wv) in waits:
                    e.wait_ge(sems[ws], wv)
                ins = fn(e)
                ins.then_inc(sems[sn], inc)
            if name == "sp":
                for (ws, wv) in final_waits:
                    e.wait_ge(sems[ws], wv)

        @block.tensor
        def _(e):
            replay("pe", e)

        @block.scalar
        def _(e):
            replay("act", e)

        @block.vector
        def _(e):
            replay("dve", e)

        @block.gpsimd
        def _(e):
            replay("pool", e)

        @block.sync
        def _(e):
            replay("sp", e)
    return nc, S


def _host_consts():
    ident = np.eye(128, dtype=np.float32)
    ones = np.ones((128, 128), np.float32)
    k = np.arange(128)[:, None]
    q = np.arange(128)[None, :]
    mask = (q >= k).astype(np.float32)
    cst = np.concatenate([ident, ones, mask], axis=1)
    half = 8
    inv_freq = (np.float32(500000.0) ** (-np.arange(half, dtype=np.float32) * np.float32(2.0) / np.float32(16))).astype(np.float32)
    invf = np.broadcast_to(inv_freq[None, :], (128, 8)).copy()
    return cst, invf


def _layout_inputs(inp, n_layers=L):
    f = lambda a: np.ascontiguousarray(np.asarray(a, dtype=np.float32))
    col8 = lambda v: v.reshape(8, 128).T
    col4 = lambda v: v.reshape(4, 128).T
    smalls = np.zeros((128, L, NSM), np.float32)
    gatew = np.zeros((L, 128, 2, 4, 128), np.float32)
    lqk = np.zeros((L, 128, 256), np.float32)
    for l in range(L):
        smalls[:, l, GM:GM + 8] = col8(f(inp["g_mix"][l]))
        smalls[:, l, GL:GL + 8] = col8(f(inp["g_mlp"][l]))
        smalls[:, l, GP:GP + 8] = col8(f(inp["g_ple"][l]))
        cw = f(inp["conv_w"][l])
        for c in range(4):
            for k in range(4):
                smalls[:, l, CW + c * 4 + k] = cw[k, c * 128:(c + 1) * 128]
        smalls[:, l, CB:CB + 4] = col4(f(inp["conv_b"][l]))
        smalls[:, l, BA:BA + 4] = col4(f(inp["b_gate_a"][l]))
        smalls[:, l, BX:BX + 4] = col4(f(inp["b_gate_x"][l]))
        smalls[:, l, LL:LL + 4] = col4(f(inp["lru_lambda"][l]))
        smalls[:, l, GS] = f(inp["g_subln"][l])
        for gi, nm in enumerate(("w_gate_a", "w_gate_x")):
            w = f(inp[nm][l])
            for c in range(4):
                for b in range(2):
                    gatew[l, b * 64:(b + 1) * 64, gi, c, b * 64:(b + 1) * 64] = w[2 * c + b]
        lqk[l, :, 0:128] = f(inp["lam_q"][l]).reshape(1, 128)
        lqk[l, :, 128:256] = f(inp["lam_k"][l]).reshape(1, 128)
    gatew = gatew.reshape(L, 128, 1024)
    gfin = np.ascontiguousarray(col8(f(inp["g_final"])))
    cst, invf = _host_consts()
    x = f(inp["x"])
    p = f(inp["p"])
    posn = np.asarray(inp["positions"]).astype(np.int32)
    shared = dict(w_in=f(inp["w_in"]), w_out=f(inp["w_out"]), w_mlp_in=f(inp["w_mlp_in"]), w_mlp_out=f(inp["w_mlp_out"]),
                  w_ple_gate=f(inp["w_ple_gate"]), w_ple_proj=f(inp["w_ple_proj"]), smalls=smalls, gatew=gatew, lqk=lqk,
                  gfin=gfin, cst=cst, invf=invf)
    maps = []
    for i in range(NC):
        xs = x[2 * i:2 * i + 2]
        m = dict(shared)
        m["xT"] = np.ascontiguousarray(xs.transpose(0, 2, 1))
        m["pT"] = np.ascontiguousarray(p[:, 2 * i:2 * i + 2].transpose(0, 1, 3, 2))
        m["pos"] = np.ascontiguousarray(posn[2 * i:2 * i + 2].reshape(2, NTT, 128).transpose(0, 2, 1))
        maps.append(m)
    return maps


_CACHE = {}


def kernel(**inputs):
    maps = _layout_inputs(inputs)
    if "nc" not in _CACHE:
        _CACHE["nc"] = build_program()[0]
    nc = _CACHE["nc"]
    res = run_bass_kernel_spmd(nc, maps, core_ids=list(range(NC)))
    out = np.empty((16, T, D), np.float32)
    for i in range(NC):
        o = res.results[i]["outT"]
        out[2 * i:2 * i + 2] = o.transpose(0, 2, 1)
    return out
```

```python
import math
import os
from contextlib import ExitStack

import numpy as np
import concourse.bass as bass
import concourse.mybir as mybir
from concourse.bass_utils import run_bass_kernel_spmd

F32 = mybir.dt.float32
BF16 = mybir.dt.bfloat16
I32 = mybir.dt.int32
AF = mybir.ActivationFunctionType
ALU = mybir.AluOpType
AX = mybir.AxisListType

D = 1024
T = 2048
L = 4
NC = 8
TB = 512
NTB = 4
NTT = 16
EPS = 1e-6
SCALE = 0.125
GM, GL, GP, CW, CB, BA, BX, LL, GS, NSM = 0, 8, 16, 24, 40, 44, 48, 52, 56, 64
PI = math.pi


class Sched:
    ENG = ("pe", "act", "dve", "pool", "sp")

    def __init__(self):
        self.q = {e: [] for e in self.ENG}
        self.cnt = {e: 0 for e in self.ENG}
        self.waited = {e: {} for e in self.ENG}
        self.lastw = {}
        self.readers = {}
        self.dcnt = {}

    def _waits(self, eng, reads, writes):
        need = {}
        for k in reads:
            ev = self.lastw.get(k)
            if ev is not None:
                need[ev[0]] = max(need.get(ev[0], 0), ev[1])
        for k in writes:
            ev = self.lastw.get(k)
            if ev is not None:
                need[ev[0]] = max(need.get(ev[0], 0), ev[1])
            for ev in self.readers.get(k, ()):
                need[ev[0]] = max(need.get(ev[0], 0), ev[1])
        waits = []
        for s, v in need.items():
            if eng == "pe" and s == "pe":
                continue
            if self.waited[eng].get(s, 0) < v:
                self.waited[eng][s] = v
                waits.append((s, v))
        return waits

    def _commit(self, ev, reads, writes):
        for k in reads:
            self.readers.setdefault(k, []).append(ev)
        for k in writes:
            self.lastw[k] = ev
            self.readers[k] = []

    def op(self, eng, fn, reads=(), writes=()):
        waits = self._waits(eng, reads, writes)
        self.cnt[eng] += 1
        ev = (eng, self.cnt[eng])
        self.q[eng].append((waits, fn, (eng, 1)))
        self._commit(ev, reads, writes)
        return ev

    def dma(self, queue, dsem, fns, reads=(), writes=()):
        waits = self._waits(queue, reads, writes)
        first = True
        for fn in fns:
            self.dcnt[dsem] = self.dcnt.get(dsem, 0) + 16
            self.q[queue].append((waits if first else [], fn, (dsem, 16)))
            first = False
        ev = (dsem, self.dcnt[dsem])
        self._commit(ev, reads, writes)
        return ev


class _Stop(Exception):
    pass


def build_program(n_layers=L, n_seq=2, lim=None):
    nc = bass.Bass("TRN2", target_bir_lowering=False)
    dt_in = lambda name, shape, dt=F32: nc.dram_tensor(name, shape, dt, kind="ExternalInput").ap()
    xT = dt_in("xT", [2, D, T])
    pT = dt_in("pT", [L, 2, 256, T])
    pos = dt_in("pos", [2, 128, NTT], I32)
    w_in = dt_in("w_in", [L, D, 2560])
    w_out = dt_in("w_out", [L, D, D])
    w1 = dt_in("w_mlp_in", [L, D, 4096])
    w2 = dt_in("w_mlp_out", [L, 4096, D])
    wg = dt_in("w_ple_gate", [L, D, D])
    wp = dt_in("w_ple_proj", [L, 256, D])
    smalls = dt_in("smalls", [128, L, NSM])
    gatew = dt_in("gatew", [L, 128, 1024])
    lqk = dt_in("lqk", [L, 128, 256])
    gfin = dt_in("gfin", [128, 8])
    cst = dt_in("cst", [128, 3 * 128])
    invf = dt_in("invf", [128, 8])
    outT = nc.dram_tensor("outT", [2, D, T], F32, kind="ExternalOutput").ap()

    S = Sched()
    with ExitStack() as es:
        sb = lambda name, shape, dt: es.enter_context(nc.sbuf_tensor(name, shape, dt))
        hT = sb("hT", [128, 8, T], F32)
        Bt = sb("Bt", [128, 8, T], BF16)
        QK = sb("QK", [128, 32, TB], BF16)
        V = sb("V", [128, NTT, 512], BF16)
        ring = sb("ring", [128, 2, 8, 512], BF16)
        Yt = sb("Yt", [128, 4, T], BF16)
        sm = sb("sm", [128, L, NSM], F32)
        gW = sb("gW", [128, 2, 4, 128], BF16)
        gf = sb("gf", [128, 8], F32)
        cb = sb("cb", [128, 3, 128], BF16)
        ivf = sb("ivf", [128, 8], F32)
        idf = sb("idf", [128, 128], F32)
        posi = sb("posi", [128, NTT], I32)
        posf = sb("posf", [128, NTT], F32)
        ang = sb("ang", [128, NTT, 8], F32)
        cosT = sb("cosT", [128, NTT, 8], F32)
        sinT = sb("sinT", [128, NTT, 8], F32)
        der = sb("der", [128, 16], F32)
        nbias = sb("nbias", [128, 8], F32)
        kint = sb("kint", [128, 128], I32)
        cst_c = sb("cst_c", [128, 4], F32)
        Pb = [[sb(f"Pb{i}{c}", [128, 512], BF16) for c in range(2)] for i in range(2)]
        E0 = sb("E0", [128, 512], F32)
        E1 = sb("E1", [128, 512], F32)
        E2 = sb("E2", [128, 512], BF16)
        xr_ext = sb("xr_ext", [128, 516], F32)
        Rt = {n: sb("R_" + n, [128, 512], F32) for n in ("gg", "xc", "r", "i", "hs")}
        xcb = sb("xcb", [128, 512], BF16)
        tcs_t = sb("tcs_t", [128, 512], F32)
        cos2T = sb("cos2T", [128, NTT, 16], F32)
        n2pi = tcs_t[:, 0:128]
        rtmp = tcs_t[:, 128:256]
        psA = [es.enter_context(nc.psum_tensor(f"psA{i}", [128, 1024], F32)) for i in range(2)]
        psB = [es.enter_context(nc.psum_tensor(f"psB{i}", [128, 512], F32)) for i in range(4)]
        bank = [psA[0][:, 0:512], psA[0][:, 512:1024], psA[1][:, 0:512], psA[1][:, 512:1024],
                psB[0][:], psB[1][:], psB[2][:], psB[3][:]]
        bkey = [f"ps{i}" for i in range(8)]

        ident = cb[:, 0, :]
        ones = cb[:, 1, :]
        cmask = cb[:, 2, :]
        eps_c = cst_c[:, 0:1]
        npi_c = cst_c[:, 1:2]

        loads = []
        st = {"next_load": 0, "n_emitted": 0, "rr": 0, "done": -1, "pending_done": []}

        def ring_key(n):
            return f"ring{n % 2}"

        def emit_load(n):
            fns = loads[n]
            S.dma("pool", f"dring{n % 2}", fns, reads=(), writes=(ring_key(n),))

        def _emit_upto(m):
            m = min(m, len(loads) - 1)
            while st["n_emitted"] <= m:
                emit_load(st["n_emitted"])
                st["n_emitted"] += 1

        def use_w(n, paired=False):
            if not paired:
                st["done"] = max(st["done"], n - 1)
            _emit_upto(max(n, st["done"] + 2))
            return ring[:, n % 2], ring_key(n)

        def done_w(n):
            st["done"] = max(st["done"], n)
            _emit_upto(st["done"] + 2)

        def mk_load(src_ap_fn, n_index_holder):
            pass

        def add_load(src):
            n = len(loads)
            slot = n % 2
            loads.append([lambda e, src=src, slot=slot: e.dma_start(out=ring[:, slot], in_=src)])
            return n

        def add_load_multi(pairs):
            n = len(loads)
            slot = n % 2
            fns = []
            for (osl, src) in pairs:
                fns.append(lambda e, src=src, osl=osl, slot=slot: e.dma_start(out=ring[:, slot, osl[0]:osl[1]], in_=src))
            loads.append(fns)
            return n

        def kpc(ap2d):
            return ap2d.rearrange("(k p) c -> p k c", p=128)

        plan = {}
        for s in range(n_seq):
            for l in range(n_layers):
                d = {}
                d["win"] = {g: add_load(kpc(w_in[l][:, g * 512:(g + 1) * 512])) for g in (2, 0, 1, 3, 4)}
                d["wo"] = [add_load(kpc(w_out[l][:, g * 512:(g + 1) * 512])) for g in range(2)]
                d["w1"], d["w2"] = [], []
                for g in range(4):
                    d["w1"].append([add_load(kpc(w1[l][:, g * 1024 + a * 512: g * 1024 + (a + 1) * 512])) for a in range(2)])
                    d["w2"].append([add_load(kpc(w2[l][g * 1024:(g + 1) * 1024, dh * 512:(dh + 1) * 512])) for dh in range(2)])
                d["wg0"] = add_load(kpc(wg[l][:, 0:512]))
                d["wp"] = add_load_multi([((0, 2), kpc(wp[l][:, 0:512])), ((2, 4), kpc(wp[l][:, 512:1024]))])
                d["wg1"] = add_load(kpc(wg[l][:, 512:1024]))
                plan[(s, l)] = d

        def nextbank(cands):
            b = cands[st["rr"] % len(cands)]
            st["rr"] += 1
            return b

        S.dma("sp", "dsm", [lambda e: e.dma_start(out=sm[:], in_=smalls)], writes=("sm",))
        S.dma("sp", "dgf", [lambda e: e.dma_start(out=gf[:], in_=gfin)], writes=("gf",))
        S.dma("sp", "divf", [lambda e: e.dma_start(out=ivf[:], in_=invf)], writes=("ivf",))
        S.dma("sp", "didf", [lambda e: e.dma_start(out=idf[:], in_=cst[:, 0:128])], writes=("idf",))
        S.dma("pool", "dcb", [lambda e: e.dma_start(out=cb[:], in_=cst.rearrange("p (a b) -> p a b", a=3))], writes=("cb",))
        S.op("dve", lambda e: e.memset(cst_c[:, 0:1], EPS), writes=("cstc0",))
        S.op("dve", lambda e: e.memset(cst_c[:, 1:2], -PI), writes=("cstc1",))
        S.op("dve", lambda e: e.memset(xr_ext[:, 0:4], 0.0), writes=("xr_ext",))

        def tbs(tb):
            return slice(tb * TB, (tb + 1) * TB)

        def rmsnorm_to_B(gcol_fn, tag):
            for tb in range(NTB):
                rmsnorm_tb(gcol_fn, tb)

        def rmsnorm_tb(gcol_fn, tb):
            if True:
                bk = nextbank([4, 5, 6, 7])
                for c in range(8):
                    sq = Pb[0][c % 2]
                    sqk = f"Pb0{c % 2}"
                    S.op("act", lambda e, sq=sq, c=c, tb=tb: e.activation(sq[:], hT[:, c, tbs(tb)], AF.Square),
                         reads=(f"h{c}_{tb}",), writes=(sqk,))
                    S.op("pe", lambda e, sq=sq, c=c, bk=bk: e.matmul(bank[bk], ones, sq[:], start=(c == 0), stop=(c == 7)),
                         reads=(sqk, "cb"), writes=(bkey[bk],))
                Er = (E0, E1)[tb % 2]
                Ek = ("E0", "E1")[tb % 2]
                S.op("act", lambda e, Er=Er, bk=bk: e.activation(Er[:], bank[bk], AF.Ln, bias=eps_c, scale=1.0 / D),
                     reads=(bkey[bk], "cstc0"), writes=(Ek,))
                S.op("act", lambda e, Er=Er: e.activation(Er[:], Er[:], AF.Exp, scale=-0.5), reads=(Ek,), writes=(Ek,))
                for c in range(8):
                    S.op("dve", lambda e, Er=Er, c=c, tb=tb: e.scalar_tensor_tensor(
                        Bt[:, c, tbs(tb)], hT[:, c, tbs(tb)], gcol_fn(c), Er[:], ALU.mult, ALU.mult),
                        reads=(f"h{c}_{tb}", Ek, "sm", "gf"), writes=(f"B{c}_{tb}",))

        def qk_idx(h, tb):
            return h * 4 + tb

        def chk(k):
            if lim is not None and k > lim:
                raise _Stop()

        for s in range(n_seq):
            S.dma("sp", "dx", [(lambda e, s=s, c=c: e.dma_start(out=hT[:, c, :], in_=xT[s, c * 128:(c + 1) * 128, :])) for c in range(8)],
                  writes=[f"h{c}_{tb}" for c in range(8) for tb in range(NTB)])
            S.dma("sp", "dpos", [lambda e, s=s: e.dma_start(out=posi[:], in_=pos[s])], writes=("posi",))
            S.op("dve", lambda e: e.tensor_copy(posf[:], posi[:]), reads=("posi",), writes=("posf",))
            S.op("dve", lambda e: e.tensor_tensor(ang[:], posf[:].unsqueeze(2).to_broadcast([128, NTT, 8]),
                                                  ivf[:].unsqueeze(1).to_broadcast([128, NTT, 8]), ALU.mult),
                 reads=("posf", "ivf"), writes=("ang",))
            S.op("dve", lambda e: e.memset(n2pi, -2 * PI), writes=("xrot",))
            af, cf, sf = ang[:].rearrange("p a b -> p (a b)"), cosT[:].rearrange("p a b -> p (a b)"), sinT[:].rearrange("p a b -> p (a b)")
            S.op("dve", lambda e: e.tensor_scalar(cf, af, 1.0 / (2 * PI), None, ALU.mult), reads=("ang",), writes=("cosT",))
            S.op("dve", lambda e: e.tensor_copy(kint[:], cf), reads=("cosT",), writes=("kint",))
            S.op("dve", lambda e: e.tensor_copy(cf, kint[:]), reads=("kint",), writes=("cosT",))
            S.op("dve", lambda e: e.scalar_tensor_tensor(af, cf, -6.28125, af, ALU.mult, ALU.add), reads=("cosT", "ang"), writes=("ang",))
            S.op("dve", lambda e: e.scalar_tensor_tensor(af, cf, -(2 * PI - 6.28125), af, ALU.mult, ALU.add), reads=("cosT", "ang"), writes=("ang",))
            S.op("dve", lambda e: e.scalar_tensor_tensor(sf, af, PI, n2pi, ALU.is_gt, ALU.mult), reads=("ang", "xrot"), writes=("sinT",))
            S.op("dve", lambda e: e.tensor_tensor(af, af, sf, ALU.add), reads=("ang", "sinT"), writes=("ang",))
            S.op("dve", lambda e: e.tensor_scalar(af, af, -PI, PI, ALU.max, ALU.min), reads=("ang",), writes=("ang",))
            S.op("act", lambda e: e.activation(sf, af, AF.Sin), reads=("ang",), writes=("sinT",))
            S.op("dve", lambda e: e.tensor_scalar(cf, af, 0.5 * PI, None, ALU.add), reads=("ang",), writes=("cosT",))
            S.op("dve", lambda e: e.scalar_tensor_tensor(rtmp, cf, PI, n2pi, ALU.is_gt, ALU.mult), reads=("cosT", "xrot"), writes=("tcv",))
            S.op("dve", lambda e: e.tensor_tensor(cf, cf, rtmp, ALU.add), reads=("cosT", "tcv"), writes=("cosT",))
            S.op("dve", lambda e: e.tensor_scalar(cf, cf, -PI, PI, ALU.max, ALU.min), reads=("cosT",), writes=("cosT",))
            S.op("act", lambda e: e.activation(cf, cf, AF.Sin), reads=("cosT",), writes=("cosT",))
            S.op("dve", lambda e: e.tensor_copy(cos2T[:, :, 0:8], cosT[:]), reads=("cosT",), writes=("cos2T",))
            S.op("dve", lambda e: e.tensor_copy(cos2T[:, :, 8:16], cosT[:]), reads=("cosT", "cos2T"), writes=("cos2T",))

            def emit_layer(s, l):
                pl = plan[(s, l)]
                chk(0)
                lam_init = 0.8 - 0.6 * math.exp(-0.3 * l)
                S.dma("pool", "dgw", [lambda e, l=l: e.dma_start(out=gW[:], in_=gatew[l].rearrange("p (g c j) -> p g c j", g=2, c=4))],
                      writes=("gW",))
                S.dma("sp", "dlqk", [lambda e, l=l: e.dma_start(out=E0[:, 0:256], in_=lqk[l])], writes=("E0",))
                S.op("dve", lambda e: e.tensor_tensor(E0[:, 0:128], E0[:, 0:128], E0[:, 128:256], ALU.mult), reads=("E0",), writes=("E0",))
                S.op("dve", lambda e: e.tensor_reduce(der[:, 0:2], E0[:, 0:128].rearrange("p (a b) -> p a b", a=2), AX.X, ALU.add),
                     reads=("E0",), writes=("der01",))
                S.op("act", lambda e: e.activation(der[:, 0:2], der[:, 0:2], AF.Exp), reads=("der01",), writes=("der01",))
                S.op("dve", lambda e: e.tensor_tensor(der[:, 2:3], der[:, 0:1], der[:, 1:2], ALU.subtract), reads=("der01",), writes=("der2",))
                S.op("dve", lambda e, li=lam_init: e.tensor_scalar(der[:, 3:4], der[:, 2:3], li, -1.0, ALU.add, ALU.mult),
                     reads=("der2",), writes=("neglam",))
                S.op("dve", lambda e, l=l, li=lam_init: e.tensor_scalar(der[:, 4:5], sm[:, l, GS:GS + 1], 1.0 - li, None, ALU.mult),
                     reads=("sm",), writes=("gsub",))
                S.op("act", lambda e, l=l: e.activation(der[:, 5:9], sm[:, l, LL:LL + 4], AF.Exp, scale=-1.0), reads=("sm",), writes=("c1",))
                S.op("act", lambda e: e.activation(der[:, 5:9], der[:, 5:9], AF.Ln, bias=1.0, scale=1.0), reads=("c1",), writes=("c1",))
                S.op("dve", lambda e: e.tensor_scalar(der[:, 9:13], der[:, 5:9], -16.0, None, ALU.mult), reads=("c1",), writes=("c2",))
                S.op("dve", lambda e: e.tensor_scalar(der[:, 5:9], der[:, 5:9], -8.0, None, ALU.mult), reads=("c1", "c2"), writes=("c1",))
                S.op("dve", lambda e, l=l: e.tensor_scalar(nbias[:], sm[:, l, BA:BA + 8], -1.0, None, ALU.mult), reads=("sm", "nbias"), writes=("nbias",))
                neglam = der[:, 3:4]
                gsub = der[:, 4:5]

                rmsnorm_to_B(lambda c, l=l: sm[:, l, GM + c:GM + c + 1], "n1")

                chk(1)
                wv, wvk_ = use_w(pl["win"][2])
                for tt in range(NTT):
                    tb = tt // 4
                    tsl = slice(tt * 128, (tt + 1) * 128)
                    vb = 4 + (tt % 2)

                    def projv(e, tsl=tsl, vb=vb):
                        ins = None
                        for c in range(8):
                            ins = e.matmul(bank[vb], Bt[:, c, tsl], wv[:, c, :], start=(c == 0), stop=(c == 7))
                        return ins
                    S.op("pe", projv, reads=[f"B{c}_{tb}" for c in range(8)] + [wvk_], writes=(bkey[vb],))
                    if tt % 2 == 0:
                        S.op("dve", lambda e, tt=tt, vb=vb: e.tensor_copy(V[:, tt, :], bank[vb]), reads=(bkey[vb],), writes=(f"V{tt}",))
                    else:
                        S.op("act", lambda e, tt=tt, vb=vb: e.activation(V[:, tt, :], bank[vb], AF.Copy), reads=(bkey[vb],), writes=(f"V{tt}",))

                done_w(pl["win"][2])
                chk(2)
                wq, wqk_ = use_w(pl["win"][0])
                wk, wkk_ = use_w(pl["win"][1], paired=True)
                tcv = tcs_t[:, 128:256].rearrange("p (g d) -> p g d", d=16)
                tsv = tcs_t[:, 256:384].rearrange("p (g d) -> p g d", d=16)
                Fnames = ("gg", "xc")
                Tb = (4, 5)

                def emit_transposes(tt):
                    tb, off = tt // 4, (tt % 4) * 128
                    for hb in range(2):
                        Ft = Rt[Fnames[hb]]
                        pb = Tb[hb]
                        for g in range(4):
                            S.op("pe", lambda e, g=g, Ft=Ft, pb=pb: e.transpose(bank[pb][:, g * 128:(g + 1) * 128], Ft[:, g * 128:(g + 1) * 128], idf[:]),
                                 reads=(Fnames[hb], "idf"), writes=(bkey[pb],))
                        src = bank[pb].rearrange("p (h t) -> p h t", h=4)
                        if hb == 0:
                            S.op("act", lambda e, tb=tb, off=off, src=src: e.activation(QK[:, tb:16:4, off:off + 128], src, AF.Copy),
                                 reads=(bkey[pb],), writes=[f"qk{qk_idx(h, tb)}" for h in range(4)])
                        else:
                            S.op("dve", lambda e, tb=tb, off=off, src=src: e.tensor_copy(QK[:, 16 + tb:32:4, off:off + 128], src),
                                 reads=(bkey[pb],), writes=[f"qk{16 + qk_idx(h, tb)}" for h in range(4)])

                for tt in range(NTT):
                    tb = tt // 4
                    tsl = slice(tt * 128, (tt + 1) * 128)
                    A = psA[tt % 2]
                    Ak = (bkey[0], bkey[1]) if tt % 2 == 0 else (bkey[2], bkey[3])

                    def proj(e, A=A, tsl=tsl):
                        ins = None
                        for c in range(8):
                            e.matmul(A[:, 0:512], Bt[:, c, tsl], wq[:, c, :], start=(c == 0), stop=(c == 7))
                            ins = e.matmul(A[:, 512:1024], Bt[:, c, tsl], wk[:, c, :], start=(c == 0), stop=(c == 7))
                        return ins
                    S.op("pe", proj, reads=[f"B{c}_{tb}" for c in range(8)] + [wqk_, wkk_], writes=(Ak[0], Ak[1]))
                    if tt >= 1 and not os.environ.get('SKIP_TR'):
                        emit_transposes(tt - 1)
                    cos2B = cos2T[:, tt, :].unsqueeze(1).to_broadcast([128, 8, 16])
                    sinB8 = sinT[:, tt, :].unsqueeze(1).to_broadcast([128, 8, 8])
                    for hb in range(2):
                        Ah = A[:, hb * 512:(hb + 1) * 512]
                        Fn = Fnames[hb]
                        Ft = Rt[Fn]
                        if hb == 0:
                            S.op("act", lambda e, Ft=Ft, Ah=Ah: e.activation(Ft[:], Ah, AF.Copy), reads=(Ak[hb],), writes=(Fn,))
                        else:
                            S.op("dve", lambda e, Ft=Ft, Ah=Ah: e.tensor_copy(Ft[:], Ah), reads=(Ak[hb],), writes=(Fn,))
                        if os.environ.get('SKIP_ROPE'):
                            continue
                        F3 = Ft[:].rearrange("p (g d) -> p g d", d=64)
                        S.op("dve", lambda e, F3=F3, cos2B=cos2B: e.tensor_tensor(tcv, F3[:, :, 0:16], cos2B, ALU.mult),
                             reads=(Fn, "cos2T"), writes=("tcv",))
                        S.op("dve", lambda e, F3=F3, sinB8=sinB8: e.scalar_tensor_tensor(tsv[:, :, 0:8], F3[:, :, 8:16], -1.0, sinB8, ALU.mult, ALU.mult),
                             reads=(Fn, "sinT"), writes=("tsv",))
                        S.op("dve", lambda e, F3=F3, sinB8=sinB8: e.tensor_tensor(tsv[:, :, 8:16], F3[:, :, 0:8], sinB8, ALU.mult),
                             reads=(Fn, "sinT", "tsv"), writes=("tsv",))
                        S.op("dve", lambda e, F3=F3: e.tensor_tensor(F3[:, :, 0:16], tcv, tsv, ALU.add),
                             reads=("tcv", "tsv", Fn), writes=(Fn,))
                if not os.environ.get('SKIP_TR'):
                    emit_transposes(NTT - 1)

                done_w(pl["win"][0])
                done_w(pl["win"][1])
                chk(3)
                wxr, wxrk = use_w(pl["win"][3])
                wgr, wgrk = use_w(pl["win"][4], paired=True)

                def rnn_proj1(u):
                    c, tb = u // 4, u % 4
                    def f(e, c=c, tb=tb):
                        ins = None
                        for k in range(8):
                            ins = e.matmul(bank[6], wxr[:, k, c * 128:(c + 1) * 128], Bt[:, k, tbs(tb)], start=(k == 0), stop=(k == 7))
                        return ins
                    S.op("pe", f, reads=[f"B{k}_{tb}" for k in range(8)] + [wxrk], writes=(bkey[6],))
                    if tb == 0:
                        S.op("dve", lambda e: e.memset(xr_ext[:, 0:4], 0.0), reads=("xr_ext",), writes=("xr_ext",))
                    else:
                        S.op("dve", lambda e: e.tensor_copy(xr_ext[:, 0:4], xr_ext[:, 512:516]), reads=("xr_ext",), writes=("xr_ext",))
                    S.op("dve", lambda e: e.tensor_copy(xr_ext[:, 4:516], bank[6]), reads=(bkey[6],), writes=("xr_ext",))

                def rnn_proj2(u):
                    c, tb = u // 4, u % 4
                    def f2(e, c=c, tb=tb):
                        ins = None
                        for k in range(8):
                            ins = e.matmul(bank[7], wgr[:, k, c * 128:(c + 1) * 128], Bt[:, k, tbs(tb)], start=(k == 0), stop=(k == 7))
                        return ins
                    S.op("pe", f2, reads=[f"B{k}_{tb}" for k in range(8)] + [wgrk], writes=(bkey[7],))
                    S.op("dve", lambda e: e.tensor_copy(Rt["gg"][:], bank[7]), reads=(bkey[7],), writes=("gg",))
                    S.op("dve", lambda e: e.tensor_tensor(Rt["hs"][:], Rt["gg"][:], Rt["gg"][:], ALU.mult), reads=("gg",), writes=("hs",))
                    S.op("dve", lambda e: e.tensor_scalar(Rt["hs"][:], Rt["hs"][:], 0.044715, 1.0, ALU.mult, ALU.add), reads=("hs",), writes=("hs",))
                    S.op("dve", lambda e: e.tensor_tensor(Rt["hs"][:], Rt["hs"][:], Rt["gg"][:], ALU.mult), reads=("hs", "gg"), writes=("hs",))
                    S.op("act", lambda e: e.activation(Rt["hs"][:], Rt["hs"][:], AF.Exp, scale=-1.5957691216057308), reads=("hs",), writes=("hs",))
                    S.op("act", lambda e: e.activation(Rt["hs"][:], Rt["hs"][:], AF.Ln, bias=1.0, scale=1.0), reads=("hs",), writes=("hs",))
                    S.op("act", lambda e: e.activation(Rt["hs"][:], Rt["hs"][:], AF.Exp, scale=-1.0), reads=("hs",), writes=("hs",))
                    S.op("dve", lambda e: e.tensor_tensor(Rt["gg"][:], Rt["gg"][:], Rt["hs"][:], ALU.mult), reads=("hs", "gg"), writes=("gg",))
                    cw = lambda k, c=c: sm[:, l, CW + c * 4 + k:CW + c * 4 + k + 1]
                    S.op("dve", lambda e, c=c: e.tensor_scalar(Rt["xc"][:], xr_ext[:, 4:516], cw(3), sm[:, l, CB + c:CB + c + 1], ALU.mult, ALU.add),
                         reads=("xr_ext", "sm"), writes=("xc",))
                    for j in (1, 2, 3):
                        S.op("dve", lambda e, j=j: e.scalar_tensor_tensor(Rt["xc"][:], xr_ext[:, 4 - j:516 - j], cw(3 - j), Rt["xc"][:], ALU.mult, ALU.add),
                             reads=("xr_ext", "xc", "sm"), writes=("xc",))
                    S.op("dve", lambda e: e.tensor_copy(xcb[:], Rt["xc"][:]), reads=("xc",), writes=("xcb",))

                def rnn_gates1(u):
                    c, tb = u // 4, u % 4
                    S.op("pe", lambda e, c=c: e.matmul(bank[6], gW[:, 0, c, :], xcb[:], start=True, stop=True), reads=("xcb", "gW"), writes=(bkey[6],))
                    S.op("act", lambda e, c=c: e.activation(Rt["r"][:], bank[6], AF.Exp, bias=nbias[:, c:c + 1], scale=-1.0),
                         reads=(bkey[6], "nbias"), writes=("r",))
                    S.op("act", lambda e: e.activation(Rt["r"][:], Rt["r"][:], AF.Ln, bias=1.0, scale=1.0), reads=("r",), writes=("r",))
                    S.op("act", lambda e: e.activation(Rt["r"][:], Rt["r"][:], AF.Exp, scale=-1.0), reads=("r",), writes=("r",))

                def rnn_gates2(u):
                    c, tb = u // 4, u % 4
                    S.op("pe", lambda e, c=c: e.matmul(bank[7], gW[:, 1, c, :], xcb[:], start=True, stop=True), reads=("xcb", "gW"), writes=(bkey[7],))
                    S.op("act", lambda e, c=c: e.activation(Rt["i"][:], bank[7], AF.Exp, bias=nbias[:, 4 + c:5 + c], scale=-1.0),
                         reads=(bkey[7], "nbias"), writes=("i",))
                    S.op("act", lambda e: e.activation(Rt["i"][:], Rt["i"][:], AF.Ln, bias=1.0, scale=1.0), reads=("i",), writes=("i",))
                    S.op("act", lambda e: e.activation(Rt["i"][:], Rt["i"][:], AF.Exp, scale=-1.0), reads=("i",), writes=("i",))
                    S.op("dve", lambda e: e.tensor_tensor(Rt["i"][:], Rt["i"][:], Rt["xc"][:], ALU.mult), reads=("i", "xc"), writes=("i",))
                    S.op("act", lambda e, c=c: e.activation(Rt["xc"][:], Rt["r"][:], AF.Exp, scale=der[:, 9 + c:10 + c]),
                         reads=("r", "c2"), writes=("xc",))
                    S.op("act", lambda e, c=c: e.activation(Rt["r"][:], Rt["r"][:], AF.Exp, scale=der[:, 5 + c:6 + c]),
                         reads=("r", "c1"), writes=("r",))
                    S.op("act", lambda e: e.activation(Rt["xc"][:], Rt["xc"][:], AF.Ln, bias=1.0, scale=-1.0), reads=("xc",), writes=("xc",))
                    S.op("act", lambda e: e.activation(Rt["xc"][:], Rt["xc"][:], AF.Exp, scale=0.5), reads=("xc",), writes=("xc",))
                    S.op("dve", lambda e: e.tensor_tensor(Rt["i"][:], Rt["i"][:], Rt["xc"][:], ALU.mult), reads=("i", "xc"), writes=("i",))
                    if tb == 0:
                        S.op("dve", lambda e: e.memset(der[:, 13:14], 0.0), reads=("carry",), writes=("carry",))
                    S.op("dve", lambda e: e.tensor_tensor_scan(Rt["hs"][:], Rt["r"][:], Rt["i"][:], der[:, 13:14], ALU.mult, ALU.add),
                         reads=("r", "i", "carry"), writes=("hs",))
                    S.op("dve", lambda e: e.tensor_copy(der[:, 13:14], Rt["hs"][:, 511:512]), reads=("hs", "carry"), writes=("carry",))
                    S.op("dve", lambda e, c=c, tb=tb: e.tensor_tensor(Yt[:, c, tbs(tb)], Rt["hs"][:], Rt["gg"][:], ALU.mult),
                         reads=("hs", "gg"), writes=(f"Y{c}_{tb}",))

                pend = {"ss": None}

                def att_block(h, j, u_proj, u_gate):
                    nkt = 4 * j + 4
                    qi = qk_idx(h, j)
                    S0b, S1b, O1b, O2b, R1b, R2b = 0, 1, 2, 3, 4, 5

                    def geo(kt):
                        m = kt - 4 * j
                        q0 = m * 128 if m > 0 else 0
                        tbk, off = kt // 4, (kt % 4) * 128
                        return m, q0, 16 + qk_idx(h, tbk), off

                    def emit_qk_exp(kt):
                        m, q0, ki, off = geo(kt)
                        pbi = kt % 2

                        def qk(e, q0=q0, ki=ki, off=off, qi=qi):
                            e.matmul(bank[S0b][:, q0:512], QK[0:64, ki, off:off + 128], QK[0:64, qi, q0:512], start=True, stop=True, tile_position=(0, 0))
                            return e.matmul(bank[S1b][:, q0:512], QK[64:128, ki, off:off + 128], QK[64:128, qi, q0:512], start=True, stop=True, tile_position=(64, 0))
                        S.op("pe", qk, reads=(f"qk{ki}", f"qk{qi}"), writes=(bkey[S0b], bkey[S1b]))
                        for cc in range(2):
                            P = Pb[pbi][cc]
                            pk = f"Pb{pbi}{cc}"
                            S.op("act", lambda e, P=P, cc=cc, q0=q0: e.activation(P[:, q0:512], bank[cc][:, q0:512], AF.Exp, scale=SCALE),
                                 reads=(bkey[cc],), writes=(pk,))
                            if m >= 0:
                                S.op("dve", lambda e, P=P, q0=q0: e.tensor_tensor(P[:, q0:q0 + 128], P[:, q0:q0 + 128], cmask, ALU.mult),
                                     reads=(pk, "cb"), writes=(pk,))

                    def emit_pv(kt):
                        m, q0, ki, off = geo(kt)
                        pbi = kt % 2

                        def pv(e, q0=q0, kt=kt, pbi=pbi, h=h, first=(kt == 0), last=(kt == nkt - 1)):
                            vv = V[:, kt, h * 128:(h + 1) * 128]
                            e.matmul(bank[O1b][:, q0:512], vv, Pb[pbi][0][:, q0:512], start=first, stop=last)
                            e.matmul(bank[R1b][:, q0:512], ones, Pb[pbi][0][:, q0:512], start=first, stop=last)
                            e.matmul(bank[O2b][:, q0:512], vv, Pb[pbi][1][:, q0:512], start=first, stop=last)
                            return e.matmul(bank[R2b][:, q0:512], ones, Pb[pbi][1][:, q0:512], start=first, stop=last)
                        S.op("pe", pv, reads=(f"V{kt}", f"Pb{pbi}0", f"Pb{pbi}1", "cb"),
                             writes=(bkey[O1b], bkey[O2b], bkey[R1b], bkey[R2b]))

                    emit_qk_exp(0)
                    for kt in range(nkt):
                        if kt + 1 < nkt:
                            emit_qk_exp(kt + 1)
                        emit_pv(kt)
                        if kt == 0:
                            if pend["ss"] is not None:
                                pend["ss"]()
                                pend["ss"] = None
                            if u_gate is not None:
                                rnn_gates1(u_gate)
                        if kt == 1 and u_gate is not None:
                            rnn_gates2(u_gate)
                        if kt == nkt - 2 and u_proj is not None:
                            rnn_proj1(u_proj)
                    S.op("act", lambda e: e.activation(E0[:], bank[R1b], AF.Ln), reads=(bkey[R1b],), writes=("E0",))
                    S.op("act", lambda e: e.activation(E1[:], bank[R2b], AF.Ln), reads=(bkey[R2b],), writes=("E1",))
                    S.op("act", lambda e: e.activation(E0[:], E0[:], AF.Exp, scale=-1.0), reads=("E0",), writes=("E0",))
                    S.op("act", lambda e: e.activation(E1[:], E1[:], AF.Exp, scale=-1.0), reads=("E1",), writes=("E1",))
                    S.op("dve", lambda e: e.tensor_tensor(E0[:], bank[O1b], E0[:], ALU.mult), reads=(bkey[O1b], "E0"), writes=("E0",))
                    S.op("dve", lambda e: e.tensor_tensor(E1[:], bank[O2b], E1[:], ALU.mult), reads=(bkey[O2b], "E1"), writes=("E1",))
                    S.op("dve", lambda e: e.scalar_tensor_tensor(E0[:], E1[:], neglam, E0[:], ALU.mult, ALU.add),
                         reads=("E0", "E1", "neglam"), writes=("E0",))
                    S.op("dve", lambda e: e.tensor_tensor(E2[:], E0[:], E0[:], ALU.mult), reads=("E0",), writes=("E2",))

                    def part_b(qi=qi):
                        S.op("pe", lambda e: e.matmul(bank[7], ones, E2[:], start=True, stop=True), reads=("E2", "cb"), writes=(bkey[7],))
                        S.op("act", lambda e: e.activation(E1[:], bank[7], AF.Ln, bias=eps_c, scale=1.0 / 128), reads=(bkey[7], "cstc0"), writes=("E1",))
                        S.op("act", lambda e: e.activation(E1[:], E1[:], AF.Exp, scale=-0.5), reads=("E1",), writes=("E1",))
                        S.op("dve", lambda e, qi=qi: e.scalar_tensor_tensor(QK[:, qi, :], E0[:], gsub, E1[:], ALU.mult, ALU.mult),
                             reads=("E0", "E1", "gsub"), writes=(f"qk{qi}",))
                    pend["ss"] = part_b
                    if u_proj is not None:
                        rnn_proj2(u_proj)

                blocks = [(h, j) for h in range(4) for j in range(NTB)]
                rnn_proj1(0)
                rnn_proj2(0)
                for bi, (h, j) in enumerate(blocks):
                    att_block(h, j, bi + 1 if bi + 1 < 16 else None, bi)
                pend["ss"]()
                pend["ss"] = None

                done_w(pl["win"][3])
                done_w(pl["win"][4])
                chk(4)
                wo0_, wok0 = use_w(pl["wo"][0])
                wo1_, wok1 = use_w(pl["wo"][1], paired=True)
                for tb in range(NTB):
                    for g2 in range(2):
                        wo_, wok = (wo0_, wok0) if g2 == 0 else (wo1_, wok1)
                        for d4 in range(4):
                            dtc = g2 * 4 + d4
                            bk = nextbank([0, 1, 2, 3, 4, 5, 6, 7])
                            def f(e, d4=d4, tb=tb, bk=bk, wo_=wo_):
                                ins = None
                                for c in range(4):
                                    e.matmul(bank[bk], wo_[:, c, d4 * 128:(d4 + 1) * 128], QK[:, c * 4 + tb, :], start=(c == 0), stop=False)
                                for c in range(4):
                                    ins = e.matmul(bank[bk], wo_[:, 4 + c, d4 * 128:(d4 + 1) * 128], Yt[:, c, tbs(tb)], start=False, stop=(c == 3))
                                return ins
                            S.op("pe", f, reads=[f"qk{c * 4 + tb}" for c in range(4)] + [f"Y{c}_{tb}" for c in range(4)] + [wok], writes=(bkey[bk],))
                            S.op("dve", lambda e, dtc=dtc, tb=tb, bk=bk: e.tensor_tensor(hT[:, dtc, tbs(tb)], bank[bk], hT[:, dtc, tbs(tb)], ALU.add),
                                 reads=(bkey[bk], f"h{dtc}_{tb}"), writes=(f"h{dtc}_{tb}",))
                    if tb >= 1:
                        rmsnorm_tb(lambda c, l=l: sm[:, l, GL + c:GL + c + 1], tb - 1)
                rmsnorm_tb(lambda c, l=l: sm[:, l, GL + c:GL + c + 1], NTB - 1)
                done_w(pl["wo"][0])
                done_w(pl["wo"][1])

                chk(5)
                pTb = V[:, 0:8, :].rearrange("p a b -> p (a b)").rearrange("p (k t) -> p k t", k=2)
                S.dma("pool", "dpt", [(lambda e, l=l, s=s, k=k: e.dma_start(out=pTb[:, k, :], in_=pT[l, s, k * 128:(k + 1) * 128, :])) for k in range(2)],
                      writes=[f"V{t}" for t in range(8)])
                sqt = [Rt["gg"], Rt["xc"]]
                sqk = ["gg", "xc"]
                cnt = 0
                for g in range(4):
                    for a in range(2):
                        w1_, w1k = use_w(pl["w1"][g][a])
                        for f4 in range(4):
                            fc = a * 4 + f4
                            for tb in range(NTB):
                                bk = nextbank([0, 1, 2, 3, 4, 5, 6, 7])
                                hk = f"qk{fc * 4 + tb}"
                                def f(e, f4=f4, tb=tb, bk=bk, w1_=w1_):
                                    ins = None
                                    for c in range(8):
                                        ins = e.matmul(bank[bk], w1_[:, c, f4 * 128:(f4 + 1) * 128], Bt[:, c, tbs(tb)], start=(c == 0), stop=(c == 7))
                                    return ins
                                S.op("pe", f, reads=[f"B{c}_{tb}" for c in range(8)] + [w1k], writes=(bkey[bk],))
                                ti = cnt % 2
                                cnt += 1
                                S.op("act", lambda e, bk=bk, ti=ti: e.activation(sqt[ti][:], bank[bk], AF.Square), reads=(bkey[bk],), writes=(sqk[ti],))
                                S.op("dve", lambda e, bk=bk, ti=ti, fc=fc, tb=tb: e.scalar_tensor_tensor(
                                    QK[:, fc * 4 + tb, :], bank[bk], 0.0, sqt[ti][:], ALU.is_gt, ALU.mult),
                                    reads=(bkey[bk], sqk[ti]), writes=(hk,))
                    def mlp_out_tile(w2_, w2k, dh, d4, tb):
                        dtc = dh * 4 + d4
                        bk = nextbank([0, 1, 2, 3, 4, 5, 6, 7])
                        def f(e, d4=d4, tb=tb, bk=bk, w2_=w2_):
                            ins = None
                            for fc in range(8):
                                ins = e.matmul(bank[bk], w2_[:, fc, d4 * 128:(d4 + 1) * 128], QK[:, fc * 4 + tb, :], start=(fc == 0), stop=(fc == 7))
                            return ins
                        S.op("pe", f, reads=[f"qk{fc * 4 + tb}" for fc in range(8)] + [w2k], writes=(bkey[bk],))
                        S.op("dve", lambda e, dtc=dtc, tb=tb, bk=bk: e.tensor_tensor(hT[:, dtc, tbs(tb)], bank[bk], hT[:, dtc, tbs(tb)], ALU.add),
                             reads=(bkey[bk], f"h{dtc}_{tb}"), writes=(f"h{dtc}_{tb}",))
                    if g < 3:
                        for dh in range(2):
                            w2_, w2k = use_w(pl["w2"][g][dh])
                            for d4 in range(4):
                                for tb in range(NTB):
                                    mlp_out_tile(w2_, w2k, dh, d4, tb)
                    else:
                        w2a = use_w(pl["w2"][g][0])
                        w2b = use_w(pl["w2"][g][1], paired=True)
                        for tb in range(NTB):
                            for dh in range(2):
                                w2_, w2k = w2a if dh == 0 else w2b
                                for d4 in range(4):
                                    mlp_out_tile(w2_, w2k, dh, d4, tb)
                            if tb >= 1:
                                rmsnorm_tb(lambda c, l=l: sm[:, l, GP + c:GP + c + 1], tb - 1)
                        rmsnorm_tb(lambda c, l=l: sm[:, l, GP + c:GP + c + 1], NTB - 1)
                        done_w(pl["w2"][g][0])
                        done_w(pl["w2"][g][1])

                chk(6)
                sgt = [Rt["r"], Rt["i"]]
                sgk = ["r", "i"]
                t2t = [Rt["hs"], Rt["gg"]]
                t2k = ["hs", "gg"]
                cnt = 0
                wp_, wpk = None, None
                for dh in range(2):
                    if dh == 0:
                        wg_, wgk_ = use_w(pl["wg0"])
                        wp_, wpk = use_w(pl["wp"], paired=True)
                    else:
                        done_w(pl["wg0"])
                        wg_, wgk_ = use_w(pl["wg1"], paired=True)
                    for d4 in range(4):
                        dtc = dh * 4 + d4
                        for tb in range(NTB):
                            bg = nextbank([0, 1, 2, 3, 4, 5, 6, 7])
                            bp = nextbank([0, 1, 2, 3, 4, 5, 6, 7])
                            def f(e, d4=d4, tb=tb, bg=bg, wg_=wg_):
                                ins = None
                                for c in range(8):
                                    ins = e.matmul(bank[bg], wg_[:, c, d4 * 128:(d4 + 1) * 128], Bt[:, c, tbs(tb)], start=(c == 0), stop=(c == 7))
                                return ins
                            S.op("pe", f, reads=[f"B{c}_{tb}" for c in range(8)] + [wgk_], writes=(bkey[bg],))
                            def f2(e, d4=d4, tb=tb, bp=bp, dh=dh, wp_=wp_):
                                ins = None
                                for k in range(2):
                                    ins = e.matmul(bank[bp], wp_[:, dh * 2 + k, d4 * 128:(d4 + 1) * 128], pTb[:, k, tbs(tb)], start=(k == 0), stop=(k == 1))
                                return ins
                            S.op("pe", f2, reads=[f"V{t}" for t in range(8)] + [wpk], writes=(bkey[bp],))
                            ti = cnt % 2
                            cnt += 1
                            S.op("act", lambda e, bg=bg, ti=ti: e.activation(sgt[ti][:], bank[bg], AF.Sigmoid), reads=(bkey[bg],), writes=(sgk[ti],))
                            S.op("dve", lambda e, bp=bp, ti=ti: e.tensor_tensor(t2t[ti][:], bank[bp], sgt[ti][:], ALU.mult),
                                 reads=(bkey[bp], sgk[ti]), writes=(t2k[ti],))
                            S.op("dve", lambda e, dtc=dtc, tb=tb, ti=ti: e.tensor_tensor(hT[:, dtc, tbs(tb)], t2t[ti][:], hT[:, dtc, tbs(tb)], ALU.add),
                                 reads=(t2k[ti], f"h{dtc}_{tb}"), writes=(f"h{dtc}_{tb}",))

            for l in range(n_layers):
                try:
                    emit_layer(s, l)
                except _Stop:
                    pass
            for tb in range(NTB):
                bk = nextbank([4, 5, 6, 7])
                for c in range(8):
                    sq = Pb[0][c % 2]
                    sqk_ = f"Pb0{c % 2}"
                    S.op("act", lambda e, sq=sq, c=c, tb=tb: e.activation(sq[:], hT[:, c, tbs(tb)], AF.Square),
                         reads=(f"h{c}_{tb}",), writes=(sqk_,))
                    S.op("pe", lambda e, sq=sq, c=c, bk=bk: e.matmul(bank[bk], ones, sq[:], start=(c == 0), stop=(c == 7)),
                         reads=(sqk_, "cb"), writes=(bkey[bk],))
                Er = (E0, E1)[tb % 2]
                Ek = ("E0", "E1")[tb % 2]
                S.op("act", lambda e, Er=Er, bk=bk: e.activation(Er[:], bank[bk], AF.Ln, bias=eps_c, scale=1.0 / D),
                     reads=(bkey[bk], "cstc0"), writes=(Ek,))
                S.op("act", lambda e, Er=Er: e.activation(Er[:], Er[:], AF.Exp, scale=-0.5), reads=(Ek,), writes=(Ek,))
                for c in range(8):
                    names = ("gg", "xc", "r", "i", "hs")
                    nm = names[c % 5]
                    ot = Rt[nm]
                    S.op("dve", lambda e, Er=Er, c=c, tb=tb, ot=ot: e.scalar_tensor_tensor(
                        ot[:], hT[:, c, tbs(tb)], gf[:, c:c + 1], Er[:], ALU.mult, ALU.mult),
                        reads=(f"h{c}_{tb}", Ek, "gf"), writes=(nm,))
                    S.dma("sp", "dout_" + nm, [lambda e, ot=ot, c=c, tb=tb, s=s: e.dma_start(out=outT[s, c * 128:(c + 1) * 128, tbs(tb)], in_=ot[:])],
                          reads=(nm,))

        sem_names = list(S.ENG) + sorted(S.dcnt.keys())
        sems = {n: es.enter_context(nc.semaphore("s_" + n)) for n in sem_names}
        final_waits = [(n, S.cnt[n]) for n in S.ENG if n != "sp" and S.cnt[n] > 0] + [(n, v) for n, v in S.dcnt.items()]
        block = es.enter_context(nc.Block())

        def replay(name, e):
            for waits, fn, (sn, inc) in S.q[name]:
                for (ws, wv) in waits:
                    e.wait_ge(sems[ws], wv)
                ins = fn(e)
                ins.then_inc(sems[sn], inc)
            if name == "sp":
                for (ws, wv) in final_waits:
                    e.wait_ge(sems[ws], wv)

        @block.tensor
        def _(e):
            replay("pe", e)

        @block.scalar
        def _(e):
            replay("act", e)

        @block.vector
        def _(e):
            replay("dve", e)

        @block.gpsimd
        def _(e):
            replay("pool", e)

        @block.sync
        def _(e):
            replay("sp", e)
    return nc, S


def _host_consts():
    ident = np.eye(128, dtype=np.float32)
    ones = np.ones((128, 128), np.float32)
    k = np.arange(128)[:, None]
    q = np.arange(128)[None, :]
    mask = (q >= k).astype(np.float32)
    cst = np.concatenate([ident, ones, mask], axis=1)
    half = 8
    inv_freq = (np.float32(500000.0) ** (-np.arange(half, dtype=np.float32) * np.float32(2.0) / np.float32(16))).astype(np.float32)
    invf = np.broadcast_to(inv_freq[None, :], (128, 8)).copy()
    return cst, invf


def _layout_inputs(inp, n_layers=L):
    f = lambda a: np.ascontiguousarray(np.asarray(a, dtype=np.float32))
    col8 = lambda v: v.reshape(8, 128).T
    col4 = lambda v: v.reshape(4, 128).T
    smalls = np.zeros((128, L, NSM), np.float32)
    gatew = np.zeros((L, 128, 2, 4, 128), np.float32)
    lqk = np.zeros((L, 128, 256), np.float32)
    for l in range(L):
        smalls[:, l, GM:GM + 8] = col8(f(inp["g_mix"][l]))
        smalls[:, l, GL:GL + 8] = col8(f(inp["g_mlp"][l]))
        smalls[:, l, GP:GP + 8] = col8(f(inp["g_ple"][l]))
        cw = f(inp["conv_w"][l])
        for c in range(4):
            for k in range(4):
                smalls[:, l, CW + c * 4 + k] = cw[k, c * 128:(c + 1) * 128]
        smalls[:, l, CB:CB + 4] = col4(f(inp["conv_b"][l]))
        smalls[:, l, BA:BA + 4] = col4(f(inp["b_gate_a"][l]))
        smalls[:, l, BX:BX + 4] = col4(f(inp["b_gate_x"][l]))
        smalls[:, l, LL:LL + 4] = col4(f(inp["lru_lambda"][l]))
        smalls[:, l, GS] = f(inp["g_subln"][l])
        for gi, nm in enumerate(("w_gate_a", "w_gate_x")):
            w = f(inp[nm][l])
            for c in range(4):
                for b in range(2):
                    gatew[l, b * 64:(b + 1) * 64, gi, c, b * 64:(b + 1) * 64] = w[2 * c + b]
        lqk[l, :, 0:128] = f(inp["lam_q"][l]).reshape(1, 128)
        lqk[l, :, 128:256] = f(inp["lam_k"][l]).reshape(1, 128)
    gatew = gatew.reshape(L, 128, 1024)
    gfin = np.ascontiguousarray(col8(f(inp["g_final"])))
    cst, invf = _host_consts()
    x = f(inp["x"])
    p = f(inp["p"])
    posn = np.asarray(inp["positions"]).astype(np.int32)
    shared = dict(w_in=f(inp["w_in"]), w_out=f(inp["w_out"]), w_mlp_in=f(inp["w_mlp_in"]), w_mlp_out=f(inp["w_mlp_out"]),
                  w_ple_gate=f(inp["w_ple_gate"]), w_ple_proj=f(inp["w_ple_proj"]), smalls=smalls, gatew=gatew, lqk=lqk,
                  gfin=gfin, cst=cst, invf=invf)
    maps = []
    for i in range(NC):
        xs = x[2 * i:2 * i + 2]
        m = dict(shared)
        m["xT"] = np.ascontiguousarray(xs.transpose(0, 2, 1))
        m["pT"] = np.ascontiguousarray(p[:, 2 * i:2 * i + 2].transpose(0, 1, 3, 2))
        m["pos"] = np.ascontiguousarray(posn[2 * i:2 * i + 2].reshape(2, NTT, 128).transpose(0, 2, 1))
        maps.append(m)
    return maps


_CACHE = {}


def kernel(**inputs):
    maps = _layout_inputs(inputs)
    if "nc" not in _CACHE:
        _CACHE["nc"] = build_program()[0]
    nc = _CACHE["nc"]
    res = run_bass_kernel_spmd(nc, maps, core_ids=list(range(NC)))
    out = np.empty((16, T, D), np.float32)
    for i in range(NC):
        o = res.results[i]["outT"]
        out[2 * i:2 * i + 2] = o.transpose(0, 2, 1)
    return out
```
